# Optimizing a Trainium2 kernel written in Bass

```python
import jax, jax.numpy as jnp
from jax import lax
import numpy as np

D_MODEL = 1024
BATCH = 8
SEQ = 8192
DEPTH = 2

HEAD_DIM = 64
D_ATTN = D_MODEL // 2
D_MLSTM = D_MODEL // 4
D_SGU = D_MODEL // 4
N_ATTN_HEADS = D_ATTN // HEAD_DIM
N_MLSTM_HEADS = D_MLSTM // HEAD_DIM
N_SGU_GROUPS = D_SGU // HEAD_DIM
SGU_GROUP_DIM = D_SGU // N_SGU_GROUPS
Q_BLOCK = 128
MLSTM_CHUNK = 64
SGU_CHUNK = 128
CONV_WIDTH = 3
D_FF = 11 * D_MODEL // 4
SPLITS = (D_ATTN, D_ATTN, D_ATTN, D_MLSTM, D_MLSTM, D_MLSTM, D_MLSTM, D_SGU, D_SGU, N_ATTN_HEADS, N_MLSTM_HEADS, N_MLSTM_HEADS)
N_IN = 3 * D_ATTN + 4 * D_MLSTM + 2 * D_SGU + N_ATTN_HEADS + 2 * N_MLSTM_HEADS
EPS = 1e-6

kernel_name = 'hybrid_fox_mlstm_sgu_block'


def rms_norm(x, g):
    xf = x.astype(jnp.float32)
    y = xf * lax.rsqrt(jnp.mean(xf * xf, axis=-1, keepdims=True) + EPS)
    return (y * g.astype(jnp.float32)).astype(x.dtype)


def layer_norm(x, g):
    xf = x.astype(jnp.float32)
    mu = jnp.mean(xf, axis=-1, keepdims=True)
    var = jnp.mean(jnp.square(xf - mu), axis=-1, keepdims=True)
    return ((xf - mu) * lax.rsqrt(var + EPS) * g.astype(jnp.float32)).astype(x.dtype)


def to_heads(t, n_heads):
    b, s, _ = t.shape
    return t.reshape(b, s, n_heads, -1)


def forgetting_attention(q, k, v, f_pre):
    s_len = q.shape[2]
    F = jnp.cumsum(jax.nn.log_sigmoid(f_pre.astype(jnp.float32)), axis=-1)
    scale = HEAD_DIM ** -0.5
    outs = []
    for blk in range(s_len // Q_BLOCK):
        lo, hi = blk * Q_BLOCK, (blk + 1) * Q_BLOCK
        qb = q[:, :, lo:hi]
        kb = k[:, :, :hi]
        vb = v[:, :, :hi]
        logits = (jnp.einsum('bhtd,bhsd->bhts', qb, kb).astype(jnp.float32) * scale
                  + F[:, :, lo:hi, None] - F[:, :, None, :hi])
        mask = (lo + jnp.arange(Q_BLOCK))[:, None] >= jnp.arange(hi)[None, :]
        logits = jnp.where(mask, logits, -jnp.inf)
        p = jax.nn.softmax(logits, axis=-1)
        outs.append(jnp.einsum('bhts,bhsd->bhtd', p.astype(v.dtype), vb))
    return jnp.concatenate(outs, axis=2)


def mlstm_chunkwise(q, k, v, i_pre, f_pre):
    out_dtype = v.dtype
    b, h, s_len, d = q.shape
    L = MLSTM_CHUNK
    n_chunks = s_len // L
    qf = q.astype(jnp.float32)
    kf = k.astype(jnp.float32) * (d ** -0.5)
    vf = v.astype(jnp.float32)
    ig = i_pre.astype(jnp.float32)
    logf = jax.nn.log_sigmoid(f_pre.astype(jnp.float32))

    def to_chunks(a):
        return jnp.moveaxis(a.reshape(b, h, n_chunks, L, *a.shape[3:]), 2, 0)

    causal = jnp.tril(jnp.ones((L, L), dtype=bool))

    def step(carry, xs):
        C, n, m = carry
        qb, kb, vb, ib, fb = xs
        bcum = jnp.cumsum(fb, axis=-1)
        log_d = bcum[..., :, None] - bcum[..., None, :] + ib[..., None, :]
        log_d = jnp.where(causal, log_d, -jnp.inf)
        log_inter = bcum + m[..., None]
        m_t = jnp.maximum(log_inter, jnp.max(log_d, axis=-1))
        w_intra = jnp.exp(log_d - m_t[..., None])
        w_inter = jnp.exp(log_inter - m_t)
        sc = jnp.einsum('bhtd,bhsd->bhts', qb, kb) * w_intra
        num = (w_inter[..., None] * jnp.einsum('bhtd,bhde->bhte', qb, C)
               + jnp.einsum('bhts,bhse->bhte', sc, vb))
        den = w_inter * jnp.einsum('bhtd,bhd->bht', qb, n) + jnp.sum(sc, axis=-1)
        h_out = num / jnp.maximum(jnp.abs(den), jnp.exp(-m_t))[..., None]
        b_last = bcum[..., -1]
        log_w_state = b_last[..., None] - bcum + ib
        m_new = jnp.maximum(b_last + m, jnp.max(log_w_state, axis=-1))
        w_state = jnp.exp(log_w_state - m_new[..., None])
        decay = jnp.exp(b_last + m - m_new)
        C_new = decay[..., None, None] * C + jnp.einsum('bhs,bhsd,bhse->bhde', w_state, kb, vb)
        n_new = decay[..., None] * n + jnp.einsum('bhs,bhsd->bhd', w_state, kb)
        return (C_new, n_new, m_new), h_out

    init = (jnp.zeros((b, h, d, d), jnp.float32),
            jnp.zeros((b, h, d), jnp.float32),
            jnp.zeros((b, h), jnp.float32))
    _, hs = lax.scan(step, init, (to_chunks(qf), to_chunks(kf), to_chunks(vf), to_chunks(ig), to_chunks(logf)))
    return jnp.moveaxis(hs, 0, 2).reshape(b, h, s_len, d).astype(out_dtype)


def spatial_gating(u, v, g, w_s, b_s):
    b, s_len, _ = u.shape
    n = s_len // SGU_CHUNK
    vn = layer_norm(v.reshape(b, s_len, N_SGU_GROUPS, SGU_GROUP_DIM), g.reshape(N_SGU_GROUPS, SGU_GROUP_DIM))
    vn = vn.reshape(b, n, SGU_CHUNK, N_SGU_GROUPS, SGU_GROUP_DIM)
    mask = jnp.tril(jnp.ones((SGU_CHUNK, SGU_CHUNK), dtype=bool))
    w = jnp.where(mask, w_s, 0.0).astype(vn.dtype)
    mixed = jnp.einsum('gts,bnsgc->bntgc', w, vn) + b_s.T[:, :, None].astype(vn.dtype)
    return u * mixed.reshape(b, s_len, D_SGU)


def causal_depthwise_conv(z, w, bias):
    ch = z.shape[-1]
    y = lax.conv_general_dilated(z, w[:, None, :].astype(z.dtype), window_strides=(1,),
                                 padding=[(CONV_WIDTH - 1, 0)],
                                 dimension_numbers=('NWC', 'WIO', 'NWC'),
                                 feature_group_count=ch)
    return y + bias.astype(z.dtype)


def conv_ffn(h, w_up, w_conv, b_conv, w_down):
    z = h @ w_up
    z = causal_depthwise_conv(z, w_conv, b_conv)
    a, gv = jnp.split(z, 2, axis=-1)
    return (jax.nn.silu(a) * gv) @ w_down


def hybrid_mixer(h, w_in, b_fox_f, b_m_i, b_m_f, g_q, g_k, g_m_out, g_sgu, w_s, b_s, w_out):
    b, s_len, _ = h.shape
    p = h @ w_in
    (q_a, k_a, v_a, q_m, k_m, v_m, o_m, u_s, v_s, f_a, i_m, f_m) = jnp.split(
        p, list(np.cumsum(SPLITS)[:-1]), axis=-1)
    qa = rms_norm(to_heads(q_a, N_ATTN_HEADS), g_q).transpose(0, 2, 1, 3)
    ka = rms_norm(to_heads(k_a, N_ATTN_HEADS), g_k).transpose(0, 2, 1, 3)
    va = to_heads(v_a, N_ATTN_HEADS).transpose(0, 2, 1, 3)
    fa = (f_a + b_fox_f).transpose(0, 2, 1)
    y_attn = forgetting_attention(qa, ka, va, fa).transpose(0, 2, 1, 3).reshape(b, s_len, D_ATTN)
    qm = to_heads(q_m, N_MLSTM_HEADS).transpose(0, 2, 1, 3)
    km = to_heads(k_m, N_MLSTM_HEADS).transpose(0, 2, 1, 3)
    vm = to_heads(v_m, N_MLSTM_HEADS).transpose(0, 2, 1, 3)
    im = (i_m + b_m_i).transpose(0, 2, 1)
    fm = (f_m + b_m_f).transpose(0, 2, 1)
    hm = mlstm_chunkwise(qm, km, vm, im, fm).transpose(0, 2, 1, 3)
    hm = rms_norm(hm, g_m_out.reshape(N_MLSTM_HEADS, HEAD_DIM)).reshape(b, s_len, D_MLSTM)
    y_mlstm = jax.nn.sigmoid(o_m) * hm
    y_sgu = spatial_gating(jax.nn.gelu(u_s), jax.nn.gelu(v_s), g_sgu, w_s, b_s)
    y = jnp.concatenate([y_attn, y_mlstm, y_sgu], axis=-1)
    return y @ w_out


def setup_inputs(seed: int = 0) -> dict:
    key = jax.random.key(seed)
    ks = jax.random.split(key, 21)

    def nrm(k, shape, scale):
        return jax.random.normal(k, shape, jnp.float32) * scale

    return {
        'x': nrm(ks[0], (BATCH, SEQ, D_MODEL), 1.0),
        'c': nrm(ks[1], (BATCH, D_MODEL), 1.0),
        'w_ada': nrm(ks[2], (DEPTH, D_MODEL, 6 * D_MODEL), D_MODEL ** -0.5),
        'b_ada': nrm(ks[3], (DEPTH, 6 * D_MODEL), 0.02),
        'g_mix': 1.0 + nrm(ks[4], (DEPTH, D_MODEL), 0.02),
        'g_ffn': 1.0 + nrm(ks[5], (DEPTH, D_MODEL), 0.02),
        'w_in': nrm(ks[6], (DEPTH, D_MODEL, N_IN), D_MODEL ** -0.5),
        'b_fox_f': 2.0 + nrm(ks[7], (DEPTH, N_ATTN_HEADS), 0.5),
        'b_m_i': nrm(ks[8], (DEPTH, N_MLSTM_HEADS), 0.1),
        'b_m_f': 3.0 + nrm(ks[9], (DEPTH, N_MLSTM_HEADS), 0.5),
        'g_q': 1.0 + nrm(ks[10], (DEPTH, HEAD_DIM), 0.02),
        'g_k': 1.0 + nrm(ks[11], (DEPTH, HEAD_DIM), 0.02),
        'g_m_out': 1.0 + nrm(ks[12], (DEPTH, D_MLSTM), 0.02),
        'g_sgu': 1.0 + nrm(ks[13], (DEPTH, D_SGU), 0.02),
        'w_s': nrm(ks[14], (DEPTH, N_SGU_GROUPS, SGU_CHUNK, SGU_CHUNK), SGU_CHUNK ** -0.5),
        'b_s': 1.0 + nrm(ks[15], (DEPTH, N_SGU_GROUPS, SGU_CHUNK), 0.1),
        'w_out': nrm(ks[16], (DEPTH, D_MODEL, D_MODEL), D_MODEL ** -0.5),
        'w_up': nrm(ks[17], (DEPTH, D_MODEL, 2 * D_FF), D_MODEL ** -0.5),
        'w_conv': nrm(ks[18], (DEPTH, CONV_WIDTH, 2 * D_FF), CONV_WIDTH ** -0.5),
        'b_conv': nrm(ks[19], (DEPTH, 2 * D_FF), 0.02),
        'w_down': nrm(ks[20], (DEPTH, D_FF, D_MODEL), D_FF ** -0.5),
    }


def reference(x, c, w_ada, b_ada, g_mix, g_ffn, w_in, b_fox_f, b_m_i, b_m_f, g_q, g_k,
              g_m_out, g_sgu, w_s, b_s, w_out, w_up, w_conv, b_conv, w_down):
    c_act = jax.nn.silu(c)
    for l in range(DEPTH):
        mod = c_act @ w_ada[l] + b_ada[l]
        sh1, sc1, gt1, sh2, sc2, gt2 = [m[:, None, :] for m in jnp.split(mod, 6, axis=-1)]
        h = rms_norm(x, g_mix[l]) * (1.0 + sc1) + sh1
        y = hybrid_mixer(h, w_in[l], b_fox_f[l], b_m_i[l], b_m_f[l], g_q[l], g_k[l],
                         g_m_out[l], g_sgu[l], w_s[l], b_s[l], w_out[l])
        x = x + gt1 * y
        h = rms_norm(x, g_ffn[l]) * (1.0 + sc2) + sh2
        x = x + gt2 * conv_ffn(h, w_up[l], w_conv[l], b_conv[l], w_down[l])
    return x
```

```python
import numpy as np
from contextlib import ExitStack
import concourse.bass as bass
import concourse.mybir as mybir
from concourse.bass_utils import run_bass_kernel_spmd

F32 = mybir.dt.float32
BF16 = mybir.dt.bfloat16
AF = mybir.ActivationFunctionType
ALU = mybir.AluOpType
AX = mybir.AxisListType

D = 1024
NIN = 3088
DFF = 2816
NF = DFF // 128
EPS = 1e-6
SEQ = 8192
DEPTH = 2
NCORES = 8


class Tok:
    __slots__ = ("w", "rs", "excl")

    def __init__(self, excl=False):
        self.w = None
        self.rs = []
        self.excl = excl


def PTok():
    return Tok(excl=True)


class Stream:
    def __init__(self, fw, eng, name, n_dma_sems=0, self_sync=True):
        self.fw = fw
        self.eng = eng
        self.name = name
        self.sem = fw.nc.alloc_semaphore("s_" + name)
        self.cnt = 0
        self.known = {}
        self.self_sync = self_sync
        self.dma_sems = [[fw.nc.alloc_semaphore(f"d_{name}{i}"), 0] for i in range(n_dma_sems)]
        self.dma_rr = 0
        self.n_wait = 0
        self.n_ins = 0

    def wait(self, ev):
        if ev is None:
            return
        sem, val = ev
        if (not self.self_sync) and sem is self.sem:
            return
        if self.known.get(sem, 0) >= val:
            return
        self.eng.wait_ge(sem, val)
        self.known[sem] = val
        self.n_wait += 1

    def _deps(self, r, w):
        for t in r:
            self.wait(t.w)
        for t in w:
            self.wait(t.w)
            for ev in t.rs:
                self.wait(ev)

    def _commit(self, ev, r, w):
        for t in r:
            t.rs.append(ev)
            if len(t.rs) > 32:
                d = {}
                for s, v in t.rs:
                    if d.get(s, 0) < v:
                        d[s] = v
                t.rs = list(d.items())
        for t in w:
            t.w = ev
            t.rs = []

    def op(self, fn, r=(), w=()):
        if any(t.excl for t in r):
            w = list(w) + [t for t in r if t.excl]
            r = [t for t in r if not t.excl]
        self._deps(r, w)
        ins = fn()
        self.cnt += 1
        ins.then_inc(self.sem, 1)
        ev = (self.sem, self.cnt)
        self._commit(ev, r, w)
        self.n_ins += 1
        return ev

    def dma(self, out, in_, r=(), w=(), **kw):
        self._deps(r, w)
        slot = self.dma_sems[self.dma_rr]
        self.dma_rr = (self.dma_rr + 1) % len(self.dma_sems)
        sem, c = slot
        if c > 0:
            self.wait((sem, c))
        ins = self.eng.dma_start(out=out, in_=in_, **kw)
        slot[1] = c + 16
        ins.then_inc(sem, 16)
        ev = (sem, c + 16)
        self._commit(ev, r, w)
        self.n_ins += 1
        return ev


class FW:
    def __init__(self, nc):
        self.nc = nc
        self.pe = Stream(self, nc.tensor, "pe", self_sync=False)
        self.act = Stream(self, nc.scalar, "act", n_dma_sems=4)
        self.dve = Stream(self, nc.vector, "dve")
        self.pool = Stream(self, nc.gpsimd, "pool", n_dma_sems=12)
        self.sp = Stream(self, nc.sync, "sp", n_dma_sems=24)
        self.streams = [self.pe, self.act, self.dve, self.pool, self.sp]

    def all_events(self):
        evs = []
        for s in self.streams:
            if s.cnt > 0:
                evs.append((s.sem, s.cnt))
            for sem, c in s.dma_sems:
                if c > 0:
                    evs.append((sem, c))
        return evs

    def barrier(self):
        evs = self.all_events()
        for s in self.streams:
            for ev in evs:
                if ev[0] is s.sem:
                    continue
                s.wait(ev)

    def finish(self):
        for ev in self.all_events():
            self.sp.wait(ev)


class RR:
    def __init__(self, items):
        self.items = items
        self.i = 0

    def next(self):
        it = self.items[self.i]
        self.i = (self.i + 1) % len(self.items)
        return it


def build(S=SEQ, depth=DEPTH, debug=False, upto=4):
    assert S % 512 == 0
    NB = S // 128
    G = S // 512
    nc = bass.Bass("TRN2", target_bir_lowering=False)
    fw = FW(nc)
    pe, act, dve, pool, sp = fw.pe, fw.act, fw.dve, fw.pool, fw.sp

    def din(name, shape, dt=F32):
        return nc.dram_tensor(name, list(shape), dt, kind="ExternalInput").ap()

    skind = "ExternalOutput" if debug else "Internal"

    def dscr(name, shape, dt):
        return nc.dram_tensor(name, list(shape), dt, kind=skind).ap()

    xT = din("xT", [D, S])
    ccol = din("ccol", [128, 8])
    w_ada = din("w_ada", [depth, D, 6 * D])
    b_adaT = din("b_adaT", [depth, 128, 48])
    g_mixT = din("g_mixT", [depth, 128, 8])
    g_ffnT = din("g_ffnT", [depth, 128, 8])
    w_in = din("w_in", [depth, D, NIN])
    gbias = din("gbias", [depth, 16])
    bfoxT = din("bfoxT", [depth, 8, 1])
    gqk = din("gqk", [depth, 128, 2])
    g_m_out = din("g_m_out", [depth, 256])
    g_sgu = din("g_sgu", [depth, 256])
    w_sT = din("w_sT", [depth, 4, 128, 128])
    b_sT = din("b_sT", [depth, 128, 4])
    w_out = din("w_out", [depth, D, D])
    w_up = din("w_up", [depth, D, 2 * DFF])
    w_down = din("w_down", [depth, DFF, D])
    w_convT = din("w_convT", [depth, 128, 2 * NF, 3])
    b_convT = din("b_convT", [depth, 128, 2 * NF])
    tri_d = din("tri", [128, 128])
    ident_d = din("ident", [128, 128])
    outT = nc.dram_tensor("outT", [D, S], F32, kind="ExternalOutput").ap()

    qT_scr = dscr("qT_scr", [512, S], BF16)
    kT_scr = dscr("kT_scr", [512, S], BF16)
    FT_scr = dscr("FT_scr", [8, S], BF16)
    v_scr = dscr("v_scr", [8, S, 128], BF16)
    yT_scr = dscr("yT_scr", [D, S], BF16)
    x1T_scr = dscr("x1T_scr", [D, S], F32)
    h2T_scr = dscr("h2T_scr", [D, S], BF16)
    xmid_scr = dscr("xmid_scr", [D, S], F32)

    def km(ap):
        return ap.rearrange("(k p) t -> p k t", p=128)

    top = ExitStack()

    uid = [0]

    def sbt(es, name, shape, dt):
        uid[0] += 1
        return es.enter_context(nc.sbuf_tensor(f"{name}_{uid[0]}", list(shape), dt))

    def pst(es, name, shape, dt=F32):
        uid[0] += 1
        return es.enter_context(nc.psum_tensor(f"{name}_{uid[0]}", list(shape), dt))

    tri_f = sbt(top, "tri_f", [128, 128], F32)
    tri4 = sbt(top, "tri4", [128, 4, 128], F32)
    ident_f = sbt(top, "ident_f", [128, 128], F32)
    ident_b = sbt(top, "ident_b", [128, 128], BF16)
    ones_f = sbt(top, "ones_f", [128, 512], F32)
    ones_b = sbt(top, "ones_b", [128, 128], BF16)
    blk_b = sbt(top, "blk_b", [128, 128], BF16)
    mneg_b = sbt(top, "mneg_b", [128, 128], BF16)
    negF = sbt(top, "negF", [128, NB, 8], F32)
    modT = [sbt(top, f"modT{l}", [128, 48], F32) for l in range(depth)]
    A1 = [sbt(top, f"A1_{l}", [128, 8], F32) for l in range(depth)]
    A2 = [sbt(top, f"A2_{l}", [128, 8], F32) for l in range(depth)]
    t_const = Tok()
    t_negF = Tok()
    t_mod = [Tok() for _ in range(depth)]

    sp.dma(tri_f[:], tri_d[:, :], w=[t_const])
    sp.dma(ident_f[:], ident_d[:, :], w=[t_const])
    for h in range(4):
        dve.op(lambda: nc.vector.tensor_copy(out=tri4[:, h, :], in_=tri_f[:]), r=[t_const], w=[t_const])
    dve.op(lambda: nc.vector.tensor_copy(out=ident_b[:], in_=ident_f[:]), r=[t_const], w=[t_const])
    dve.op(lambda: nc.vector.memset(ones_f[:], 1.0), w=[t_const])
    dve.op(lambda: nc.vector.memset(ones_b[:], 1.0), w=[t_const])
    dve.op(lambda: nc.vector.memset(blk_b[:], 0.0), w=[t_const])
    dve.op(lambda: nc.vector.memset(blk_b[0:64, 0:64], 1.0), w=[t_const])
    dve.op(lambda: nc.vector.memset(blk_b[64:128, 64:128], 1.0), w=[t_const])
    dve.op(lambda: nc.vector.tensor_scalar(out=mneg_b[:], in0=tri_f[:], scalar1=-1.0, scalar2=30000.0, op0=ALU.add, op1=ALU.mult),
           r=[t_const], w=[t_const])

    def phase0():
        with ExitStack() as es:
            cc = sbt(es, "cc", [128, 8], F32)
            cact = sbt(es, "cact", [128, 8], F32)
            wa = [sbt(es, f"wa{i}", [128, 8, 512], F32) for i in range(2)]
            t_wa = [Tok(), Tok()]
            mrow = sbt(es, "mrow", [1, 6 * D], F32)
            t_mrow = Tok()
            badd = sbt(es, "badd", [128, 48], F32)
            gm = sbt(es, "gm", [128, 8], F32)
            gf = sbt(es, "gf", [128, 8], F32)
            t_small = Tok()
            prow = [pst(es, f"prow{i}", [1, 512]) for i in range(2)]
            t_prow = [PTok(), PTok()]
            pcol = pst(es, "pcol", [128, 48])
            t_pcol = PTok()
            t_c = Tok()
            sp.dma(cc[:], ccol[:, :], w=[t_c])
            act.op(lambda: nc.scalar.activation(out=cact[:], in_=cc[:], func=AF.Silu), r=[t_c], w=[t_c])
            it = 0
            for l in range(depth):
                sp.dma(badd[:], b_adaT[l], w=[t_small])
                sp.dma(gm[:], g_mixT[l], w=[t_small])
                sp.dma(gf[:], g_ffnT[l], w=[t_small])
                for jg in range(12):
                    b = it % 2
                    it += 1
                    src = w_ada[l].rearrange("(k p) c -> p k c", p=128)[:, :, jg * 512:(jg + 1) * 512]
                    (sp if jg % 2 == 0 else act).dma(wa[b][:], src, w=[t_wa[b]])
                    for k in range(8):
                        pe.op(lambda: nc.tensor.matmul(prow[b][:], lhsT=cact[:, k:k + 1], rhs=wa[b][:, k, :],
                                                       start=(k == 0), stop=(k == 7)),
                              r=[t_c, t_wa[b]], w=[t_prow[b]])
                    dve.op(lambda: nc.vector.tensor_copy(out=mrow[0:1, jg * 512:(jg + 1) * 512], in_=prow[b][:]),
                           r=[t_prow[b]], w=[t_mrow])
                for j in range(48):
                    pe.op(lambda: nc.tensor.matmul(pcol[:, j:j + 1], lhsT=mrow[0:1, j * 128:(j + 1) * 128],
                                                   rhs=ones_f[0:1, 0:1], start=True, stop=True),
                          r=[t_mrow, t_const], w=[t_pcol])
                dve.op(lambda: nc.vector.tensor_tensor(out=modT[l][:], in0=pcol[:], in1=badd[:], op=ALU.add),
                       r=[t_pcol, t_small], w=[t_mod[l]])
                dve.op(lambda: nc.vector.scalar_tensor_tensor(out=A1[l][:], in0=modT[l][:, 8:16], scalar=1.0, in1=gm[:],
                                                              op0=ALU.add, op1=ALU.mult),
                       r=[t_mod[l], t_small], w=[t_mod[l]])
                dve.op(lambda: nc.vector.scalar_tensor_tensor(out=A2[l][:], in0=modT[l][:, 32:40], scalar=1.0, in1=gf[:],
                                                              op0=ALU.add, op1=ALU.mult),
                       r=[t_mod[l], t_small], w=[t_mod[l]])
        fw.barrier()

    def norm_group(xg, t_xg, sq, t_sq, st_ps, t_st, rstd, t_rstd, hT, t_hT, Acol, Bcol, t_ab):
        act.op(lambda: nc.scalar.activation(out=sq[:], in_=xg[:], func=AF.Square), r=[t_xg], w=[t_sq])
        for k in range(8):
            pe.op(lambda: nc.tensor.matmul(st_ps[:], lhsT=ones_b[:], rhs=sq[:, k, :], start=(k == 0), stop=(k == 7)),
                  r=[t_sq, t_const], w=[t_st])
        act.op(lambda: nc.scalar.activation(out=rstd[:], in_=st_ps[:], func=AF.Sqrt, bias=EPS, scale=1.0 / D),
               r=[t_st], w=[t_rstd])
        dve.op(lambda: nc.vector.reciprocal(out=rstd[:], in_=rstd[:]), r=[t_rstd], w=[t_rstd])
        for k in range(8):
            dve.op(lambda: nc.vector.scalar_tensor_tensor(out=xg[:, k, :], in0=xg[:, k, :], scalar=Acol[:, k:k + 1],
                                                          in1=rstd[:], op0=ALU.mult, op1=ALU.mult),
                   r=[t_xg, t_rstd, t_ab], w=[t_xg])
            act.op(lambda: nc.scalar.activation(out=hT[:, k, :], in_=xg[:, k, :], func=AF.Identity,
                                                bias=Bcol[:, k:k + 1], scale=1.0),
                   r=[t_xg, t_ab], w=[t_hT])

    def phase1(l, x_src):
        with ExitStack() as es:
            w_sb = sbt(es, "w_in_sb", [128, 8, NIN], BF16)
            t_w = [Tok() for _ in range(8)]
            for k in range(8):
                pool.dma(w_sb[:, k, :], w_in[l, k * 128:(k + 1) * 128, :], w=[t_w[k]])
            gb = sbt(es, "gb", [128, 16], F32)
            nbfox = sbt(es, "nbfox", [8, 1], F32)
            gq = sbt(es, "gq", [128, 2], F32)
            gmo = sbt(es, "gmo", [128, 256], F32)
            gsg = sbt(es, "gsg", [128, 256], F32)
            wsf = sbt(es, "wsf", [128, 4, 128], F32)
            wsb = sbt(es, "wsb", [128, 4, 128], BF16)
            bs = sbt(es, "bs", [128, 4], F32)
            t_p = Tok()
            sp.dma(gb[:], gbias[l:l + 1, :].partition_broadcast(128), w=[t_p])
            sp.dma(nbfox[:], bfoxT[l], w=[t_p])
            sp.dma(gq[:], gqk[l], w=[t_p])
            sp.dma(gmo[:], g_m_out[l:l + 1, :].partition_broadcast(128), w=[t_p])
            sp.dma(gsg[:], g_sgu[l:l + 1, :].partition_broadcast(128), w=[t_p])
            sp.dma(wsf[:], w_sT[l].rearrange("g s t -> s g t"), w=[t_p])
            sp.dma(bs[:], b_sT[l], w=[t_p])
            dve.op(lambda: nc.vector.tensor_scalar(out=nbfox[:], in0=nbfox[:], scalar1=-1.0, scalar2=None, op0=ALU.mult),
                   r=[t_p], w=[t_p])
            dve.op(lambda: nc.vector.tensor_scalar(out=gq[:, 0:1], in0=gq[:, 0:1], scalar1=0.125, scalar2=None, op0=ALU.mult),
                   r=[t_p], w=[t_p])
            dve.op(lambda: nc.vector.tensor_tensor(out=wsb[:], in0=wsf[:], in1=tri4[:], op=ALU.mult),
                   r=[t_p, t_const], w=[t_p])

            xg = [sbt(es, f"xg{i}", [128, 8, 512], F32) for i in range(2)]
            t_xg = [Tok(), Tok()]
            sq = sbt(es, "sq", [128, 8, 512], BF16)
            t_sq = Tok()
            rstd = sbt(es, "rstd", [128, 512], F32)
            t_rstd = Tok()
            hT = [sbt(es, f"hT{i}", [128, 8, 512], BF16) for i in range(2)]
            t_hT = [Tok(), Tok()]
            sqq = [sbt(es, f"sqq{i}", [128, 512], BF16) for i in range(2)]
            rq = [sbt(es, f"rq{i}", [128, 512], F32) for i in range(2)]
            qn = [sbt(es, f"qn{i}", [128, 512], BF16) for i in range(3)]
            t_sqq = [Tok(), Tok()]
            t_rq = [Tok(), Tok()]
            t_qn = [Tok(), Tok(), Tok()]
            qmT = sbt(es, "qmT", [64, 4, 512], BF16)
            kmT = sbt(es, "kmT", [64, 4, 512], BF16)
            t_qmT = [Tok() for _ in range(4)]
            t_kmT = [Tok() for _ in range(4)]
            fe = sbt(es, "fe", [8, 512], F32)
            fneg = sbt(es, "fneg", [8, 512], F32)
            fcar = sbt(es, "fcar", [8, 1], F32)
            fbf = sbt(es, "fbf", [8, 512], BF16)
            t_fe, t_fneg, t_fcar, t_fbf = Tok(), Tok(), Tok(), Tok()
            dve.op(lambda: nc.vector.memset(fcar[:], 0.0), w=[t_fcar])
            va = [sbt(es, f"va{i}", [128, 8, 128], BF16) for i in range(2)]
            t_va = [Tok(), Tok()]
            for i in range(2):
                pool.op(lambda: nc.gpsimd.memset(va[i][:], 1.0), w=[t_va[i]])
            kmt = sbt(es, "kmt", [128, 256], BF16)
            t_kmt = Tok()
            vaug = sbt(es, "vaug", [128, 4, 68], BF16)
            t_vaug = Tok()
            go = sbt(es, "go", [128, 256], F32)
            t_go = Tok()
            ug = sbt(es, "ug", [128, 256], F32)
            t_ug = Tok()
            vg = sbt(es, "vg", [128, 256], F32)
            t_vg = Tok()
            gt = sbt(es, "gt", [128, 16], F32)
            nl = sbt(es, "nl", [128, 16], F32)
            t_gt, t_nl = Tok(), Tok()
            carF = sbt(es, "carF", [128, 8], F32)
            t_carF = Tok()
            dve.op(lambda: nc.vector.memset(carF[:], 0.0), w=[t_carF])
            sm = sbt(es, "sm", [128, 64], F32)
            t_sm = Tok()
            a_all = sbt(es, "a_all", [128, 4], F32)
            t_a = Tok()
            AT = sbt(es, "AT", [128, 4, 128], BF16)
            t_AT = Tok()
            S_f = sbt(es, "S_f", [64, 4, 68], F32)
            S_b = sbt(es, "S_b", [64, 4, 68], BF16)
            t_Sf, t_Sb = Tok(), Tok()
            dve.op(lambda: nc.vector.memset(S_f[:], 0.0), w=[t_Sf])
            dve.op(lambda: nc.vector.memset(S_b[:], 0.0), w=[t_Sb])
            num_sb = sbt(es, "num_sb", [128, 4, 65], F32)
            t_num = Tok()
            sqn = sbt(es, "sqn", [128, 4, 64], F32)
            t_sqn = Tok()
            ymix = sbt(es, "ymix", [128, 512], BF16)
            t_ymix = Tok()
            ymT = [sbt(es, f"ymT{i}", [128, 4, 512], BF16) for i in range(2)]
            t_ymT = [Tok(), Tok()]
            vtmp = sbt(es, "vtmp", [128, 256], F32)
            t_vtmp = Tok()
            vn = sbt(es, "vn", [128, 256], BF16)
            t_vn = Tok()

            st_ps = pst(es, "st_ps", [128, 512])
            t_st = PTok()
            big = RR([(pst(es, f"big{i}", [128, 512]), PTok()) for i in range(3)])
            sc_ps = pst(es, "sc_ps", [128, 4, 128])
            t_sc = PTok()
            misc = pst(es, "misc", [128, 512])
            nm_ps = misc[:, 0:260].rearrange("p (h c) -> p h c", h=4)
            t_nm = PTok()
            misc2 = pst(es, "misc2", [128, 512])
            cb_ps = misc2[:, 260:292]
            t_cb = PTok()
            dS_ps = misc2[0:64, 0:260].rearrange("p (h c) -> p h c", h=4)
            t_dS = t_cb
            tr_ps = pst(es, "tr_ps", [128, 4, 128], BF16)
            t_tr = PTok()

            def proj_fm(col0, ncols, rhs_hT, t_rhs):
                ps_, t_ = big.next()
                for k in range(8):
                    pe.op(lambda: nc.tensor.matmul(ps_[0:ncols, :], lhsT=w_sb[:, k, col0:col0 + ncols], rhs=rhs_hT[:, k, :],
                                                   start=(k == 0), stop=(k == 7)),
                          r=[t_w[k], t_rhs], w=[t_])
                return ps_, t_

            def proj_tm(col0, ncols, hTg, t_rhs, tb):
                ps_, t_ = big.next()
                for k in range(8):
                    pe.op(lambda: nc.tensor.matmul(ps_[:, 0:ncols], lhsT=hTg[:, k, tb:tb + 128], rhs=w_sb[:, k, col0:col0 + ncols],
                                                   start=(k == 0), stop=(k == 7)),
                          r=[t_w[k], t_rhs], w=[t_])
                return ps_, t_

            xsrc = km(x_src)
            iq = 0
            STG = getattr(build, "p1stage", 99)
            SUB = getattr(build, "p1sub", 99)
            for g in range(G if STG >= 1 else 0):
                b = g % 2
                if g == 0:
                    sp.dma(xg[0][:], xsrc[:, :, 0:512], w=[t_xg[0]])
                if g + 1 < G:
                    sp.dma(xg[1 - b][:], xsrc[:, :, (g + 1) * 512:(g + 2) * 512], w=[t_xg[1 - b]])
                norm_group(xg[b], t_xg[b], sq, t_sq, st_ps, t_st, rstd, t_rstd, hT[b], t_hT[b],
                           A1[l], modT[l][:, 0:8], t_mod[l])
                hTg, t_h = hT[b], t_hT[b]
                if STG < 2:
                    continue
                for which in range(2):
                    for j in range(4):
                        ps_, t_ = proj_fm(which * 512 + j * 128, 128, hTg, t_h)
                        i2 = iq % 2
                        i3 = iq % 3
                        iq += 1
                        act.op(lambda: nc.scalar.activation(out=sqq[i2][:], in_=ps_[:], func=AF.Square), r=[t_], w=[t_sqq[i2]])
                        ps2, t2 = big.next()
                        pe.op(lambda: nc.tensor.matmul(ps2[:], lhsT=blk_b[:], rhs=sqq[i2][:], start=True, stop=True),
                              r=[t_sqq[i2], t_const], w=[t2])
                        act.op(lambda: nc.scalar.activation(out=rq[i2][:], in_=ps2[:], func=AF.Sqrt, bias=EPS, scale=1.0 / 64),
                               r=[t2], w=[t_rq[i2]])
                        dve.op(lambda: nc.vector.reciprocal(out=rq[i2][:], in_=rq[i2][:]), r=[t_rq[i2]], w=[t_rq[i2]])
                        dve.op(lambda: nc.vector.scalar_tensor_tensor(out=qn[i3][:], in0=ps_[:], scalar=gq[:, which:which + 1],
                                                                      in1=rq[i2][:], op0=ALU.mult, op1=ALU.mult),
                               r=[t_, t_rq[i2], t_p], w=[t_qn[i3]])
                        dst = (qT_scr if which == 0 else kT_scr)[j * 128:(j + 1) * 128, g * 512:(g + 1) * 512]
                        sp.dma(dst, qn[i3][:], r=[t_qn[i3]])
                if STG < 3:
                    continue
                for j in range(4):
                    ps_, t_ = proj_fm(1536 + j * 64, 64, hTg, t_h)
                    act.op(lambda: nc.scalar.copy(out=qmT[:, j, :], in_=ps_[0:64, :]), r=[t_], w=[t_qmT[j]])
                    ps_, t_ = proj_fm(1792 + j * 64, 64, hTg, t_h)
                    act.op(lambda: nc.scalar.mul(out=kmT[:, j, :], in_=ps_[0:64, :], mul=0.125), r=[t_], w=[t_kmT[j]])
                ps_, t_ = proj_fm(3072, 8, hTg, t_h)
                act.op(lambda: nc.scalar.activation(out=fe[:], in_=ps_[0:8, :], func=AF.Exp, bias=nbfox[:, 0:1], scale=-1.0),
                       r=[t_, t_p], w=[t_fe])
                act.op(lambda: nc.scalar.activation(out=fe[:], in_=fe[:], func=AF.Ln, bias=1.0, scale=1.0), r=[t_fe], w=[t_fe])
                dve.op(lambda: nc.vector.tensor_tensor_scan(out=fneg[:], data0=ones_f[0:8, :], data1=fe[:], initial=fcar[:, 0:1],
                                                            op0=ALU.mult, op1=ALU.add),
                       r=[t_fe, t_fcar, t_const], w=[t_fneg])
                dve.op(lambda: nc.vector.tensor_copy(out=fcar[:], in_=fneg[:, 511:512]), r=[t_fneg], w=[t_fcar])
                dve.op(lambda: nc.vector.tensor_scalar(out=fbf[:], in0=fneg[:], scalar1=-1.0, scalar2=None, op0=ALU.mult),
                       r=[t_fneg], w=[t_fbf])
                sp.dma(FT_scr[:, g * 512:(g + 1) * 512], fbf[:], r=[t_fbf])

                if STG < 4:
                    continue
                for bi in range(4 if STG >= 5 else 0):
                    blk = g * 4 + bi
                    tb = bi * 128
                    vb = blk % 2
                    ps_g, t_g = proj_tm(2816, 272, hTg, t_h, tb)
                    dve.op(lambda: nc.vector.tensor_tensor(out=gt[:], in0=ps_g[:, 256:272], in1=gb[:], op=ALU.add),
                           r=[t_g, t_p], w=[t_gt])
                    act.op(lambda: nc.scalar.activation(out=vg[:], in_=ps_g[:, 0:256], func=AF.Gelu_apprx_tanh), r=[t_g], w=[t_vg])
                    if SUB < 1:
                        continue
                    act.op(lambda: nc.scalar.activation(out=nl[:], in_=gt[:], func=AF.Exp, scale=-1.0), r=[t_gt], w=[t_nl])
                    act.op(lambda: nc.scalar.activation(out=nl[:], in_=nl[:], func=AF.Ln, bias=1.0, scale=1.0), r=[t_nl], w=[t_nl])
                    if SUB < 2:
                        continue
                    pe.op(lambda: nc.tensor.matmul(cb_ps[:, 0:16], lhsT=tri_f[:], rhs=nl[:], start=True, stop=True),
                          r=[t_nl, t_const], w=[t_cb])
                    pe.op(lambda: nc.tensor.matmul(cb_ps[:, 16:32], lhsT=ones_f[:, 0:128], rhs=nl[:], start=True, stop=True),
                          r=[t_nl, t_const], w=[t_cb])
                    if SUB < 3:
                        continue
                    dve.op(lambda: nc.vector.tensor_tensor(out=negF[:, blk, :], in0=cb_ps[:, 0:8], in1=carF[:], op=ALU.add),
                           r=[t_cb, t_carF], w=[t_negF])
                    dve.op(lambda: nc.vector.tensor_tensor(out=carF[:], in0=cb_ps[:, 16:24], in1=carF[:], op=ALU.add),
                           r=[t_cb, t_carF], w=[t_carF])
                    if SUB < 4:
                        continue
                    dve.op(lambda: nc.vector.tensor_tensor(out=sm[:, 8:12], in0=cb_ps[:, 12:16], in1=gt[:, 8:12], op=ALU.add),
                           r=[t_cb, t_gt], w=[t_sm])
                    act.op(lambda: nc.scalar.activation(out=sm[:, 0:4], in_=sm[:, 8:12], func=AF.Exp), r=[t_sm], w=[t_sm])
                    act.op(lambda: nc.scalar.activation(out=sm[:, 4:8], in_=cb_ps[:, 12:16], func=AF.Exp), r=[t_cb, t_sm], w=[t_sm])
                    act.op(lambda: nc.scalar.activation(out=a_all[:], in_=cb_ps[:, 28:32], func=AF.Exp, scale=-1.0),
                           r=[t_cb], w=[t_a])
                    if SUB < 5:
                        continue
                    ps_k, t_k = proj_tm(1792, 512, hTg, t_h, tb)
                    X5 = getattr(build, "x5", 99)
                    if X5 < 1:
                        continue
                    act.op(lambda: nc.scalar.mul(out=kmt[:], in_=ps_k[:, 0:256], mul=0.125), r=[t_k], w=[t_kmt])
                    if X5 < 2:
                        continue
                    for h in range(4 if X5 >= 3 else 1):
                        dve.op(lambda: nc.vector.tensor_scalar(out=vaug[:, h, 0:64], in0=ps_k[:, 256 + h * 64:320 + h * 64],
                                                               scalar1=sm[:, h:h + 1], scalar2=None, op0=ALU.mult),
                               r=[t_k, t_sm], w=[t_vaug])
                    if X5 < 4:
                        continue
                    dve.op(lambda: nc.vector.tensor_copy(out=vaug[:, :, 64], in_=sm[:, 0:4]), r=[t_sm], w=[t_vaug])
                    if SUB < 6:
                        continue
                    ps_o, t_o = proj_tm(2304, 512, hTg, t_h, tb)
                    act.op(lambda: nc.scalar.activation(out=go[:], in_=ps_o[:, 0:256], func=AF.Sigmoid), r=[t_o], w=[t_go])
                    act.op(lambda: nc.scalar.activation(out=ug[:], in_=ps_o[:, 256:512], func=AF.Gelu_apprx_tanh), r=[t_o], w=[t_ug])
                    dve.op(lambda: nc.vector.tensor_tensor(out=go[:], in0=go[:], in1=gmo[:], op=ALU.mult), r=[t_go, t_p], w=[t_go])
                    if SUB < 7:
                        continue
                    ps_v, t_v = proj_tm(1024, 512, hTg, t_h, tb)
                    act.op(lambda: nc.scalar.copy(out=va[vb][:, :, 0:64], in_=ps_v[:].rearrange("p (h d) -> p h d", h=8)),
                           r=[t_v], w=[t_va[vb]])
                    if not getattr(build, "skip_vscr", False):
                        sp.dma(v_scr[:, blk * 128:(blk + 1) * 128, :].rearrange("h t c -> t h c"), va[vb][:], r=[t_va[vb]])

                    if STG < 6:
                        continue
                    for h in range(4):
                        pe.op(lambda: nc.tensor.matmul(sc_ps[:, h, :], lhsT=kmT[:, h, tb:tb + 128],
                                                       rhs=qmT[:, h, tb:tb + 128], start=True, stop=True),
                              r=[t_kmT[h], t_qmT[h]], w=[t_sc])
                    dve.op(lambda: nc.vector.tensor_tensor(out=AT[:], in0=sc_ps[:], in1=tri4[:], op=ALU.mult),
                           r=[t_sc, t_const], w=[t_AT])
                    X6 = getattr(build, "x6", 99)
                    if X6 < 2:
                        continue
                    for h in range(4):
                        pe.op(lambda: nc.tensor.matmul(nm_ps[:, h, :], lhsT=AT[:, h, :], rhs=vaug[:, h, 0:65], start=True, stop=(X6 < 3)),
                              r=[t_AT, t_vaug], w=[t_nm])
                        if X6 >= 3:
                            pe.op(lambda: nc.tensor.matmul(nm_ps[:, h, :], lhsT=qmT[:, h, tb:tb + 128], rhs=S_b[:, h, 0:65],
                                                           start=False, stop=True),
                                  r=[t_qmT[h], t_Sb], w=[t_nm])
                        if X6 >= 4:
                            pe.op(lambda: nc.tensor.matmul(dS_ps[:, h, :], lhsT=kmt[:, h * 64:(h + 1) * 64], rhs=vaug[:, h, 0:65],
                                                           start=True, stop=True),
                                  r=[t_kmt, t_vaug], w=[t_dS])
                    if X6 < 5:
                        continue
                    dve.op(lambda: nc.vector.tensor_tensor(out=S_f[:, :, 0:65], in0=dS_ps[:], in1=S_f[:, :, 0:65], op=ALU.add),
                           r=[t_dS, t_Sf], w=[t_Sf])
                    for h in range(4):
                        dve.op(lambda: nc.vector.tensor_scalar(out=S_f[:, h, 0:65], in0=S_f[:, h, 0:65],
                                                               scalar1=a_all[0:64, h:h + 1], scalar2=None, op0=ALU.mult),
                               r=[t_Sf, t_a], w=[t_Sf])
                    dve.op(lambda: nc.vector.tensor_copy(out=S_b[:], in_=S_f[:]), r=[t_Sf], w=[t_Sb])
                    if X6 < 6:
                        continue
                    act.op(lambda: nc.scalar.copy(out=num_sb[:], in_=nm_ps[:]), r=[t_nm], w=[t_num])
                    dve.op(lambda: nc.vector.tensor_scalar(out=sm[:, 12:16], in0=num_sb[:, :, 64], scalar1=-1.0, scalar2=None, op0=ALU.mult),
                           r=[t_num, t_sm], w=[t_sm])
                    dve.op(lambda: nc.vector.tensor_tensor(out=sm[:, 16:20], in0=num_sb[:, :, 64], in1=sm[:, 12:16], op=ALU.max),
                           r=[t_num, t_sm], w=[t_sm])
                    dve.op(lambda: nc.vector.tensor_tensor(out=sm[:, 16:20], in0=sm[:, 16:20], in1=sm[:, 4:8], op=ALU.max),
                           r=[t_sm], w=[t_sm])
                    dve.op(lambda: nc.vector.reciprocal(out=sm[:, 16:20], in_=sm[:, 16:20]), r=[t_sm], w=[t_sm])
                    for h in range(4):
                        dve.op(lambda: nc.vector.tensor_scalar(out=num_sb[:, h, 0:64], in0=num_sb[:, h, 0:64],
                                                               scalar1=sm[:, 16 + h:17 + h], scalar2=None, op0=ALU.mult),
                               r=[t_num, t_sm], w=[t_num])
                    dve.op(lambda: nc.vector.tensor_tensor(out=sqn[:], in0=num_sb[:, :, 0:64], in1=num_sb[:, :, 0:64], op=ALU.mult),
                           r=[t_num], w=[t_sqn])
                    dve.op(lambda: nc.vector.tensor_reduce(out=sm[:, 20:24], in_=sqn[:], axis=AX.X, op=ALU.add),
                           r=[t_sqn, t_sm], w=[t_sm])
                    act.op(lambda: nc.scalar.activation(out=sm[:, 24:28], in_=sm[:, 20:24], func=AF.Sqrt, bias=EPS, scale=1.0 / 64),
                           r=[t_sm], w=[t_sm])
                    dve.op(lambda: nc.vector.reciprocal(out=sm[:, 28:32], in_=sm[:, 24:28]), r=[t_sm], w=[t_sm])
                    for h in range(4):
                        dve.op(lambda: nc.vector.scalar_tensor_tensor(out=ymix[:, h * 64:(h + 1) * 64], in0=num_sb[:, h, 0:64],
                                                                      scalar=sm[:, 28 + h:29 + h], in1=go[:, h * 64:(h + 1) * 64],
                                                                      op0=ALU.mult, op1=ALU.mult),
                               r=[t_num, t_sm, t_go], w=[t_ymix])
                    if STG < 7:
                        continue
                    dve.op(lambda: nc.vector.tensor_reduce(out=sm[:, 32:36], in_=vg[:].rearrange("p (g c) -> p g c", g=4), axis=AX.X, op=ALU.add),
                           r=[t_vg, t_sm], w=[t_sm])
                    dve.op(lambda: nc.vector.tensor_tensor(out=vtmp[:], in0=vg[:], in1=vg[:], op=ALU.mult), r=[t_vg], w=[t_vtmp])
                    dve.op(lambda: nc.vector.tensor_reduce(out=sm[:, 36:40], in_=vtmp[:].rearrange("p (g c) -> p g c", g=4), axis=AX.X, op=ALU.add),
                           r=[t_vtmp, t_sm], w=[t_sm])
                    dve.op(lambda: nc.vector.tensor_scalar(out=sm[:, 40:44], in0=sm[:, 32:36], scalar1=1.0 / 64, scalar2=None, op0=ALU.mult),
                           r=[t_sm], w=[t_sm])
                    dve.op(lambda: nc.vector.tensor_tensor(out=sm[:, 44:48], in0=sm[:, 40:44], in1=sm[:, 40:44], op=ALU.mult),
                           r=[t_sm], w=[t_sm])
                    dve.op(lambda: nc.vector.scalar_tensor_tensor(out=sm[:, 44:48], in0=sm[:, 36:40], scalar=1.0 / 64, in1=sm[:, 44:48],
                                                                  op0=ALU.mult, op1=ALU.subtract),
                           r=[t_sm], w=[t_sm])
                    act.op(lambda: nc.scalar.activation(out=sm[:, 44:48], in_=sm[:, 44:48], func=AF.Sqrt, bias=EPS, scale=1.0),
                           r=[t_sm], w=[t_sm])
                    dve.op(lambda: nc.vector.reciprocal(out=sm[:, 44:48], in_=sm[:, 44:48]), r=[t_sm], w=[t_sm])
                    for gi in range(4):
                        dve.op(lambda: nc.vector.tensor_scalar(out=vtmp[:, gi * 64:(gi + 1) * 64], in0=vg[:, gi * 64:(gi + 1) * 64],
                                                               scalar1=sm[:, 40 + gi:41 + gi], scalar2=sm[:, 44 + gi:45 + gi],
                                                               op0=ALU.subtract, op1=ALU.mult),
                               r=[t_vg, t_sm, t_vtmp], w=[t_vtmp])
                    dve.op(lambda: nc.vector.tensor_tensor(out=vn[:], in0=vtmp[:], in1=gsg[:], op=ALU.mult), r=[t_vtmp, t_p], w=[t_vn])
                    mx_ps, t_mx = big.next()
                    for gi in range(4):
                        pe.op(lambda: nc.tensor.matmul(mx_ps[:, gi * 64:(gi + 1) * 64], lhsT=wsb[:, gi, :], rhs=vn[:, gi * 64:(gi + 1) * 64],
                                                       start=True, stop=True),
                              r=[t_vn, t_p], w=[t_mx])
                    for gi in range(4):
                        dve.op(lambda: nc.vector.scalar_tensor_tensor(out=ymix[:, 256 + gi * 64:320 + gi * 64], in0=mx_ps[:, gi * 64:(gi + 1) * 64],
                                                                      scalar=bs[:, gi:gi + 1], in1=ug[:, gi * 64:(gi + 1) * 64],
                                                                      op0=ALU.add, op1=ALU.mult),
                               r=[t_mx, t_ug, t_p], w=[t_ymix])
                    for c in range(4):
                        pe.op(lambda: nc.tensor.transpose(tr_ps[:, c, :], ymix[:, c * 128:(c + 1) * 128], ident_b[:]),
                              r=[t_ymix, t_const], w=[t_tr])
                    act.op(lambda: nc.scalar.copy(out=ymT[b][:, :, tb:tb + 128], in_=tr_ps[:]), r=[t_tr], w=[t_ymT[b]])
                if STG >= 7:
                    sp.dma(yT_scr[512:1024, g * 512:(g + 1) * 512].rearrange("(c p) t -> p c t", p=128), ymT[b][:], r=[t_ymT[b]])
        fw.barrier()

    def phase2(l):
        with ExitStack() as es:
            Qa = [sbt(es, f"Qa{i}", [65, S], BF16) for i in range(2)]
            Ka = [sbt(es, f"Ka{i}", [65, S], BF16) for i in range(2)]
            Va = [sbt(es, f"Va{i}", [128, NB, 128], BF16) for i in range(2)]
            t_Q = [Tok(), Tok()]
            t_K = [Tok(), Tok()]
            t_V = [Tok(), Tok()]
            for i in range(2):
                pool.op(lambda: nc.gpsimd.memset(Ka[i][64:65, :], 1.0), w=[t_K[i]])
            P = RR([(sbt(es, f"P{i}", [128, 512], BF16), Tok()) for i in range(4)])
            rden = [sbt(es, f"rden{i}", [128, 512], F32) for i in range(2)]
            t_rden = [Tok(), Tok()]
            yo = [sbt(es, f"yo{i}", [64, 512], BF16) for i in range(2)]
            t_yo = [Tok(), Tok()]
            scp = RR([(pst(es, f"scp{i}", [128, 512]), PTok()) for i in range(4)])
            op_ = RR([(pst(es, f"op{i}", [128, 512]), PTok()) for i in range(2)])
            io = 0
            for h in range(8):
                hb = h % 2

                def load_head(hh):
                    bb = hh % 2
                    sp.dma(Ka[bb][0:64, :], kT_scr[hh * 64:(hh + 1) * 64, :], w=[t_K[bb]])
                    sp.dma(Qa[bb][0:64, :], qT_scr[hh * 64:(hh + 1) * 64, :], w=[t_Q[bb]])
                    sp.dma(Qa[bb][64:65, :], FT_scr[hh:hh + 1, :], w=[t_Q[bb]])
                    sp.dma(Va[bb][:], v_scr[hh].rearrange("(j p) c -> p j c", p=128), w=[t_V[bb]])
                if h == 0:
                    load_head(0)
                if h + 1 < 8:
                    load_head(h + 1)
                for g in range(G):
                    o_ps, t_o = op_.next()
                    nj = 4 * g + 4
                    for j in range(nj):
                        r_ = max(0, j - 4 * g)
                        c0 = r_ * 128
                        s_ps, t_s = scp.next()
                        diag = j >= 4 * g
                        pe.op(lambda: nc.tensor.matmul(s_ps[:, c0:512], lhsT=Ka[hb][:, j * 128:(j + 1) * 128],
                                                       rhs=Qa[hb][:, g * 512 + c0:(g + 1) * 512], start=True, stop=not diag),
                              r=[t_K[hb], t_Q[hb]], w=[t_s])
                        if diag:
                            pe.op(lambda: nc.tensor.matmul(s_ps[:, c0:c0 + 128], lhsT=ident_b[:], rhs=mneg_b[:], start=False, stop=True),
                                  r=[t_const], w=[t_s])
                        p_sb, t_p_ = P.next()
                        act.op(lambda: nc.scalar.activation(out=p_sb[:, c0:512], in_=s_ps[:, c0:512], func=AF.Exp,
                                                            bias=negF[:, j, h:h + 1], scale=1.0),
                               r=[t_s, t_negF], w=[t_p_])
                        pe.op(lambda: nc.tensor.matmul(o_ps[:, c0:512], lhsT=Va[hb][:, j, :], rhs=p_sb[:, c0:512],
                                                       start=(j == 0), stop=(j == nj - 1)),
                              r=[t_V[hb], t_p_], w=[t_o])
                    ib = io % 2
                    io += 1
                    dve.op(lambda: nc.vector.reciprocal(out=rden[ib][64:128, :], in_=o_ps[64:128, :]), r=[t_o], w=[t_rden[ib]])
                    dve.op(lambda: nc.vector.tensor_tensor(out=yo[ib][:], in0=o_ps[0:64, :], in1=rden[ib][64:128, :], op=ALU.mult),
                           r=[t_o, t_rden[ib]], w=[t_yo[ib]])
                    sp.dma(yT_scr[h * 64:(h + 1) * 64, g * 512:(g + 1) * 512], yo[ib][:], r=[t_yo[ib]])
        fw.barrier()

    def phase3a(l, x_src):
        with ExitStack() as es:
            wo = sbt(es, "wo", [128, 8, D], BF16)
            t_wo = [Tok() for _ in range(8)]
            for k in range(8):
                pool.dma(wo[:, k, :], w_out[l, k * 128:(k + 1) * 128, :], w=[t_wo[k]])
            xg = [sbt(es, f"xg{i}", [128, 8, 512], F32) for i in range(2)]
            t_xg = [Tok(), Tok()]
            x1 = [sbt(es, f"x1{i}", [128, 8, 512], F32) for i in range(2)]
            t_x1 = [Tok(), Tok()]
            yg = [sbt(es, f"yg{i}", [128, 8, 512], BF16) for i in range(2)]
            t_yg = [Tok(), Tok()]
            sq = sbt(es, "sq", [128, 8, 512], BF16)
            t_sq = Tok()
            rstd = sbt(es, "rstd", [128, 512], F32)
            t_rstd = Tok()
            hT = [sbt(es, f"hT{i}", [128, 8, 512], BF16) for i in range(2)]
            t_hT = [Tok(), Tok()]
            st_ps = pst(es, "st_ps", [128, 512])
            t_st = PTok()
            big = RR([(pst(es, f"big{i}", [128, 512]), PTok()) for i in range(4)])
            xsrc = km(x_src)
            for g in range(G):
                b = g % 2
                sl = slice(g * 512, (g + 1) * 512)
                if g == 0:
                    sp.dma(xg[0][:], xsrc[:, :, 0:512], w=[t_xg[0]])
                    sp.dma(yg[0][:], km(yT_scr)[:, :, 0:512], w=[t_yg[0]])
                if g + 1 < G:
                    sl2 = slice((g + 1) * 512, (g + 2) * 512)
                    sp.dma(xg[1 - b][:], xsrc[:, :, sl2], w=[t_xg[1 - b]])
                    sp.dma(yg[1 - b][:], km(yT_scr)[:, :, sl2], w=[t_yg[1 - b]])
                for c in range(8):
                    ps_, t_ = big.next()
                    for k in range(8):
                        pe.op(lambda: nc.tensor.matmul(ps_[:], lhsT=wo[:, k, c * 128:(c + 1) * 128], rhs=yg[b][:, k, :],
                                                       start=(k == 0), stop=(k == 7)),
                              r=[t_wo[k], t_yg[b]], w=[t_])
                    dve.op(lambda: nc.vector.scalar_tensor_tensor(out=x1[b][:, c, :], in0=ps_[:], scalar=modT[l][:, 16 + c:17 + c],
                                                                  in1=xg[b][:, c, :], op0=ALU.mult, op1=ALU.add),
                           r=[t_, t_xg[b], t_mod[l]], w=[t_x1[b]])
                sp.dma(km(x1T_scr)[:, :, sl], x1[b][:], r=[t_x1[b]])
                norm_group(x1[b], t_x1[b], sq, t_sq, st_ps, t_st, rstd, t_rstd, hT[b], t_hT[b],
                           A2[l], modT[l][:, 24:32], t_mod[l])
                sp.dma(km(h2T_scr)[:, :, sl], hT[b][:], r=[t_hT[b]])
        fw.barrier()

    def phase3b(l, x_dst):
        with ExitStack() as es:
            wu = sbt(es, "wu", [128, 8, 2 * DFF], BF16)
            t_wu = [Tok() for _ in range(8)]
            wd = sbt(es, "wd", [128, NF, D], BF16)
            t_wd = [Tok() for _ in range(NF)]
            for k in range(8):
                pool.dma(wu[:, k, :], w_up[l, k * 128:(k + 1) * 128, :], w=[t_wu[k]])
            for f in range(NF):
                pool.dma(wd[:, f, :], w_down[l, f * 128:(f + 1) * 128, :], w=[t_wd[f]])
            wc = sbt(es, "wc", [128, 2 * NF, 3], F32)
            bc = sbt(es, "bc", [128, 2 * NF], F32)
            t_c = Tok()
            sp.dma(wc[:], w_convT[l], w=[t_c])
            sp.dma(bc[:], b_convT[l], w=[t_c])
            tail = sbt(es, "tail", [128, 2 * NF, 2], F32)
            t_tail = [Tok() for _ in range(2 * NF)]
            pool.op(lambda: nc.gpsimd.memset(tail[:], 0.0), w=t_tail)
            hT = [sbt(es, f"hT{i}", [128, 8, 512], BF16) for i in range(2)]
            t_hT = [Tok(), Tok()]
            zs = RR([(sbt(es, f"zs{i}", [128, 514], F32), Tok()) for i in range(4)])
            cv = RR([(sbt(es, f"cv{i}", [128, 512], F32), Tok()) for i in range(4)])
            sa = RR([(sbt(es, f"sa{i}", [128, 512], F32), Tok()) for i in range(2)])
            actT1 = sbt(es, "actT", [128, NF, 512], BF16)
            actT = [actT1, actT1]
            t_actT1 = [Tok() for _ in range(NF)]
            t_actT = [t_actT1, t_actT1]
            x1t = RR([(sbt(es, f"x1t{i}", [128, 512], F32), Tok()) for i in range(3)])
            zp = RR([(pst(es, f"zp{i}", [128, 512]), PTok()) for i in range(5)])
            dp = RR([(pst(es, f"dp{i}", [128, 512]), PTok()) for i in range(3)])
            dst = km(x_dst)
            for g in range(G):
                b = g % 2
                sl = slice(g * 512, (g + 1) * 512)
                if g == 0:
                    sp.dma(hT[0][:], km(h2T_scr)[:, :, 0:512], w=[t_hT[0]])
                if g + 1 < G:
                    sp.dma(hT[1 - b][:], km(h2T_scr)[:, :, (g + 1) * 512:(g + 2) * 512], w=[t_hT[1 - b]])
                for f in range(NF):
                    cvs = []
                    for br in range(2):
                        ti = br * NF + f
                        z_ps, t_z = zp.next()
                        for k in range(8):
                            pe.op(lambda: nc.tensor.matmul(z_ps[:], lhsT=wu[:, k, ti * 128:(ti + 1) * 128], rhs=hT[b][:, k, :],
                                                           start=(k == 0), stop=(k == 7)),
                                  r=[t_wu[k], t_hT[b]], w=[t_z])
                        z_sb, t_zs = zs.next()
                        act.op(lambda: nc.scalar.copy(out=z_sb[:, 2:514], in_=z_ps[:]), r=[t_z], w=[t_zs])
                        pool.op(lambda: nc.gpsimd.tensor_copy(out=z_sb[:, 0:2], in_=tail[:, ti, :]), r=[t_tail[ti]], w=[t_zs])
                        pool.op(lambda: nc.gpsimd.tensor_copy(out=tail[:, ti, :], in_=z_sb[:, 512:514]), r=[t_zs], w=[t_tail[ti]])
                        c_sb, t_cv = cv.next()
                        dve.op(lambda: nc.vector.tensor_scalar(out=c_sb[:], in0=z_sb[:, 2:514], scalar1=wc[:, ti, 2:3],
                                                               scalar2=bc[:, ti:ti + 1], op0=ALU.mult, op1=ALU.add),
                               r=[t_zs, t_c], w=[t_cv])
                        dve.op(lambda: nc.vector.scalar_tensor_tensor(out=c_sb[:], in0=z_sb[:, 1:513], scalar=wc[:, ti, 1:2],
                                                                      in1=c_sb[:], op0=ALU.mult, op1=ALU.add),
                               r=[t_zs, t_c, t_cv], w=[t_cv])
                        dve.op(lambda: nc.vector.scalar_tensor_tensor(out=c_sb[:], in0=z_sb[:, 0:512], scalar=wc[:, ti, 0:1],
                                                                      in1=c_sb[:], op0=ALU.mult, op1=ALU.add),
                               r=[t_zs, t_c, t_cv], w=[t_cv])
                        cvs.append((c_sb, t_cv))
                    s_sb, t_sa = sa.next()
                    act.op(lambda: nc.scalar.activation(out=s_sb[:], in_=cvs[0][0][:], func=AF.Silu), r=[cvs[0][1]], w=[t_sa])
                    pool.op(lambda: nc.gpsimd.tensor_tensor(out=actT[b][:, f, :], in0=s_sb[:], in1=cvs[1][0][:], op=ALU.mult),
                            r=[t_sa, cvs[1][1]], w=[t_actT[b][f]])
                for c in range(8):
                    xt_, t_xt = x1t.next()
                    sp.dma(xt_[:], x1T_scr[c * 128:(c + 1) * 128, sl], w=[t_xt])
                    d_ps, t_d = dp.next()
                    for f in range(NF):
                        pe.op(lambda: nc.tensor.matmul(d_ps[:], lhsT=wd[:, f, c * 128:(c + 1) * 128], rhs=actT[b][:, f, :],
                                                       start=(f == 0), stop=(f == NF - 1)),
                              r=[t_wd[f], t_actT[b][f]], w=[t_d])
                    dve.op(lambda: nc.vector.scalar_tensor_tensor(out=xt_[:], in0=d_ps[:], scalar=modT[l][:, 40 + c:41 + c],
                                                                  in1=xt_[:], op0=ALU.mult, op1=ALU.add),
                           r=[t_d, t_xt, t_mod[l]], w=[t_xt])
                    sp.dma(x_dst[c * 128:(c + 1) * 128, sl], xt_[:], r=[t_xt])
        fw.barrier()

    phase0()
    cur = xT
    for l in range(depth):
        nxt = outT if l == depth - 1 else xmid_scr
        if upto >= 1:
            phase1(l, cur)
        if upto >= 2:
            phase2(l)
        if upto >= 3:
            phase3a(l, cur)
        if upto >= 4:
            phase3b(l, nxt)
        cur = nxt
    if upto < 4:
        with ExitStack() as es:
            dbg = sbt(es, "dbg", [128, 48], F32)
            t_dbg = Tok()
            dve.op(lambda: nc.vector.tensor_copy(out=dbg[:], in_=modT[0][:]), r=[t_mod[0]], w=[t_dbg])
            sp.dma(outT[0:128, 0:48], dbg[:], r=[t_dbg])
            dve.op(lambda: nc.vector.tensor_copy(out=dbg[:, 0:8], in_=A1[0][:]), r=[t_mod[0], t_dbg], w=[t_dbg])
            dve.op(lambda: nc.vector.tensor_copy(out=dbg[:, 8:16], in_=A2[0][:]), r=[t_mod[0], t_dbg], w=[t_dbg])
            sp.dma(outT[128:256, 0:48], dbg[:], r=[t_dbg])
            fw.finish()
    fw.finish()
    top.close()
    build.stats = {s.name: (s.n_ins, s.n_wait) for s in fw.streams}
    return nc


def prep_shared(inputs, depth=DEPTH):
    f = lambda a: np.ascontiguousarray(np.asarray(a, dtype=np.float32))
    sh = {}
    sh["w_ada"] = f(inputs["w_ada"])
    sh["b_adaT"] = f(np.asarray(inputs["b_ada"]).reshape(depth, 48, 128).transpose(0, 2, 1))
    sh["g_mixT"] = f(np.asarray(inputs["g_mix"]).reshape(depth, 8, 128).transpose(0, 2, 1))
    sh["g_ffnT"] = f(np.asarray(inputs["g_ffn"]).reshape(depth, 8, 128).transpose(0, 2, 1))
    sh["w_in"] = f(inputs["w_in"])
    sh["gbias"] = f(np.concatenate([np.asarray(inputs["b_fox_f"]), np.asarray(inputs["b_m_i"]), np.asarray(inputs["b_m_f"])], axis=1))
    sh["bfoxT"] = f(np.asarray(inputs["b_fox_f"])[:, :, None])
    gq = np.tile(np.asarray(inputs["g_q"]), (1, 2))
    gk = np.tile(np.asarray(inputs["g_k"]), (1, 2))
    sh["gqk"] = f(np.stack([gq, gk], axis=-1))
    sh["g_m_out"] = f(inputs["g_m_out"])
    sh["g_sgu"] = f(inputs["g_sgu"])
    sh["w_sT"] = f(np.asarray(inputs["w_s"]).transpose(0, 1, 3, 2))
    sh["b_sT"] = f(np.asarray(inputs["b_s"]).transpose(0, 2, 1))
    sh["w_out"] = f(inputs["w_out"])
    sh["w_up"] = f(inputs["w_up"])
    sh["w_down"] = f(inputs["w_down"])
    sh["w_convT"] = f(np.asarray(inputs["w_conv"]).reshape(depth, 3, 2 * NF, 128).transpose(0, 3, 2, 1))
    sh["b_convT"] = f(np.asarray(inputs["b_conv"]).reshape(depth, 2 * NF, 128).transpose(0, 2, 1))
    sh["tri"] = np.triu(np.ones((128, 128), np.float32))
    sh["ident"] = np.eye(128, dtype=np.float32)
    return sh


def prep_core(x_b, c_b):
    return {"xT": np.ascontiguousarray(np.asarray(x_b, np.float32).T),
            "ccol": np.ascontiguousarray(np.asarray(c_b, np.float32).reshape(8, 128).T)}


def kernel(**inputs):
    x = np.asarray(inputs["x"])
    c = np.asarray(inputs["c"])
    B, S, _ = x.shape
    nc = build(S=S, depth=DEPTH)
    sh = prep_shared(inputs)
    in_maps = []
    for b in range(B):
        m = dict(sh)
        m.update(prep_core(x[b], c[b]))
        in_maps.append(m)
    res = run_bass_kernel_spmd(nc, in_maps, core_ids=list(range(B)))
    out = np.stack([np.asarray(r["outT"]).T for r in res.results], axis=0)
    return np.ascontiguousarray(out.astype(np.float32))
```

```python
import numpy as np
from contextlib import ExitStack
import concourse.bass as bass
import concourse.mybir as mybir
from concourse.bass_utils import run_bass_kernel_spmd

F32 = mybir.dt.float32
BF16 = mybir.dt.bfloat16
AF = mybir.ActivationFunctionType
ALU = mybir.AluOpType
AX = mybir.AxisListType

D = 1024
NIN = 3088
DFF = 2816
NF = DFF // 128
EPS = 1e-6
SEQ = 8192
DEPTH = 2
NCORES = 8


class Tok:
    __slots__ = ("w", "rs", "excl")

    def __init__(self, excl=False):
        self.w = None
        self.rs = []
        self.excl = excl


def PTok():
    return Tok(excl=True)


class Stream:
    def __init__(self, fw, eng, name, n_dma_sems=0, self_sync=True):
        self.fw = fw
        self.eng = eng
        self.name = name
        self.sem = fw.nc.alloc_semaphore("s_" + name)
        self.cnt = 0
        self.known = {}
        self.self_sync = self_sync
        self.dma_sems = [[fw.nc.alloc_semaphore(f"d_{name}{i}"), 0] for i in range(n_dma_sems)]
        self.dma_rr = 0
        self.n_wait = 0
        self.n_ins = 0

    def wait(self, ev):
        if ev is None:
            return
        sem, val = ev
        if (not self.self_sync) and sem is self.sem:
            return
        if self.known.get(sem, 0) >= val:
            return
        self.eng.wait_ge(sem, val)
        self.known[sem] = val
        self.n_wait += 1

    def _deps(self, r, w):
        for t in r:
            self.wait(t.w)
        for t in w:
            self.wait(t.w)
            for ev in t.rs:
                self.wait(ev)

    def _commit(self, ev, r, w):
        for t in r:
            t.rs.append(ev)
            if len(t.rs) > 32:
                d = {}
                for s, v in t.rs:
                    if d.get(s, 0) < v:
                        d[s] = v
                t.rs = list(d.items())
        for t in w:
            t.w = ev
            t.rs = []

    def op(self, fn, r=(), w=()):
        if any(t.excl for t in r):
            w = list(w) + [t for t in r if t.excl]
            r = [t for t in r if not t.excl]
        self._deps(r, w)
        ins = fn()
        self.cnt += 1
        ins.then_inc(self.sem, 1)
        ev = (self.sem, self.cnt)
        self._commit(ev, r, w)
        self.n_ins += 1
        return ev

    def dma(self, out, in_, r=(), w=(), **kw):
        self._deps(r, w)
        slot = self.dma_sems[self.dma_rr]
        self.dma_rr = (self.dma_rr + 1) % len(self.dma_sems)
        sem, c = slot
        if c > 0:
            self.wait((sem, c))
        ins = self.eng.dma_start(out=out, in_=in_, **kw)
        slot[1] = c + 16
        ins.then_inc(sem, 16)
        ev = (sem, c + 16)
        self._commit(ev, r, w)
        self.n_ins += 1
        return ev


class FW:
    def __init__(self, nc):
        self.nc = nc
        self.pe = Stream(self, nc.tensor, "pe", self_sync=False)
        self.act = Stream(self, nc.scalar, "act", n_dma_sems=4)
        self.dve = Stream(self, nc.vector, "dve")
        self.pool = Stream(self, nc.gpsimd, "pool", n_dma_sems=12)
        self.sp = Stream(self, nc.sync, "sp", n_dma_sems=24)
        self.streams = [self.pe, self.act, self.dve, self.pool, self.sp]

    def all_events(self):
        evs = []
        for s in self.streams:
            if s.cnt > 0:
                evs.append((s.sem, s.cnt))
            for sem, c in s.dma_sems:
                if c > 0:
                    evs.append((sem, c))
        return evs

    def barrier(self):
        evs = self.all_events()
        for s in self.streams:
            for ev in evs:
                if ev[0] is s.sem:
                    continue
                s.wait(ev)

    def finish(self):
        for ev in self.all_events():
            self.sp.wait(ev)


class RR:
    def __init__(self, items):
        self.items = items
        self.i = 0

    def next(self):
        it = self.items[self.i]
        self.i = (self.i + 1) % len(self.items)
        return it


def build(S=SEQ, depth=DEPTH, debug=False, upto=4):
    assert S % 512 == 0
    NB = S // 128
    G = S // 512
    nc = bass.Bass("TRN2", target_bir_lowering=False)
    fw = FW(nc)
    pe, act, dve, pool, sp = fw.pe, fw.act, fw.dve, fw.pool, fw.sp

    def din(name, shape, dt=F32):
        return nc.dram_tensor(name, list(shape), dt, kind="ExternalInput").ap()

    skind = "ExternalOutput" if debug else "Internal"

    def dscr(name, shape, dt):
        return nc.dram_tensor(name, list(shape), dt, kind=skind).ap()

    xT = din("xT", [D, S])
    ccol = din("ccol", [128, 8])
    w_ada = din("w_ada", [depth, D, 6 * D])
    b_adaT = din("b_adaT", [depth, 128, 48])
    g_mixT = din("g_mixT", [depth, 128, 8])
    g_ffnT = din("g_ffnT", [depth, 128, 8])
    w_in = din("w_in", [depth, D, NIN])
    gbias = din("gbias", [depth, 16])
    bfoxT = din("bfoxT", [depth, 8, 1])
    gqk = din("gqk", [depth, 128, 2])
    g_m_out = din("g_m_out", [depth, 256])
    g_sgu = din("g_sgu", [depth, 256])
    w_sT = din("w_sT", [depth, 4, 128, 128])
    b_sT = din("b_sT", [depth, 128, 4])
    w_out = din("w_out", [depth, D, D])
    w_up = din("w_up", [depth, D, 2 * DFF])
    w_down = din("w_down", [depth, DFF, D])
    w_convT = din("w_convT", [depth, 128, 2 * NF, 3])
    b_convT = din("b_convT", [depth, 128, 2 * NF])
    tri_d = din("tri", [128, 128])
    ident_d = din("ident", [128, 128])
    outT = nc.dram_tensor("outT", [D, S], F32, kind="ExternalOutput").ap()

    qT_scr = dscr("qT_scr", [512, S], BF16)
    kT_scr = dscr("kT_scr", [512, S], BF16)
    FT_scr = dscr("FT_scr", [8, S], BF16)
    v_scr = dscr("v_scr", [8, S, 128], BF16)
    yT_scr = dscr("yT_scr", [D, S], BF16)
    x1T_scr = dscr("x1T_scr", [D, S], F32)
    h2T_scr = dscr("h2T_scr", [D, S], BF16)
    xmid_scr = dscr("xmid_scr", [D, S], F32)

    def km(ap):
        return ap.rearrange("(k p) t -> p k t", p=128)

    top = ExitStack()

    uid = [0]

    def sbt(es, name, shape, dt):
        uid[0] += 1
        return es.enter_context(nc.sbuf_tensor(f"{name}_{uid[0]}", list(shape), dt))

    def pst(es, name, shape, dt=F32):
        uid[0] += 1
        return es.enter_context(nc.psum_tensor(f"{name}_{uid[0]}", list(shape), dt))

    tri_f = sbt(top, "tri_f", [128, 128], F32)
    tri4 = sbt(top, "tri4", [128, 4, 128], F32)
    ident_f = sbt(top, "ident_f", [128, 128], F32)
    ident_b = sbt(top, "ident_b", [128, 128], BF16)
    ones_f = sbt(top, "ones_f", [128, 512], F32)
    ones_b = sbt(top, "ones_b", [128, 128], BF16)
    blk_b = sbt(top, "blk_b", [128, 128], BF16)
    mneg_b = sbt(top, "mneg_b", [128, 128], BF16)
    negF = sbt(top, "negF", [128, NB, 8], F32)
    modT = [sbt(top, f"modT{l}", [128, 48], F32) for l in range(depth)]
    A1 = [sbt(top, f"A1_{l}", [128, 8], F32) for l in range(depth)]
    A2 = [sbt(top, f"A2_{l}", [128, 8], F32) for l in range(depth)]
    t_const = Tok()
    t_negF = Tok()
    t_mod = [Tok() for _ in range(depth)]

    sp.dma(tri_f[:], tri_d[:, :], w=[t_const])
    sp.dma(ident_f[:], ident_d[:, :], w=[t_const])
    for h in range(4):
        dve.op(lambda: nc.vector.tensor_copy(out=tri4[:, h, :], in_=tri_f[:]), r=[t_const], w=[t_const])
    dve.op(lambda: nc.vector.tensor_copy(out=ident_b[:], in_=ident_f[:]), r=[t_const], w=[t_const])
    dve.op(lambda: nc.vector.memset(ones_f[:], 1.0), w=[t_const])
    dve.op(lambda: nc.vector.memset(ones_b[:], 1.0), w=[t_const])
    dve.op(lambda: nc.vector.memset(blk_b[:], 0.0), w=[t_const])
    dve.op(lambda: nc.vector.memset(blk_b[0:64, 0:64], 1.0), w=[t_const])
    dve.op(lambda: nc.vector.memset(blk_b[64:128, 64:128], 1.0), w=[t_const])
    dve.op(lambda: nc.vector.tensor_scalar(out=mneg_b[:], in0=tri_f[:], scalar1=-1.0, scalar2=30000.0, op0=ALU.add, op1=ALU.mult),
           r=[t_const], w=[t_const])

    def phase0():
        with ExitStack() as es:
            cc = sbt(es, "cc", [128, 8], F32)
            cact = sbt(es, "cact", [128, 8], F32)
            wa = [sbt(es, f"wa{i}", [128, 8, 512], F32) for i in range(2)]
            t_wa = [Tok(), Tok()]
            mrow = sbt(es, "mrow", [1, 6 * D], F32)
            t_mrow = Tok()
            badd = sbt(es, "badd", [128, 48], F32)
            gm = sbt(es, "gm", [128, 8], F32)
            gf = sbt(es, "gf", [128, 8], F32)
            t_small = Tok()
            prow = [pst(es, f"prow{i}", [1, 512]) for i in range(2)]
            t_prow = [PTok(), PTok()]
            pcol = pst(es, "pcol", [128, 48])
            t_pcol = PTok()
            t_c = Tok()
            sp.dma(cc[:], ccol[:, :], w=[t_c])
            act.op(lambda: nc.scalar.activation(out=cact[:], in_=cc[:], func=AF.Silu), r=[t_c], w=[t_c])
            it = 0
            for l in range(depth):
                sp.dma(badd[:], b_adaT[l], w=[t_small])
                sp.dma(gm[:], g_mixT[l], w=[t_small])
                sp.dma(gf[:], g_ffnT[l], w=[t_small])
                for jg in range(12):
                    b = it % 2
                    it += 1
                    src = w_ada[l].rearrange("(k p) c -> p k c", p=128)[:, :, jg * 512:(jg + 1) * 512]
                    (sp if jg % 2 == 0 else act).dma(wa[b][:], src, w=[t_wa[b]])
                    for k in range(8):
                        pe.op(lambda: nc.tensor.matmul(prow[b][:], lhsT=cact[:, k:k + 1], rhs=wa[b][:, k, :],
                                                       start=(k == 0), stop=(k == 7)),
                              r=[t_c, t_wa[b]], w=[t_prow[b]])
                    dve.op(lambda: nc.vector.tensor_copy(out=mrow[0:1, jg * 512:(jg + 1) * 512], in_=prow[b][:]),
                           r=[t_prow[b]], w=[t_mrow])
                for j in range(48):
                    pe.op(lambda: nc.tensor.matmul(pcol[:, j:j + 1], lhsT=mrow[0:1, j * 128:(j + 1) * 128],
                                                   rhs=ones_f[0:1, 0:1], start=True, stop=True),
                          r=[t_mrow, t_const], w=[t_pcol])
                dve.op(lambda: nc.vector.tensor_tensor(out=modT[l][:], in0=pcol[:], in1=badd[:], op=ALU.add),
                       r=[t_pcol, t_small], w=[t_mod[l]])
                dve.op(lambda: nc.vector.scalar_tensor_tensor(out=A1[l][:], in0=modT[l][:, 8:16], scalar=1.0, in1=gm[:],
                                                              op0=ALU.add, op1=ALU.mult),
                       r=[t_mod[l], t_small], w=[t_mod[l]])
                dve.op(lambda: nc.vector.scalar_tensor_tensor(out=A2[l][:], in0=modT[l][:, 32:40], scalar=1.0, in1=gf[:],
                                                              op0=ALU.add, op1=ALU.mult),
                       r=[t_mod[l], t_small], w=[t_mod[l]])
        fw.barrier()

    def norm_group(xg, t_xg, sq, t_sq, st_ps, t_st, rstd, t_rstd, hT, t_hT, Acol, Bcol, t_ab):
        act.op(lambda: nc.scalar.activation(out=sq[:], in_=xg[:], func=AF.Square), r=[t_xg], w=[t_sq])
        for k in range(8):
            pe.op(lambda: nc.tensor.matmul(st_ps[:], lhsT=ones_b[:], rhs=sq[:, k, :], start=(k == 0), stop=(k == 7)),
                  r=[t_sq, t_const], w=[t_st])
        act.op(lambda: nc.scalar.activation(out=rstd[:], in_=st_ps[:], func=AF.Sqrt, bias=EPS, scale=1.0 / D),
               r=[t_st], w=[t_rstd])
        dve.op(lambda: nc.vector.reciprocal(out=rstd[:], in_=rstd[:]), r=[t_rstd], w=[t_rstd])
        for k in range(8):
            dve.op(lambda: nc.vector.scalar_tensor_tensor(out=xg[:, k, :], in0=xg[:, k, :], scalar=Acol[:, k:k + 1],
                                                          in1=rstd[:], op0=ALU.mult, op1=ALU.mult),
                   r=[t_xg, t_rstd, t_ab], w=[t_xg])
            act.op(lambda: nc.scalar.activation(out=hT[:, k, :], in_=xg[:, k, :], func=AF.Identity,
                                                bias=Bcol[:, k:k + 1], scale=1.0),
                   r=[t_xg, t_ab], w=[t_hT])

    def phase1(l, x_src):
        with ExitStack() as es:
            w_sb = sbt(es, "w_in_sb", [128, 8, NIN], BF16)
            t_w = [Tok() for _ in range(8)]
            for k in range(8):
                pool.dma(w_sb[:, k, :], w_in[l, k * 128:(k + 1) * 128, :], w=[t_w[k]])
            gb = sbt(es, "gb", [128, 16], F32)
            nbfox = sbt(es, "nbfox", [8, 1], F32)
            gq = sbt(es, "gq", [128, 2], F32)
            gmo = sbt(es, "gmo", [128, 256], F32)
            gsg = sbt(es, "gsg", [128, 256], F32)
            wsf = sbt(es, "wsf", [128, 4, 128], F32)
            wsb = sbt(es, "wsb", [128, 4, 128], BF16)
            bs = sbt(es, "bs", [128, 4], F32)
            t_p = Tok()
            sp.dma(gb[:], gbias[l:l + 1, :].partition_broadcast(128), w=[t_p])
            sp.dma(nbfox[:], bfoxT[l], w=[t_p])
            sp.dma(gq[:], gqk[l], w=[t_p])
            sp.dma(gmo[:], g_m_out[l:l + 1, :].partition_broadcast(128), w=[t_p])
            sp.dma(gsg[:], g_sgu[l:l + 1, :].partition_broadcast(128), w=[t_p])
            sp.dma(wsf[:], w_sT[l].rearrange("g s t -> s g t"), w=[t_p])
            sp.dma(bs[:], b_sT[l], w=[t_p])
            dve.op(lambda: nc.vector.tensor_scalar(out=nbfox[:], in0=nbfox[:], scalar1=-1.0, scalar2=None, op0=ALU.mult),
                   r=[t_p], w=[t_p])
            dve.op(lambda: nc.vector.tensor_scalar(out=gq[:, 0:1], in0=gq[:, 0:1], scalar1=0.125, scalar2=None, op0=ALU.mult),
                   r=[t_p], w=[t_p])
            dve.op(lambda: nc.vector.tensor_tensor(out=wsb[:], in0=wsf[:], in1=tri4[:], op=ALU.mult),
                   r=[t_p, t_const], w=[t_p])

            xg = [sbt(es, f"xg{i}", [128, 8, 512], F32) for i in range(2)]
            t_xg = [Tok(), Tok()]
            sq = sbt(es, "sq", [128, 8, 512], BF16)
            t_sq = Tok()
            rstd = sbt(es, "rstd", [128, 512], F32)
            t_rstd = Tok()
            hT = [sbt(es, f"hT{i}", [128, 8, 512], BF16) for i in range(2)]
            t_hT = [Tok(), Tok()]
            sqq = [sbt(es, f"sqq{i}", [128, 512], BF16) for i in range(3)]
            rq = [sbt(es, f"rq{i}", [128, 512], F32) for i in range(3)]
            qn = [sbt(es, f"qn{i}", [128, 512], BF16) for i in range(3)]
            t_sqq = [Tok() for _ in range(3)]
            t_rq = [Tok() for _ in range(3)]
            t_qn = [Tok() for _ in range(3)]
            qmT = [sbt(es, f"qmT{i}", [64, 4, 512], BF16) for i in range(2)]
            kmT = [sbt(es, f"kmT{i}", [64, 4, 512], BF16) for i in range(2)]
            t_qmT = [[Tok() for _ in range(4)] for _ in range(2)]
            t_kmT = [[Tok() for _ in range(4)] for _ in range(2)]
            fe = sbt(es, "fe", [8, 512], F32)
            fneg = sbt(es, "fneg", [8, 512], F32)
            fcar = sbt(es, "fcar", [8, 1], F32)
            fbf = sbt(es, "fbf", [8, 512], BF16)
            t_fe, t_fneg, t_fcar, t_fbf = Tok(), Tok(), Tok(), Tok()
            dve.op(lambda: nc.vector.memset(fcar[:], 0.0), w=[t_fcar])
            ymT = [sbt(es, f"ymT{i}", [128, 4, 512], BF16) for i in range(2)]
            t_ymT = [Tok(), Tok()]

            class BB:
                pass
            bbs = []
            for i in range(2):
                o = BB()
                o.va = sbt(es, f"va{i}", [128, 8, 128], BF16)
                o.t_va = Tok()
                pool.op(lambda: nc.gpsimd.memset(o.va[:], 1.0), w=[o.t_va])
                o.kmt = sbt(es, f"kmt{i}", [128, 256], BF16)
                o.t_kmt = Tok()
                o.vaug = sbt(es, f"vaug{i}", [128, 4, 68], BF16)
                o.t_vaug = Tok()
                o.go = sbt(es, f"go{i}", [128, 256], F32)
                o.t_go = Tok()
                o.ug = sbt(es, f"ug{i}", [128, 256], F32)
                o.t_ug = Tok()
                o.vg = sbt(es, f"vg{i}", [128, 256], F32)
                o.t_vg = Tok()
                o.gt = sbt(es, f"gt{i}", [128, 16], F32)
                o.nl = sbt(es, f"nl{i}", [128, 16], F32)
                o.t_gt, o.t_nl = Tok(), Tok()
                o.sm = sbt(es, f"sm{i}", [128, 64], F32)
                o.t_smA = Tok()
                o.t_smB = Tok()
                o.a_all = sbt(es, f"a_all{i}", [128, 4], F32)
                o.t_a = Tok()
                bbs.append(o)
            carF = sbt(es, "carF", [128, 8], F32)
            t_carF = Tok()
            dve.op(lambda: nc.vector.memset(carF[:], 0.0), w=[t_carF])
            AT = sbt(es, "AT", [128, 4, 128], BF16)
            t_AT = Tok()
            S_f = sbt(es, "S_f", [64, 4, 68], F32)
            S_b = sbt(es, "S_b", [64, 4, 68], BF16)
            t_Sf, t_Sb = Tok(), Tok()
            dve.op(lambda: nc.vector.memset(S_f[:], 0.0), w=[t_Sf])
            dve.op(lambda: nc.vector.memset(S_b[:], 0.0), w=[t_Sb])
            num_sb = sbt(es, "num_sb", [128, 4, 65], F32)
            t_num = Tok()
            sqn = sbt(es, "sqn", [128, 4, 64], F32)
            t_sqn = Tok()
            ymix = sbt(es, "ymix", [128, 512], BF16)
            t_ymix = Tok()
            vtmp = sbt(es, "vtmp", [128, 256], F32)
            t_vtmp = Tok()
            vsq = sbt(es, "vsq", [128, 256], F32)
            t_vsq = Tok()
            vn = sbt(es, "vn", [128, 256], BF16)
            t_vn = Tok()

            st_ps = pst(es, "st_ps", [128, 512])
            t_st = PTok()
            big = RR([(pst(es, f"big{i}", [128, 512]), PTok()) for i in range(3)])
            sc_ps = pst(es, "sc_ps", [128, 4, 128])
            t_sc = PTok()
            misc = pst(es, "misc", [128, 512])
            nm_ps = misc[:, 0:260].rearrange("p (h c) -> p h c", h=4)
            t_nm = PTok()
            misc2 = pst(es, "misc2", [128, 512])
            cb_ps = misc2[:, 260:292]
            t_cb = PTok()
            dS_ps = misc2[0:64, 0:260].rearrange("p (h c) -> p h c", h=4)
            t_dS = t_cb
            tr_ps = pst(es, "tr_ps", [128, 4, 128], BF16)
            t_tr = PTok()

            def proj_fm(col0, ncols, rhs_hT, t_rhs):
                ps_, t_ = big.next()
                for k in range(8):
                    pe.op(lambda: nc.tensor.matmul(ps_[0:ncols, :], lhsT=w_sb[:, k, col0:col0 + ncols], rhs=rhs_hT[:, k, :],
                                                   start=(k == 0), stop=(k == 7)),
                          r=[t_w[k], t_rhs], w=[t_])
                return ps_, t_

            def proj_tm(col0, ncols, hTg, t_rhs, tb):
                ps_, t_ = big.next()
                for k in range(8):
                    pe.op(lambda: nc.tensor.matmul(ps_[:, 0:ncols], lhsT=hTg[:, k, tb:tb + 128], rhs=w_sb[:, k, col0:col0 + ncols],
                                                   start=(k == 0), stop=(k == 7)),
                          r=[t_w[k], t_rhs], w=[t_])
                return ps_, t_

            xsrc = km(x_src)

            def load_x(g):
                sp.dma(xg[g % 2][:], xsrc[:, :, g * 512:(g + 1) * 512], w=[t_xg[g % 2]])

            def norm_pre(g):
                b = g % 2
                act.op(lambda: nc.scalar.activation(out=sq[:], in_=xg[b][:], func=AF.Square), r=[t_xg[b]], w=[t_sq])

            def norm_post(g):
                b = g % 2
                Acol, Bcol, t_ab = A1[l], modT[l][:, 0:8], t_mod[l]
                for k in range(8):
                    pe.op(lambda: nc.tensor.matmul(st_ps[:], lhsT=ones_b[:], rhs=sq[:, k, :], start=(k == 0), stop=(k == 7)),
                          r=[t_sq, t_const], w=[t_st])
                act.op(lambda: nc.scalar.activation(out=rstd[:], in_=st_ps[:], func=AF.Sqrt, bias=EPS, scale=1.0 / D),
                       r=[t_st], w=[t_rstd])
                dve.op(lambda: nc.vector.reciprocal(out=rstd[:], in_=rstd[:]), r=[t_rstd], w=[t_rstd])
                for k in range(8):
                    dve.op(lambda: nc.vector.scalar_tensor_tensor(out=xg[b][:, k, :], in0=xg[b][:, k, :], scalar=Acol[:, k:k + 1],
                                                                  in1=rstd[:], op0=ALU.mult, op1=ALU.mult),
                           r=[t_xg[b], t_rstd, t_ab], w=[t_xg[b]])
                    act.op(lambda: nc.scalar.activation(out=hT[b][:, k, :], in_=xg[b][:, k, :], func=AF.Identity,
                                                        bias=Bcol[:, k:k + 1], scale=1.0),
                           r=[t_xg[b], t_ab], w=[t_hT[b]])

            iq = [0]

            def FM(g):
                b = g % 2
                hTg, t_h = hT[b], t_hT[b]
                pend = None

                def finish_qk(ps_, t_, which, j, i3):
                    ps2, t2 = st_ps, t_st
                    pe.op(lambda: nc.tensor.matmul(ps2[:], lhsT=blk_b[:], rhs=sqq[i3][:], start=True, stop=True),
                          r=[t_sqq[i3], t_const], w=[t2])
                    act.op(lambda: nc.scalar.activation(out=rq[i3][:], in_=ps2[:], func=AF.Sqrt, bias=EPS, scale=1.0 / 64),
                           r=[t2], w=[t_rq[i3]])
                    dve.op(lambda: nc.vector.reciprocal(out=rq[i3][:], in_=rq[i3][:]), r=[t_rq[i3]], w=[t_rq[i3]])
                    dve.op(lambda: nc.vector.scalar_tensor_tensor(out=qn[i3][:], in0=ps_[:], scalar=gq[:, which:which + 1],
                                                                  in1=rq[i3][:], op0=ALU.mult, op1=ALU.mult),
                           r=[t_, t_rq[i3], t_p], w=[t_qn[i3]])
                    dst = (qT_scr if which == 0 else kT_scr)[j * 128:(j + 1) * 128, g * 512:(g + 1) * 512]
                    sp.dma(dst, qn[i3][:], r=[t_qn[i3]])

                for which in range(2):
                    for j in range(4):
                        ps_, t_ = proj_fm(which * 512 + j * 128, 128, hTg, t_h)
                        i3 = iq[0] % 3
                        iq[0] += 1
                        act.op(lambda: nc.scalar.activation(out=sqq[i3][:], in_=ps_[:], func=AF.Square), r=[t_], w=[t_sqq[i3]])
                        if pend is not None:
                            finish_qk(*pend)
                        pend = (ps_, t_, which, j, i3)
                for j in range(4):
                    ps_, t_ = proj_fm(1536 + j * 64, 64, hTg, t_h)
                    act.op(lambda: nc.scalar.copy(out=qmT[b][:, j, :], in_=ps_[0:64, :]), r=[t_], w=[t_qmT[b][j]])
                    if pend is not None:
                        finish_qk(*pend)
                        pend = None
                    ps_, t_ = proj_fm(1792 + j * 64, 64, hTg, t_h)
                    act.op(lambda: nc.scalar.mul(out=kmT[b][:, j, :], in_=ps_[0:64, :], mul=0.125), r=[t_], w=[t_kmT[b][j]])
                ps_, t_ = proj_fm(3072, 8, hTg, t_h)
                act.op(lambda: nc.scalar.activation(out=fe[:], in_=ps_[0:8, :], func=AF.Exp, bias=nbfox[:, 0:1], scale=-1.0),
                       r=[t_, t_p], w=[t_fe])
                act.op(lambda: nc.scalar.activation(out=fe[:], in_=fe[:], func=AF.Ln, bias=1.0, scale=1.0), r=[t_fe], w=[t_fe])
                dve.op(lambda: nc.vector.tensor_tensor_scan(out=fneg[:], data0=ones_f[0:8, :], data1=fe[:], initial=fcar[:, 0:1],
                                                            op0=ALU.mult, op1=ALU.add),
                       r=[t_fe, t_fcar, t_const], w=[t_fneg])
                dve.op(lambda: nc.vector.tensor_copy(out=fcar[:], in_=fneg[:, 511:512]), r=[t_fneg], w=[t_fcar])
                dve.op(lambda: nc.vector.tensor_scalar(out=fbf[:], in0=fneg[:], scalar1=-1.0, scalar2=None, op0=ALU.mult),
                       r=[t_fneg], w=[t_fbf])
                sp.dma(FT_scr[:, g * 512:(g + 1) * 512], fbf[:], r=[t_fbf])

            def stageA(t):
                g, bi = divmod(t, 4)
                tb = bi * 128
                hTg, t_h = hT[g % 2], t_hT[g % 2]
                o = bbs[t % 2]
                sm = o.sm
                ps_g, t_g = proj_tm(2816, 272, hTg, t_h, tb)
                dve.op(lambda: nc.vector.tensor_tensor(out=o.gt[:], in0=ps_g[:, 256:272], in1=gb[:], op=ALU.add),
                       r=[t_g, t_p], w=[o.t_gt])
                act.op(lambda: nc.scalar.activation(out=o.vg[:], in_=ps_g[:, 0:256], func=AF.Gelu_apprx_tanh), r=[t_g], w=[o.t_vg])
                act.op(lambda: nc.scalar.activation(out=o.nl[:], in_=o.gt[:], func=AF.Exp, scale=-1.0), r=[o.t_gt], w=[o.t_nl])
                act.op(lambda: nc.scalar.activation(out=o.nl[:], in_=o.nl[:], func=AF.Ln, bias=1.0, scale=1.0), r=[o.t_nl], w=[o.t_nl])
                ps_o, t_o = proj_tm(2304, 512, hTg, t_h, tb)
                act.op(lambda: nc.scalar.activation(out=o.go[:], in_=ps_o[:, 0:256], func=AF.Sigmoid), r=[t_o], w=[o.t_go])
                act.op(lambda: nc.scalar.activation(out=o.ug[:], in_=ps_o[:, 256:512], func=AF.Gelu_apprx_tanh), r=[t_o], w=[o.t_ug])
                pool.op(lambda: nc.gpsimd.tensor_tensor(out=o.go[:], in0=o.go[:], in1=gmo[:], op=ALU.mult), r=[o.t_go, t_p], w=[o.t_go])
                ps_v, t_v = proj_tm(1024, 512, hTg, t_h, tb)
                act.op(lambda: nc.scalar.copy(out=o.va[:, :, 0:64], in_=ps_v[:].rearrange("p (h d) -> p h d", h=8)),
                       r=[t_v], w=[o.t_va])
                sp.dma(v_scr[:, t * 128:(t + 1) * 128, :].rearrange("h t c -> t h c"), o.va[:], r=[o.t_va])
                pe.op(lambda: nc.tensor.matmul(cb_ps[:, 0:16], lhsT=tri_f[:], rhs=o.nl[:], start=True, stop=True),
                      r=[o.t_nl, t_const], w=[t_cb])
                pe.op(lambda: nc.tensor.matmul(cb_ps[:, 16:32], lhsT=ones_f[:, 0:128], rhs=o.nl[:], start=True, stop=True),
                      r=[o.t_nl, t_const], w=[t_cb])
                dve.op(lambda: nc.vector.tensor_tensor(out=negF[:, t, :], in0=cb_ps[:, 0:8], in1=carF[:], op=ALU.add),
                       r=[t_cb, t_carF], w=[t_negF])
                dve.op(lambda: nc.vector.tensor_tensor(out=carF[:], in0=cb_ps[:, 16:24], in1=carF[:], op=ALU.add),
                       r=[t_cb, t_carF], w=[t_carF])
                dve.op(lambda: nc.vector.tensor_tensor(out=sm[:, 8:12], in0=cb_ps[:, 12:16], in1=o.gt[:, 8:12], op=ALU.add),
                       r=[t_cb, o.t_gt], w=[o.t_smA])
                act.op(lambda: nc.scalar.activation(out=sm[:, 0:4], in_=sm[:, 8:12], func=AF.Exp), r=[o.t_smA], w=[o.t_smA])
                act.op(lambda: nc.scalar.activation(out=sm[:, 4:8], in_=cb_ps[:, 12:16], func=AF.Exp), r=[t_cb, o.t_smA], w=[o.t_smA])
                act.op(lambda: nc.scalar.activation(out=o.a_all[:], in_=cb_ps[:, 28:32], func=AF.Exp, scale=-1.0),
                       r=[t_cb], w=[o.t_a])
                ps_k, t_k = proj_tm(1792, 512, hTg, t_h, tb)
                act.op(lambda: nc.scalar.mul(out=o.kmt[:], in_=ps_k[:, 0:256], mul=0.125), r=[t_k], w=[o.t_kmt])
                for h in range(4):
                    dve.op(lambda: nc.vector.tensor_scalar(out=o.vaug[:, h, 0:64], in0=ps_k[:, 256 + h * 64:320 + h * 64],
                                                           scalar1=sm[:, h:h + 1], scalar2=None, op0=ALU.mult),
                           r=[t_k, o.t_smA], w=[o.t_vaug])
                dve.op(lambda: nc.vector.tensor_copy(out=o.vaug[:, :, 64], in_=sm[:, 0:4]), r=[o.t_smA], w=[o.t_vaug])

            def stageB(t):
                g, bi = divmod(t, 4)
                tb = bi * 128
                gb2 = g % 2
                o = bbs[t % 2]
                sm = o.sm
                t_sm = o.t_smB
                for h in range(4):
                    pe.op(lambda: nc.tensor.matmul(sc_ps[:, h, :], lhsT=kmT[gb2][:, h, tb:tb + 128],
                                                   rhs=qmT[gb2][:, h, tb:tb + 128], start=True, stop=True),
                          r=[t_kmT[gb2][h], t_qmT[gb2][h]], w=[t_sc])
                dve.op(lambda: nc.vector.tensor_tensor(out=AT[:], in0=sc_ps[:], in1=tri4[:], op=ALU.mult),
                       r=[t_sc, t_const], w=[t_AT])
                dve.op(lambda: nc.vector.tensor_reduce(out=sm[:, 32:36], in_=o.vg[:].rearrange("p (g c) -> p g c", g=4), axis=AX.X, op=ALU.add),
                       r=[o.t_vg, t_sm], w=[t_sm])
                pool.op(lambda: nc.gpsimd.tensor_tensor(out=vsq[:], in0=o.vg[:], in1=o.vg[:], op=ALU.mult), r=[o.t_vg], w=[t_vsq])
                dve.op(lambda: nc.vector.tensor_reduce(out=sm[:, 36:40], in_=vsq[:].rearrange("p (g c) -> p g c", g=4), axis=AX.X, op=ALU.add),
                       r=[t_vsq, t_sm], w=[t_sm])
                dve.op(lambda: nc.vector.tensor_scalar(out=sm[:, 40:44], in0=sm[:, 32:36], scalar1=1.0 / 64, scalar2=None, op0=ALU.mult),
                       r=[t_sm], w=[t_sm])
                dve.op(lambda: nc.vector.tensor_tensor(out=sm[:, 44:48], in0=sm[:, 40:44], in1=sm[:, 40:44], op=ALU.mult),
                       r=[t_sm], w=[t_sm])
                dve.op(lambda: nc.vector.scalar_tensor_tensor(out=sm[:, 44:48], in0=sm[:, 36:40], scalar=1.0 / 64, in1=sm[:, 44:48],
                                                              op0=ALU.mult, op1=ALU.subtract),
                       r=[t_sm], w=[t_sm])
                act.op(lambda: nc.scalar.activation(out=sm[:, 44:48], in_=sm[:, 44:48], func=AF.Sqrt, bias=EPS, scale=1.0),
                       r=[t_sm], w=[t_sm])
                dve.op(lambda: nc.vector.reciprocal(out=sm[:, 44:48], in_=sm[:, 44:48]), r=[t_sm], w=[t_sm])
                for gi in range(4):
                    pool.op(lambda: nc.gpsimd.tensor_scalar(out=vtmp[:, gi * 64:(gi + 1) * 64], in0=o.vg[:, gi * 64:(gi + 1) * 64],
                                                            scalar1=sm[:, 40 + gi:41 + gi], scalar2=sm[:, 44 + gi:45 + gi],
                                                            op0=ALU.subtract, op1=ALU.mult),
                            r=[o.t_vg, t_sm, t_vtmp], w=[t_vtmp])
                pool.op(lambda: nc.gpsimd.tensor_tensor(out=vn[:], in0=vtmp[:], in1=gsg[:], op=ALU.mult), r=[t_vtmp, t_p], w=[t_vn])
                for h in range(4):
                    pe.op(lambda: nc.tensor.matmul(nm_ps[:, h, :], lhsT=AT[:, h, :], rhs=o.vaug[:, h, 0:65], start=True, stop=False),
                          r=[t_AT, o.t_vaug], w=[t_nm])
                    pe.op(lambda: nc.tensor.matmul(nm_ps[:, h, :], lhsT=qmT[gb2][:, h, tb:tb + 128], rhs=S_b[:, h, 0:65],
                                                   start=False, stop=True),
                          r=[t_qmT[gb2][h], t_Sb], w=[t_nm])
                for h in range(4):
                    pe.op(lambda: nc.tensor.matmul(dS_ps[:, h, :], lhsT=o.kmt[:, h * 64:(h + 1) * 64], rhs=o.vaug[:, h, 0:65],
                                                   start=True, stop=True),
                          r=[o.t_kmt, o.t_vaug], w=[t_dS])
                mx_ps, t_mx = big.next()
                for gi in range(4):
                    pe.op(lambda: nc.tensor.matmul(mx_ps[:, gi * 64:(gi + 1) * 64], lhsT=wsb[:, gi, :], rhs=vn[:, gi * 64:(gi + 1) * 64],
                                                   start=True, stop=True),
                          r=[t_vn, t_p], w=[t_mx])
                dve.op(lambda: nc.vector.tensor_tensor(out=S_f[:, :, 0:65], in0=dS_ps[:], in1=S_f[:, :, 0:65], op=ALU.add),
                       r=[t_dS, t_Sf], w=[t_Sf])
                for h in range(4):
                    dve.op(lambda: nc.vector.tensor_scalar(out=S_f[:, h, 0:65], in0=S_f[:, h, 0:65],
                                                           scalar1=o.a_all[0:64, h:h + 1], scalar2=None, op0=ALU.mult),
                           r=[t_Sf, o.t_a], w=[t_Sf])
                dve.op(lambda: nc.vector.tensor_copy(out=S_b[:], in_=S_f[:]), r=[t_Sf], w=[t_Sb])
                act.op(lambda: nc.scalar.copy(out=num_sb[:], in_=nm_ps[:]), r=[t_nm], w=[t_num])
                dve.op(lambda: nc.vector.tensor_scalar(out=sm[:, 12:16], in0=num_sb[:, :, 64], scalar1=-1.0, scalar2=None, op0=ALU.mult),
                       r=[t_num, t_sm], w=[t_sm])
                dve.op(lambda: nc.vector.tensor_tensor(out=sm[:, 16:20], in0=num_sb[:, :, 64], in1=sm[:, 12:16], op=ALU.max),
                       r=[t_num, t_sm], w=[t_sm])
                dve.op(lambda: nc.vector.tensor_tensor(out=sm[:, 16:20], in0=sm[:, 16:20], in1=sm[:, 4:8], op=ALU.max),
                       r=[t_sm, o.t_smA], w=[t_sm])
                dve.op(lambda: nc.vector.reciprocal(out=sm[:, 16:20], in_=sm[:, 16:20]), r=[t_sm], w=[t_sm])
                for h in range(4):
                    dve.op(lambda: nc.vector.tensor_scalar(out=num_sb[:, h, 0:64], in0=num_sb[:, h, 0:64],
                                                           scalar1=sm[:, 16 + h:17 + h], scalar2=None, op0=ALU.mult),
                           r=[t_num, t_sm], w=[t_num])
                pool.op(lambda: nc.gpsimd.tensor_tensor(out=sqn[:], in0=num_sb[:, :, 0:64], in1=num_sb[:, :, 0:64], op=ALU.mult),
                        r=[t_num], w=[t_sqn])
                dve.op(lambda: nc.vector.tensor_reduce(out=sm[:, 20:24], in_=sqn[:], axis=AX.X, op=ALU.add),
                       r=[t_sqn, t_sm], w=[t_sm])
                act.op(lambda: nc.scalar.activation(out=sm[:, 24:28], in_=sm[:, 20:24], func=AF.Sqrt, bias=EPS, scale=1.0 / 64),
                       r=[t_sm], w=[t_sm])
                dve.op(lambda: nc.vector.reciprocal(out=sm[:, 28:32], in_=sm[:, 24:28]), r=[t_sm], w=[t_sm])
                for h in range(4):
                    dve.op(lambda: nc.vector.scalar_tensor_tensor(out=ymix[:, h * 64:(h + 1) * 64], in0=num_sb[:, h, 0:64],
                                                                  scalar=sm[:, 28 + h:29 + h], in1=o.go[:, h * 64:(h + 1) * 64],
                                                                  op0=ALU.mult, op1=ALU.mult),
                           r=[t_num, t_sm, o.t_go], w=[t_ymix])
                for gi in range(4):
                    dve.op(lambda: nc.vector.scalar_tensor_tensor(out=ymix[:, 256 + gi * 64:320 + gi * 64], in0=mx_ps[:, gi * 64:(gi + 1) * 64],
                                                                  scalar=bs[:, gi:gi + 1], in1=o.ug[:, gi * 64:(gi + 1) * 64],
                                                                  op0=ALU.add, op1=ALU.mult),
                           r=[t_mx, o.t_ug, t_p], w=[t_ymix])
                for c in range(4):
                    pe.op(lambda: nc.tensor.transpose(tr_ps[:, c, :], ymix[:, c * 128:(c + 1) * 128], ident_b[:]),
                          r=[t_ymix, t_const], w=[t_tr])
                act.op(lambda: nc.scalar.copy(out=ymT[gb2][:, :, tb:tb + 128], in_=tr_ps[:]), r=[t_tr], w=[t_ymT[gb2]])
                if bi == 3:
                    sp.dma(yT_scr[512:1024, g * 512:(g + 1) * 512].rearrange("(c p) t -> p c t", p=128), ymT[gb2][:], r=[t_ymT[gb2]])

            load_x(0)
            norm_pre(0)
            norm_post(0)
            for t in range(NB + 1):
                if t < NB:
                    g, bi = divmod(t, 4)
                    if bi == 0:
                        if g + 1 < G:
                            load_x(g + 1)
                            norm_pre(g + 1)
                        FM(g)
                    if bi == 1 and g + 1 < G:
                        norm_post(g + 1)
                    stageA(t)
                if t >= 1:
                    stageB(t - 1)
        fw.barrier()

    def phase2(l):
        with ExitStack() as es:
            Qa = [sbt(es, f"Qa{i}", [65, S], BF16) for i in range(2)]
            Ka = [sbt(es, f"Ka{i}", [65, S], BF16) for i in range(2)]
            Va = [sbt(es, f"Va{i}", [128, NB, 128], BF16) for i in range(2)]
            t_Q = [Tok(), Tok()]
            t_K = [Tok(), Tok()]
            t_V = [Tok(), Tok()]
            for i in range(2):
                pool.op(lambda: nc.gpsimd.memset(Ka[i][64:65, :], 1.0), w=[t_K[i]])
            P = RR([(sbt(es, f"P{i}", [128, 512], BF16), Tok()) for i in range(6)])
            rden = [sbt(es, f"rden{i}", [128, 512], F32) for i in range(2)]
            t_rden = [Tok(), Tok()]
            yo = [sbt(es, f"yo{i}", [64, 512], BF16) for i in range(2)]
            t_yo = [Tok(), Tok()]
            scp = RR([(pst(es, f"scp{i}", [128, 512]), PTok()) for i in range(6)])
            op_ = RR([(pst(es, f"op{i}", [128, 512]), PTok()) for i in range(2)])
            def load_head(hh):
                bb = hh % 2
                sp.dma(Ka[bb][0:64, :], kT_scr[hh * 64:(hh + 1) * 64, :], w=[t_K[bb]])
                sp.dma(Qa[bb][0:64, :], qT_scr[hh * 64:(hh + 1) * 64, :], w=[t_Q[bb]])
                sp.dma(Qa[bb][64:65, :], FT_scr[hh:hh + 1, :], w=[t_Q[bb]])
                sp.dma(Va[bb][:], v_scr[hh].rearrange("(j p) c -> p j c", p=128), w=[t_V[bb]])

            units = [(h, g, j) for h in range(8) for g in range(G) for j in range(4 * g + 4)]
            LA = 3
            inflight = {}
            cur_o = {}
            io = 0
            load_head(0)
            for i in range(len(units) + LA):
                if i < len(units):
                    h, g, j = units[i]
                    hb = h % 2
                    c0 = max(0, j - 4 * g) * 128
                    s_ps, t_s = scp.next()
                    diag = j >= 4 * g
                    pe.op(lambda: nc.tensor.matmul(s_ps[:, c0:512], lhsT=Ka[hb][:, j * 128:(j + 1) * 128],
                                                   rhs=Qa[hb][:, g * 512 + c0:(g + 1) * 512], start=True, stop=not diag),
                          r=[t_K[hb], t_Q[hb]], w=[t_s])
                    if diag:
                        pe.op(lambda: nc.tensor.matmul(s_ps[:, c0:c0 + 128], lhsT=ident_b[:], rhs=mneg_b[:], start=False, stop=True),
                              r=[t_const], w=[t_s])
                    inflight[i] = (s_ps, t_s, c0)
                k = i - LA
                if k < 0:
                    continue
                h, g, j = units[k]
                hb = h % 2
                nj = 4 * g + 4
                if g == 0 and j == 0 and h + 1 < 8:
                    load_head(h + 1)
                s_ps, t_s, c0 = inflight.pop(k)
                if j == 0:
                    cur_o[(h, g)] = op_.next()
                o_ps, t_o = cur_o[(h, g)]
                p_sb, t_p_ = P.next()
                act.op(lambda: nc.scalar.activation(out=p_sb[:, c0:512], in_=s_ps[:, c0:512], func=AF.Exp,
                                                    bias=negF[:, j, h:h + 1], scale=1.0),
                       r=[t_s, t_negF], w=[t_p_])
                pe.op(lambda: nc.tensor.matmul(o_ps[:, c0:512], lhsT=Va[hb][:, j, :], rhs=p_sb[:, c0:512],
                                               start=(j == 0), stop=(j == nj - 1)),
                      r=[t_V[hb], t_p_], w=[t_o])
                if j == nj - 1:
                    del cur_o[(h, g)]
                    ib = io % 2
                    io += 1
                    dve.op(lambda: nc.vector.reciprocal(out=rden[ib][64:128, :], in_=o_ps[64:128, :]), r=[t_o], w=[t_rden[ib]])
                    dve.op(lambda: nc.vector.tensor_tensor(out=yo[ib][:], in0=o_ps[0:64, :], in1=rden[ib][64:128, :], op=ALU.mult),
                           r=[t_o, t_rden[ib]], w=[t_yo[ib]])
                    sp.dma(yT_scr[h * 64:(h + 1) * 64, g * 512:(g + 1) * 512], yo[ib][:], r=[t_yo[ib]])
        fw.barrier()

    def phase3a(l, x_src):
        with ExitStack() as es:
            wo = sbt(es, "wo", [128, 8, D], BF16)
            t_wo = [Tok() for _ in range(8)]
            for k in range(8):
                pool.dma(wo[:, k, :], w_out[l, k * 128:(k + 1) * 128, :], w=[t_wo[k]])
            xg = [sbt(es, f"xg{i}", [128, 8, 512], F32) for i in range(2)]
            t_xg = [Tok(), Tok()]
            x1 = [sbt(es, f"x1{i}", [128, 8, 512], F32) for i in range(2)]
            t_x1 = [Tok(), Tok()]
            yg = [sbt(es, f"yg{i}", [128, 8, 512], BF16) for i in range(2)]
            t_yg = [Tok(), Tok()]
            sq = sbt(es, "sq", [128, 8, 512], BF16)
            t_sq = Tok()
            rstd = sbt(es, "rstd", [128, 512], F32)
            t_rstd = Tok()
            hT = [sbt(es, f"hT{i}", [128, 8, 512], BF16) for i in range(2)]
            t_hT = [Tok(), Tok()]
            st_ps = pst(es, "st_ps", [128, 512])
            t_st = PTok()
            big = RR([(pst(es, f"big{i}", [128, 512]), PTok()) for i in range(4)])
            xsrc = km(x_src)
            for g in range(G):
                b = g % 2
                sl = slice(g * 512, (g + 1) * 512)
                if g == 0:
                    sp.dma(xg[0][:], xsrc[:, :, 0:512], w=[t_xg[0]])
                    sp.dma(yg[0][:], km(yT_scr)[:, :, 0:512], w=[t_yg[0]])
                if g + 1 < G:
                    sl2 = slice((g + 1) * 512, (g + 2) * 512)
                    sp.dma(xg[1 - b][:], xsrc[:, :, sl2], w=[t_xg[1 - b]])
                    sp.dma(yg[1 - b][:], km(yT_scr)[:, :, sl2], w=[t_yg[1 - b]])
                for c in range(8):
                    ps_, t_ = big.next()
                    for k in range(8):
                        pe.op(lambda: nc.tensor.matmul(ps_[:], lhsT=wo[:, k, c * 128:(c + 1) * 128], rhs=yg[b][:, k, :],
                                                       start=(k == 0), stop=(k == 7)),
                              r=[t_wo[k], t_yg[b]], w=[t_])
                    dve.op(lambda: nc.vector.scalar_tensor_tensor(out=x1[b][:, c, :], in0=ps_[:], scalar=modT[l][:, 16 + c:17 + c],
                                                                  in1=xg[b][:, c, :], op0=ALU.mult, op1=ALU.add),
                           r=[t_, t_xg[b], t_mod[l]], w=[t_x1[b]])
                sp.dma(km(x1T_scr)[:, :, sl], x1[b][:], r=[t_x1[b]])
                norm_group(x1[b], t_x1[b], sq, t_sq, st_ps, t_st, rstd, t_rstd, hT[b], t_hT[b],
                           A2[l], modT[l][:, 24:32], t_mod[l])
                sp.dma(km(h2T_scr)[:, :, sl], hT[b][:], r=[t_hT[b]])
        fw.barrier()

    def phase3b(l, x_dst):
        with ExitStack() as es:
            wu = sbt(es, "wu", [128, 8, 2 * DFF], BF16)
            t_wu = [Tok() for _ in range(8)]
            wd = sbt(es, "wd", [128, NF, D], BF16)
            t_wd = [Tok() for _ in range(NF)]
            for k in range(8):
                pool.dma(wu[:, k, :], w_up[l, k * 128:(k + 1) * 128, :], w=[t_wu[k]])
            for f in range(NF):
                pool.dma(wd[:, f, :], w_down[l, f * 128:(f + 1) * 128, :], w=[t_wd[f]])
            wc = sbt(es, "wc", [128, 2 * NF, 3], F32)
            bc = sbt(es, "bc", [128, 2 * NF], F32)
            t_c = Tok()
            sp.dma(wc[:], w_convT[l], w=[t_c])
            sp.dma(bc[:], b_convT[l], w=[t_c])
            tail = sbt(es, "tail", [128, 2 * NF, 2], F32)
            t_tail = [Tok() for _ in range(2 * NF)]
            pool.op(lambda: nc.gpsimd.memset(tail[:], 0.0), w=t_tail)
            hT = [sbt(es, f"hT{i}", [128, 8, 512], BF16) for i in range(2)]
            t_hT = [Tok(), Tok()]
            zs = RR([(sbt(es, f"zs{i}", [128, 514], F32), Tok()) for i in range(4)])
            cv = RR([(sbt(es, f"cv{i}", [128, 512], F32), Tok()) for i in range(4)])
            sa = RR([(sbt(es, f"sa{i}", [128, 512], F32), Tok()) for i in range(2)])
            actT1 = sbt(es, "actT", [128, NF, 512], BF16)
            actT = [actT1, actT1]
            t_actT1 = [Tok() for _ in range(NF)]
            t_actT = [t_actT1, t_actT1]
            x1t = RR([(sbt(es, f"x1t{i}", [128, 512], F32), Tok()) for i in range(3)])
            zp = RR([(pst(es, f"zp{i}", [128, 512]), PTok()) for i in range(5)])
            dp = RR([(pst(es, f"dp{i}", [128, 512]), PTok()) for i in range(3)])
            dst = km(x_dst)
            for g in range(G):
                b = g % 2
                sl = slice(g * 512, (g + 1) * 512)
                if g == 0:
                    sp.dma(hT[0][:], km(h2T_scr)[:, :, 0:512], w=[t_hT[0]])
                if g + 1 < G:
                    sp.dma(hT[1 - b][:], km(h2T_scr)[:, :, (g + 1) * 512:(g + 2) * 512], w=[t_hT[1 - b]])
                for f in range(NF):
                    cvs = []
                    for br in range(2):
                        ti = br * NF + f
                        z_ps, t_z = zp.next()
                        for k in range(8):
                            pe.op(lambda: nc.tensor.matmul(z_ps[:], lhsT=wu[:, k, ti * 128:(ti + 1) * 128], rhs=hT[b][:, k, :],
                                                           start=(k == 0), stop=(k == 7)),
                                  r=[t_wu[k], t_hT[b]], w=[t_z])
                        z_sb, t_zs = zs.next()
                        act.op(lambda: nc.scalar.copy(out=z_sb[:, 2:514], in_=z_ps[:]), r=[t_z], w=[t_zs])
                        pool.op(lambda: nc.gpsimd.tensor_copy(out=z_sb[:, 0:2], in_=tail[:, ti, :]), r=[t_tail[ti]], w=[t_zs])
                        pool.op(lambda: nc.gpsimd.tensor_copy(out=tail[:, ti, :], in_=z_sb[:, 512:514]), r=[t_zs], w=[t_tail[ti]])
                        c_sb, t_cv = cv.next()
                        dve.op(lambda: nc.vector.tensor_scalar(out=c_sb[:], in0=z_sb[:, 2:514], scalar1=wc[:, ti, 2:3],
                                                               scalar2=bc[:, ti:ti + 1], op0=ALU.mult, op1=ALU.add),
                               r=[t_zs, t_c], w=[t_cv])
                        dve.op(lambda: nc.vector.scalar_tensor_tensor(out=c_sb[:], in0=z_sb[:, 1:513], scalar=wc[:, ti, 1:2],
                                                                      in1=c_sb[:], op0=ALU.mult, op1=ALU.add),
                               r=[t_zs, t_c, t_cv], w=[t_cv])
                        dve.op(lambda: nc.vector.scalar_tensor_tensor(out=c_sb[:], in0=z_sb[:, 0:512], scalar=wc[:, ti, 0:1],
                                                                      in1=c_sb[:], op0=ALU.mult, op1=ALU.add),
                               r=[t_zs, t_c, t_cv], w=[t_cv])
                        cvs.append((c_sb, t_cv))
                    s_sb, t_sa = sa.next()
                    act.op(lambda: nc.scalar.activation(out=s_sb[:], in_=cvs[0][0][:], func=AF.Silu), r=[cvs[0][1]], w=[t_sa])
                    pool.op(lambda: nc.gpsimd.tensor_tensor(out=actT[b][:, f, :], in0=s_sb[:], in1=cvs[1][0][:], op=ALU.mult),
                            r=[t_sa, cvs[1][1]], w=[t_actT[b][f]])
                for c in range(8):
                    xt_, t_xt = x1t.next()
                    sp.dma(xt_[:], x1T_scr[c * 128:(c + 1) * 128, sl], w=[t_xt])
                    d_ps, t_d = dp.next()
                    for f in range(NF):
                        pe.op(lambda: nc.tensor.matmul(d_ps[:], lhsT=wd[:, f, c * 128:(c + 1) * 128], rhs=actT[b][:, f, :],
                                                       start=(f == 0), stop=(f == NF - 1)),
                              r=[t_wd[f], t_actT[b][f]], w=[t_d])
                    dve.op(lambda: nc.vector.scalar_tensor_tensor(out=xt_[:], in0=d_ps[:], scalar=modT[l][:, 40 + c:41 + c],
                                                                  in1=xt_[:], op0=ALU.mult, op1=ALU.add),
                           r=[t_d, t_xt, t_mod[l]], w=[t_xt])
                    sp.dma(x_dst[c * 128:(c + 1) * 128, sl], xt_[:], r=[t_xt])
        fw.barrier()

    phase0()
    cur = xT
    for l in range(depth):
        nxt = outT if l == depth - 1 else xmid_scr
        if upto >= 1:
            phase1(l, cur)
        if upto >= 2:
            phase2(l)
        if upto >= 3:
            phase3a(l, cur)
        if upto >= 4:
            phase3b(l, nxt)
        cur = nxt
    if upto < 4:
        with ExitStack() as es:
            dbg = sbt(es, "dbg", [128, 48], F32)
            t_dbg = Tok()
            dve.op(lambda: nc.vector.tensor_copy(out=dbg[:], in_=modT[0][:]), r=[t_mod[0]], w=[t_dbg])
            sp.dma(outT[0:128, 0:48], dbg[:], r=[t_dbg])
            dve.op(lambda: nc.vector.tensor_copy(out=dbg[:, 0:8], in_=A1[0][:]), r=[t_mod[0], t_dbg], w=[t_dbg])
            dve.op(lambda: nc.vector.tensor_copy(out=dbg[:, 8:16], in_=A2[0][:]), r=[t_mod[0], t_dbg], w=[t_dbg])
            sp.dma(outT[128:256, 0:48], dbg[:], r=[t_dbg])
            fw.finish()
    fw.finish()
    top.close()
    build.stats = {s.name: (s.n_ins, s.n_wait) for s in fw.streams}
    return nc


def prep_shared(inputs, depth=DEPTH):
    f = lambda a: np.ascontiguousarray(np.asarray(a, dtype=np.float32))
    sh = {}
    sh["w_ada"] = f(inputs["w_ada"])
    sh["b_adaT"] = f(np.asarray(inputs["b_ada"]).reshape(depth, 48, 128).transpose(0, 2, 1))
    sh["g_mixT"] = f(np.asarray(inputs["g_mix"]).reshape(depth, 8, 128).transpose(0, 2, 1))
    sh["g_ffnT"] = f(np.asarray(inputs["g_ffn"]).reshape(depth, 8, 128).transpose(0, 2, 1))
    sh["w_in"] = f(inputs["w_in"])
    sh["gbias"] = f(np.concatenate([np.asarray(inputs["b_fox_f"]), np.asarray(inputs["b_m_i"]), np.asarray(inputs["b_m_f"])], axis=1))
    sh["bfoxT"] = f(np.asarray(inputs["b_fox_f"])[:, :, None])
    gq = np.tile(np.asarray(inputs["g_q"]), (1, 2))
    gk = np.tile(np.asarray(inputs["g_k"]), (1, 2))
    sh["gqk"] = f(np.stack([gq, gk], axis=-1))
    sh["g_m_out"] = f(inputs["g_m_out"])
    sh["g_sgu"] = f(inputs["g_sgu"])
    sh["w_sT"] = f(np.asarray(inputs["w_s"]).transpose(0, 1, 3, 2))
    sh["b_sT"] = f(np.asarray(inputs["b_s"]).transpose(0, 2, 1))
    sh["w_out"] = f(inputs["w_out"])
    sh["w_up"] = f(inputs["w_up"])
    sh["w_down"] = f(inputs["w_down"])
    sh["w_convT"] = f(np.asarray(inputs["w_conv"]).reshape(depth, 3, 2 * NF, 128).transpose(0, 3, 2, 1))
    sh["b_convT"] = f(np.asarray(inputs["b_conv"]).reshape(depth, 2 * NF, 128).transpose(0, 2, 1))
    sh["tri"] = np.triu(np.ones((128, 128), np.float32))
    sh["ident"] = np.eye(128, dtype=np.float32)
    return sh


def prep_core(x_b, c_b):
    return {"xT": np.ascontiguousarray(np.asarray(x_b, np.float32).T),
            "ccol": np.ascontiguousarray(np.asarray(c_b, np.float32).reshape(8, 128).T)}


def kernel(**inputs):
    x = np.asarray(inputs["x"])
    c = np.asarray(inputs["c"])
    B, S, _ = x.shape
    nc = build(S=S, depth=DEPTH)
    sh = prep_shared(inputs)
    in_maps = []
    for b in range(B):
        m = dict(sh)
        m.update(prep_core(x[b], c[b]))
        in_maps.append(m)
    res = run_bass_kernel_spmd(nc, in_maps, core_ids=list(range(B)))
    out = np.stack([np.asarray(r["outT"]).T for r in res.results], axis=0)
    return np.ascontiguousarray(out.astype(np.float32))
```

```python
import numpy as np
from contextlib import ExitStack
import concourse.bass as bass
import concourse.mybir as mybir
from concourse.bass_utils import run_bass_kernel_spmd

F32 = mybir.dt.float32
BF16 = mybir.dt.bfloat16
AF = mybir.ActivationFunctionType
ALU = mybir.AluOpType
AX = mybir.AxisListType

D = 1024
NIN = 3088
DFF = 2816
NF = DFF // 128
EPS = 1e-6
SEQ = 8192
DEPTH = 2
NCORES = 8


class Tok:
    __slots__ = ("w", "rs", "excl")

    def __init__(self, excl=False):
        self.w = None
        self.rs = []
        self.excl = excl


def PTok():
    return Tok(excl=True)


class Stream:
    def __init__(self, fw, eng, name, n_dma_sems=0, self_sync=True):
        self.fw = fw
        self.eng = eng
        self.name = name
        self.sem = fw.nc.alloc_semaphore("s_" + name)
        self.cnt = 0
        self.known = {}
        self.self_sync = self_sync
        self.dma_sems = [[fw.nc.alloc_semaphore(f"d_{name}{i}"), 0] for i in range(n_dma_sems)]
        self.dma_rr = 0
        self.n_wait = 0
        self.n_ins = 0

    def wait(self, ev):
        if ev is None:
            return
        sem, val = ev
        if (not self.self_sync) and sem is self.sem:
            return
        if self.known.get(sem, 0) >= val:
            return
        self.eng.wait_ge(sem, val)
        self.known[sem] = val
        self.n_wait += 1

    def _deps(self, r, w):
        for t in r:
            self.wait(t.w)
        for t in w:
            self.wait(t.w)
            for ev in t.rs:
                self.wait(ev)

    def _commit(self, ev, r, w):
        for t in r:
            t.rs.append(ev)
            if len(t.rs) > 32:
                d = {}
                for s, v in t.rs:
                    if d.get(s, 0) < v:
                        d[s] = v
                t.rs = list(d.items())
        for t in w:
            t.w = ev
            t.rs = []

    def op(self, fn, r=(), w=()):
        if any(t.excl for t in r):
            w = list(w) + [t for t in r if t.excl]
            r = [t for t in r if not t.excl]
        self._deps(r, w)
        ins = fn()
        self.cnt += 1
        ins.then_inc(self.sem, 1)
        ev = (self.sem, self.cnt)
        self._commit(ev, r, w)
        self.n_ins += 1
        return ev

    def dma(self, out, in_, r=(), w=(), **kw):
        self._deps(r, w)
        slot = self.dma_sems[self.dma_rr]
        self.dma_rr = (self.dma_rr + 1) % len(self.dma_sems)
        sem, c = slot
        if c > 0:
            self.wait((sem, c))
        ins = self.eng.dma_start(out=out, in_=in_, **kw)
        slot[1] = c + 16
        ins.then_inc(sem, 16)
        ev = (sem, c + 16)
        self._commit(ev, r, w)
        self.n_ins += 1
        return ev


class FW:
    def __init__(self, nc):
        self.nc = nc
        self.pe = Stream(self, nc.tensor, "pe", self_sync=False)
        self.act = Stream(self, nc.scalar, "act", n_dma_sems=4)
        self.dve = Stream(self, nc.vector, "dve")
        self.pool = Stream(self, nc.gpsimd, "pool", n_dma_sems=12)
        self.sp = Stream(self, nc.sync, "sp", n_dma_sems=24)
        self.streams = [self.pe, self.act, self.dve, self.pool, self.sp]

    def all_events(self):
        evs = []
        for s in self.streams:
            if s.cnt > 0:
                evs.append((s.sem, s.cnt))
            for sem, c in s.dma_sems:
                if c > 0:
                    evs.append((sem, c))
        return evs

    def barrier(self):
        evs = self.all_events()
        for s in self.streams:
            for ev in evs:
                if ev[0] is s.sem:
                    continue
                s.wait(ev)

    def finish(self):
        for ev in self.all_events():
            self.sp.wait(ev)


class RR:
    def __init__(self, items):
        self.items = items
        self.i = 0

    def next(self):
        it = self.items[self.i]
        self.i = (self.i + 1) % len(self.items)
        return it


def build(S=SEQ, depth=DEPTH, debug=False, upto=4):
    assert S % 512 == 0
    NB = S // 128
    G = S // 512
    nc = bass.Bass("TRN2", target_bir_lowering=False)
    fw = FW(nc)
    pe, act, dve, pool, sp = fw.pe, fw.act, fw.dve, fw.pool, fw.sp

    def din(name, shape, dt=F32):
        return nc.dram_tensor(name, list(shape), dt, kind="ExternalInput").ap()

    skind = "ExternalOutput" if debug else "Internal"

    def dscr(name, shape, dt):
        return nc.dram_tensor(name, list(shape), dt, kind=skind).ap()

    xT = din("xT", [D, S])
    ccol = din("ccol", [128, 8])
    w_ada = din("w_ada", [depth, D, 6 * D])
    b_adaT = din("b_adaT", [depth, 128, 48])
    g_mixT = din("g_mixT", [depth, 128, 8])
    g_ffnT = din("g_ffnT", [depth, 128, 8])
    w_in = din("w_in", [depth, D, NIN])
    gbias = din("gbias", [depth, 16])
    bfoxT = din("bfoxT", [depth, 8, 1])
    gqk = din("gqk", [depth, 128, 2])
    g_m_out = din("g_m_out", [depth, 256])
    g_sgu = din("g_sgu", [depth, 256])
    w_sT = din("w_sT", [depth, 4, 128, 128])
    b_sT = din("b_sT", [depth, 128, 4])
    w_out = din("w_out", [depth, D, D])
    w_up = din("w_up", [depth, D, 2 * DFF])
    w_down = din("w_down", [depth, DFF, D])
    w_convT = din("w_convT", [depth, 128, 2 * NF, 3])
    b_convT = din("b_convT", [depth, 128, 2 * NF])
    tri_d = din("tri", [128, 128])
    ident_d = din("ident", [128, 128])
    outT = nc.dram_tensor("outT", [D, S], F32, kind="ExternalOutput").ap()

    qT_scr = dscr("qT_scr", [512, S], BF16)
    kT_scr = dscr("kT_scr", [512, S], BF16)
    FT_scr = dscr("FT_scr", [8, S], BF16)
    v_scr = dscr("v_scr", [8, S, 128], BF16)
    yT_scr = dscr("yT_scr", [D, S], BF16)
    x1T_scr = dscr("x1T_scr", [D, S], F32)
    h2T_scr = dscr("h2T_scr", [D, S], BF16)
    xmid_scr = dscr("xmid_scr", [D, S], F32)

    def km(ap):
        return ap.rearrange("(k p) t -> p k t", p=128)

    top = ExitStack()

    uid = [0]

    def sbt(es, name, shape, dt):
        uid[0] += 1
        return es.enter_context(nc.sbuf_tensor(f"{name}_{uid[0]}", list(shape), dt))

    def pst(es, name, shape, dt=F32):
        uid[0] += 1
        return es.enter_context(nc.psum_tensor(f"{name}_{uid[0]}", list(shape), dt))

    tri_f = sbt(top, "tri_f", [128, 128], F32)
    tri4 = sbt(top, "tri4", [128, 4, 128], F32)
    ident_f = sbt(top, "ident_f", [128, 128], F32)
    ident_b = sbt(top, "ident_b", [128, 128], BF16)
    ones_f = sbt(top, "ones_f", [128, 512], F32)
    ones_b = sbt(top, "ones_b", [128, 128], BF16)
    blk_b = sbt(top, "blk_b", [128, 128], BF16)
    mneg_b = sbt(top, "mneg_b", [128, 128], BF16)
    negF = sbt(top, "negF", [128, NB, 8], F32)
    modT = [sbt(top, f"modT{l}", [128, 48], F32) for l in range(depth)]
    A1 = [sbt(top, f"A1_{l}", [128, 8], F32) for l in range(depth)]
    A2 = [sbt(top, f"A2_{l}", [128, 8], F32) for l in range(depth)]
    t_const = Tok()
    t_negF = Tok()
    t_mod = [Tok() for _ in range(depth)]

    sp.dma(tri_f[:], tri_d[:, :], w=[t_const])
    sp.dma(ident_f[:], ident_d[:, :], w=[t_const])
    for h in range(4):
        dve.op(lambda: nc.vector.tensor_copy(out=tri4[:, h, :], in_=tri_f[:]), r=[t_const], w=[t_const])
    dve.op(lambda: nc.vector.tensor_copy(out=ident_b[:], in_=ident_f[:]), r=[t_const], w=[t_const])
    dve.op(lambda: nc.vector.memset(ones_f[:], 1.0), w=[t_const])
    dve.op(lambda: nc.vector.memset(ones_b[:], 1.0), w=[t_const])
    dve.op(lambda: nc.vector.memset(blk_b[:], 0.0), w=[t_const])
    dve.op(lambda: nc.vector.memset(blk_b[0:64, 0:64], 1.0), w=[t_const])
    dve.op(lambda: nc.vector.memset(blk_b[64:128, 64:128], 1.0), w=[t_const])
    dve.op(lambda: nc.vector.tensor_scalar(out=mneg_b[:], in0=tri_f[:], scalar1=-1.0, scalar2=30000.0, op0=ALU.add, op1=ALU.mult),
           r=[t_const], w=[t_const])

    def phase0():
        with ExitStack() as es:
            cc = sbt(es, "cc", [128, 8], F32)
            cact = sbt(es, "cact", [128, 8], F32)
            wa = [sbt(es, f"wa{i}", [128, 8, 512], F32) for i in range(2)]
            t_wa = [Tok(), Tok()]
            mrow = sbt(es, "mrow", [1, 6 * D], F32)
            t_mrow = Tok()
            badd = sbt(es, "badd", [128, 48], F32)
            gm = sbt(es, "gm", [128, 8], F32)
            gf = sbt(es, "gf", [128, 8], F32)
            t_small = Tok()
            prow = [pst(es, f"prow{i}", [1, 512]) for i in range(2)]
            t_prow = [PTok(), PTok()]
            pcol = pst(es, "pcol", [128, 48])
            t_pcol = PTok()
            t_c = Tok()
            sp.dma(cc[:], ccol[:, :], w=[t_c])
            act.op(lambda: nc.scalar.activation(out=cact[:], in_=cc[:], func=AF.Silu), r=[t_c], w=[t_c])
            it = 0
            for l in range(depth):
                sp.dma(badd[:], b_adaT[l], w=[t_small])
                sp.dma(gm[:], g_mixT[l], w=[t_small])
                sp.dma(gf[:], g_ffnT[l], w=[t_small])
                for jg in range(12):
                    b = it % 2
                    it += 1
                    src = w_ada[l].rearrange("(k p) c -> p k c", p=128)[:, :, jg * 512:(jg + 1) * 512]
                    (sp if jg % 2 == 0 else act).dma(wa[b][:], src, w=[t_wa[b]])
                    for k in range(8):
                        pe.op(lambda: nc.tensor.matmul(prow[b][:], lhsT=cact[:, k:k + 1], rhs=wa[b][:, k, :],
                                                       start=(k == 0), stop=(k == 7)),
                              r=[t_c, t_wa[b]], w=[t_prow[b]])
                    dve.op(lambda: nc.vector.tensor_copy(out=mrow[0:1, jg * 512:(jg + 1) * 512], in_=prow[b][:]),
                           r=[t_prow[b]], w=[t_mrow])
                for j in range(48):
                    pe.op(lambda: nc.tensor.matmul(pcol[:, j:j + 1], lhsT=mrow[0:1, j * 128:(j + 1) * 128],
                                                   rhs=ones_f[0:1, 0:1], start=True, stop=True),
                          r=[t_mrow, t_const], w=[t_pcol])
                dve.op(lambda: nc.vector.tensor_tensor(out=modT[l][:], in0=pcol[:], in1=badd[:], op=ALU.add),
                       r=[t_pcol, t_small], w=[t_mod[l]])
                dve.op(lambda: nc.vector.scalar_tensor_tensor(out=A1[l][:], in0=modT[l][:, 8:16], scalar=1.0, in1=gm[:],
                                                              op0=ALU.add, op1=ALU.mult),
                       r=[t_mod[l], t_small], w=[t_mod[l]])
                dve.op(lambda: nc.vector.scalar_tensor_tensor(out=A2[l][:], in0=modT[l][:, 32:40], scalar=1.0, in1=gf[:],
                                                              op0=ALU.add, op1=ALU.mult),
                       r=[t_mod[l], t_small], w=[t_mod[l]])
        fw.barrier()

    def norm_group(xg, t_xg, sq, t_sq, st_ps, t_st, rstd, t_rstd, hT, t_hT, Acol, Bcol, t_ab):
        act.op(lambda: nc.scalar.activation(out=sq[:], in_=xg[:], func=AF.Square), r=[t_xg], w=[t_sq])
        for k in range(8):
            pe.op(lambda: nc.tensor.matmul(st_ps[:], lhsT=ones_b[:], rhs=sq[:, k, :], start=(k == 0), stop=(k == 7)),
                  r=[t_sq, t_const], w=[t_st])
        act.op(lambda: nc.scalar.activation(out=rstd[:], in_=st_ps[:], func=AF.Sqrt, bias=EPS, scale=1.0 / D),
               r=[t_st], w=[t_rstd])
        dve.op(lambda: nc.vector.reciprocal(out=rstd[:], in_=rstd[:]), r=[t_rstd], w=[t_rstd])
        for k in range(8):
            dve.op(lambda: nc.vector.scalar_tensor_tensor(out=xg[:, k, :], in0=xg[:, k, :], scalar=Acol[:, k:k + 1],
                                                          in1=rstd[:], op0=ALU.mult, op1=ALU.mult),
                   r=[t_xg, t_rstd, t_ab], w=[t_xg])
            act.op(lambda: nc.scalar.activation(out=hT[:, k, :], in_=xg[:, k, :], func=AF.Identity,
                                                bias=Bcol[:, k:k + 1], scale=1.0),
                   r=[t_xg, t_ab], w=[t_hT])

    def phase1(l, x_src):
        with ExitStack() as es:
            w_sb = sbt(es, "w_in_sb", [128, 8, NIN], BF16)
            t_w = [Tok() for _ in range(8)]
            for k in range(8):
                pool.dma(w_sb[:, k, :], w_in[l, k * 128:(k + 1) * 128, :], w=[t_w[k]])
            gb = sbt(es, "gb", [128, 16], F32)
            nbfox = sbt(es, "nbfox", [8, 1], F32)
            gq = sbt(es, "gq", [128, 2], F32)
            gmo = sbt(es, "gmo", [128, 256], F32)
            gsg = sbt(es, "gsg", [128, 256], F32)
            wsf = sbt(es, "wsf", [128, 4, 128], F32)
            wsb = sbt(es, "wsb", [128, 4, 128], BF16)
            bs = sbt(es, "bs", [128, 4], F32)
            t_p = Tok()
            sp.dma(gb[:], gbias[l:l + 1, :].partition_broadcast(128), w=[t_p])
            sp.dma(nbfox[:], bfoxT[l], w=[t_p])
            sp.dma(gq[:], gqk[l], w=[t_p])
            sp.dma(gmo[:], g_m_out[l:l + 1, :].partition_broadcast(128), w=[t_p])
            sp.dma(gsg[:], g_sgu[l:l + 1, :].partition_broadcast(128), w=[t_p])
            sp.dma(wsf[:], w_sT[l].rearrange("g s t -> s g t"), w=[t_p])
            sp.dma(bs[:], b_sT[l], w=[t_p])
            dve.op(lambda: nc.vector.tensor_scalar(out=nbfox[:], in0=nbfox[:], scalar1=-1.0, scalar2=None, op0=ALU.mult),
                   r=[t_p], w=[t_p])
            dve.op(lambda: nc.vector.tensor_scalar(out=gq[:, 0:1], in0=gq[:, 0:1], scalar1=0.125, scalar2=None, op0=ALU.mult),
                   r=[t_p], w=[t_p])
            dve.op(lambda: nc.vector.tensor_tensor(out=wsb[:], in0=wsf[:], in1=tri4[:], op=ALU.mult),
                   r=[t_p, t_const], w=[t_p])

            xg = [sbt(es, f"xg{i}", [128, 8, 512], F32) for i in range(2)]
            t_xg = [Tok(), Tok()]
            sq = sbt(es, "sq", [128, 8, 512], BF16)
            t_sq = Tok()
            rstd = sbt(es, "rstd", [128, 512], F32)
            t_rstd = Tok()
            hT = [sbt(es, f"hT{i}", [128, 8, 512], BF16) for i in range(2)]
            t_hT = [Tok(), Tok()]
            sqq = [sbt(es, f"sqq{i}", [128, 512], BF16) for i in range(3)]
            rq = [sbt(es, f"rq{i}", [128, 512], F32) for i in range(3)]
            qn = [sbt(es, f"qn{i}", [128, 512], BF16) for i in range(3)]
            t_sqq = [Tok() for _ in range(3)]
            t_rq = [Tok() for _ in range(3)]
            t_qn = [Tok() for _ in range(3)]
            qmT = [sbt(es, f"qmT{i}", [64, 4, 512], BF16) for i in range(2)]
            kmT = [sbt(es, f"kmT{i}", [64, 4, 512], BF16) for i in range(2)]
            t_qmT = [[Tok() for _ in range(4)] for _ in range(2)]
            t_kmT = [[Tok() for _ in range(4)] for _ in range(2)]
            fe = sbt(es, "fe", [8, 512], F32)
            fneg = sbt(es, "fneg", [8, 512], F32)
            fcar = sbt(es, "fcar", [8, 1], F32)
            fbf = sbt(es, "fbf", [8, 512], BF16)
            t_fe, t_fneg, t_fcar, t_fbf = Tok(), Tok(), Tok(), Tok()
            dve.op(lambda: nc.vector.memset(fcar[:], 0.0), w=[t_fcar])
            ymT = [sbt(es, f"ymT{i}", [128, 4, 512], BF16) for i in range(2)]
            t_ymT = [Tok(), Tok()]

            class BB:
                pass
            bbs = []
            for i in range(2):
                o = BB()
                o.va = sbt(es, f"va{i}", [128, 8, 128], BF16)
                o.t_va = Tok()
                pool.op(lambda: nc.gpsimd.memset(o.va[:], 1.0), w=[o.t_va])
                o.kmt = sbt(es, f"kmt{i}", [128, 256], BF16)
                o.t_kmt = Tok()
                o.vaug = sbt(es, f"vaug{i}", [128, 4, 68], BF16)
                o.t_vaug = Tok()
                o.go = sbt(es, f"go{i}", [128, 256], F32)
                o.t_go = Tok()
                o.ug = sbt(es, f"ug{i}", [128, 256], F32)
                o.t_ug = Tok()
                o.vg = sbt(es, f"vg{i}", [128, 256], F32)
                o.t_vg = Tok()
                o.gt = sbt(es, f"gt{i}", [128, 16], F32)
                o.nl = sbt(es, f"nl{i}", [128, 16], F32)
                o.t_gt, o.t_nl = Tok(), Tok()
                o.sm = sbt(es, f"sm{i}", [128, 64], F32)
                o.t_smA = Tok()
                o.t_smB = Tok()
                o.a_all = sbt(es, f"a_all{i}", [128, 4], F32)
                o.t_a = Tok()
                bbs.append(o)
            carF = sbt(es, "carF", [128, 8], F32)
            t_carF = Tok()
            dve.op(lambda: nc.vector.memset(carF[:], 0.0), w=[t_carF])
            AT = sbt(es, "AT", [128, 4, 128], BF16)
            t_AT = Tok()
            S_f = sbt(es, "S_f", [64, 4, 68], F32)
            S_b = sbt(es, "S_b", [64, 4, 68], BF16)
            t_Sf, t_Sb = Tok(), Tok()
            dve.op(lambda: nc.vector.memset(S_f[:], 0.0), w=[t_Sf])
            dve.op(lambda: nc.vector.memset(S_b[:], 0.0), w=[t_Sb])
            num_sb = sbt(es, "num_sb", [128, 4, 65], F32)
            t_num = Tok()
            sqn = sbt(es, "sqn", [128, 4, 64], F32)
            t_sqn = Tok()
            ymix = sbt(es, "ymix", [128, 512], BF16)
            t_ymix = Tok()
            vtmp = sbt(es, "vtmp", [128, 256], F32)
            t_vtmp = Tok()
            vsq = sbt(es, "vsq", [128, 256], F32)
            t_vsq = Tok()
            vn = sbt(es, "vn", [128, 256], BF16)
            t_vn = Tok()

            st_ps = pst(es, "st_ps", [128, 512])
            t_st = PTok()
            big = RR([(pst(es, f"big{i}", [128, 512]), PTok()) for i in range(3)])
            sc_ps = pst(es, "sc_ps", [128, 4, 128])
            t_sc = PTok()
            misc = pst(es, "misc", [128, 512])
            nm_ps = misc[:, 0:260].rearrange("p (h c) -> p h c", h=4)
            t_nm = PTok()
            misc2 = pst(es, "misc2", [128, 512])
            cb_ps = misc2[:, 260:292]
            t_cb = PTok()
            dS_ps = misc2[0:64, 0:260].rearrange("p (h c) -> p h c", h=4)
            t_dS = t_cb
            tr_ps = pst(es, "tr_ps", [128, 4, 128], BF16)
            t_tr = PTok()

            def proj_fm(col0, ncols, rhs_hT, t_rhs):
                ps_, t_ = big.next()
                for k in range(8):
                    pe.op(lambda: nc.tensor.matmul(ps_[0:ncols, :], lhsT=w_sb[:, k, col0:col0 + ncols], rhs=rhs_hT[:, k, :],
                                                   start=(k == 0), stop=(k == 7)),
                          r=[t_w[k], t_rhs], w=[t_])
                return ps_, t_

            def proj_tm(col0, ncols, hTg, t_rhs, tb):
                ps_, t_ = big.next()
                for k in range(8):
                    pe.op(lambda: nc.tensor.matmul(ps_[:, 0:ncols], lhsT=hTg[:, k, tb:tb + 128], rhs=w_sb[:, k, col0:col0 + ncols],
                                                   start=(k == 0), stop=(k == 7)),
                          r=[t_w[k], t_rhs], w=[t_])
                return ps_, t_

            xsrc = km(x_src)

            def load_x(g):
                sp.dma(xg[g % 2][:], xsrc[:, :, g * 512:(g + 1) * 512], w=[t_xg[g % 2]])

            def norm_pre(g):
                b = g % 2
                act.op(lambda: nc.scalar.activation(out=sq[:], in_=xg[b][:], func=AF.Square), r=[t_xg[b]], w=[t_sq])

            def norm_post(g):
                b = g % 2
                Acol, Bcol, t_ab = A1[l], modT[l][:, 0:8], t_mod[l]
                for k in range(8):
                    pe.op(lambda: nc.tensor.matmul(st_ps[:], lhsT=ones_b[:], rhs=sq[:, k, :], start=(k == 0), stop=(k == 7)),
                          r=[t_sq, t_const], w=[t_st])
                act.op(lambda: nc.scalar.activation(out=rstd[:], in_=st_ps[:], func=AF.Sqrt, bias=EPS, scale=1.0 / D),
                       r=[t_st], w=[t_rstd])
                dve.op(lambda: nc.vector.reciprocal(out=rstd[:], in_=rstd[:]), r=[t_rstd], w=[t_rstd])
                for k in range(8):
                    dve.op(lambda: nc.vector.scalar_tensor_tensor(out=xg[b][:, k, :], in0=xg[b][:, k, :], scalar=Acol[:, k:k + 1],
                                                                  in1=rstd[:], op0=ALU.mult, op1=ALU.mult),
                           r=[t_xg[b], t_rstd, t_ab], w=[t_xg[b]])
                    act.op(lambda: nc.scalar.activation(out=hT[b][:, k, :], in_=xg[b][:, k, :], func=AF.Identity,
                                                        bias=Bcol[:, k:k + 1], scale=1.0),
                           r=[t_xg[b], t_ab], w=[t_hT[b]])

            iq = [0]

            def FM(g):
                b = g % 2
                hTg, t_h = hT[b], t_hT[b]
                pend = None

                def finish_qk(ps_, t_, which, j, i3):
                    ps2, t2 = st_ps, t_st
                    pe.op(lambda: nc.tensor.matmul(ps2[:], lhsT=blk_b[:], rhs=sqq[i3][:], start=True, stop=True),
                          r=[t_sqq[i3], t_const], w=[t2])
                    act.op(lambda: nc.scalar.activation(out=rq[i3][:], in_=ps2[:], func=AF.Sqrt, bias=EPS, scale=1.0 / 64),
                           r=[t2], w=[t_rq[i3]])
                    dve.op(lambda: nc.vector.reciprocal(out=rq[i3][:], in_=rq[i3][:]), r=[t_rq[i3]], w=[t_rq[i3]])
                    dve.op(lambda: nc.vector.scalar_tensor_tensor(out=qn[i3][:], in0=ps_[:], scalar=gq[:, which:which + 1],
                                                                  in1=rq[i3][:], op0=ALU.mult, op1=ALU.mult),
                           r=[t_, t_rq[i3], t_p], w=[t_qn[i3]])
                    dst = (qT_scr if which == 0 else kT_scr)[j * 128:(j + 1) * 128, g * 512:(g + 1) * 512]
                    sp.dma(dst, qn[i3][:], r=[t_qn[i3]])

                for which in range(2):
                    for j in range(4):
                        ps_, t_ = proj_fm(which * 512 + j * 128, 128, hTg, t_h)
                        i3 = iq[0] % 3
                        iq[0] += 1
                        act.op(lambda: nc.scalar.activation(out=sqq[i3][:], in_=ps_[:], func=AF.Square), r=[t_], w=[t_sqq[i3]])
                        if pend is not None:
                            finish_qk(*pend)
                        pend = (ps_, t_, which, j, i3)
                for j in range(4):
                    ps_, t_ = proj_fm(1536 + j * 64, 64, hTg, t_h)
                    act.op(lambda: nc.scalar.copy(out=qmT[b][:, j, :], in_=ps_[0:64, :]), r=[t_], w=[t_qmT[b][j]])
                    if pend is not None:
                        finish_qk(*pend)
                        pend = None
                    ps_, t_ = proj_fm(1792 + j * 64, 64, hTg, t_h)
                    act.op(lambda: nc.scalar.mul(out=kmT[b][:, j, :], in_=ps_[0:64, :], mul=0.125), r=[t_], w=[t_kmT[b][j]])
                ps_, t_ = proj_fm(3072, 8, hTg, t_h)
                act.op(lambda: nc.scalar.activation(out=fe[:], in_=ps_[0:8, :], func=AF.Exp, bias=nbfox[:, 0:1], scale=-1.0),
                       r=[t_, t_p], w=[t_fe])
                act.op(lambda: nc.scalar.activation(out=fe[:], in_=fe[:], func=AF.Ln, bias=1.0, scale=1.0), r=[t_fe], w=[t_fe])
                dve.op(lambda: nc.vector.tensor_tensor_scan(out=fneg[:], data0=ones_f[0:8, :], data1=fe[:], initial=fcar[:, 0:1],
                                                            op0=ALU.mult, op1=ALU.add),
                       r=[t_fe, t_fcar, t_const], w=[t_fneg])
                dve.op(lambda: nc.vector.tensor_copy(out=fcar[:], in_=fneg[:, 511:512]), r=[t_fneg], w=[t_fcar])
                dve.op(lambda: nc.vector.tensor_scalar(out=fbf[:], in0=fneg[:], scalar1=-1.0, scalar2=None, op0=ALU.mult),
                       r=[t_fneg], w=[t_fbf])
                sp.dma(FT_scr[:, g * 512:(g + 1) * 512], fbf[:], r=[t_fbf])

            def stageA(t):
                g, bi = divmod(t, 4)
                tb = bi * 128
                hTg, t_h = hT[g % 2], t_hT[g % 2]
                o = bbs[t % 2]
                sm = o.sm
                ps_g, t_g = proj_tm(2816, 272, hTg, t_h, tb)
                dve.op(lambda: nc.vector.tensor_tensor(out=o.gt[:], in0=ps_g[:, 256:272], in1=gb[:], op=ALU.add),
                       r=[t_g, t_p], w=[o.t_gt])
                act.op(lambda: nc.scalar.activation(out=o.vg[:], in_=ps_g[:, 0:256], func=AF.Gelu_apprx_tanh), r=[t_g], w=[o.t_vg])
                act.op(lambda: nc.scalar.activation(out=o.nl[:], in_=o.gt[:], func=AF.Exp, scale=-1.0), r=[o.t_gt], w=[o.t_nl])
                act.op(lambda: nc.scalar.activation(out=o.nl[:], in_=o.nl[:], func=AF.Ln, bias=1.0, scale=1.0), r=[o.t_nl], w=[o.t_nl])
                ps_o, t_o = proj_tm(2304, 512, hTg, t_h, tb)
                act.op(lambda: nc.scalar.activation(out=o.go[:], in_=ps_o[:, 0:256], func=AF.Sigmoid), r=[t_o], w=[o.t_go])
                act.op(lambda: nc.scalar.activation(out=o.ug[:], in_=ps_o[:, 256:512], func=AF.Gelu_apprx_tanh), r=[t_o], w=[o.t_ug])
                pool.op(lambda: nc.gpsimd.tensor_tensor(out=o.go[:], in0=o.go[:], in1=gmo[:], op=ALU.mult), r=[o.t_go, t_p], w=[o.t_go])
                ps_v, t_v = proj_tm(1024, 512, hTg, t_h, tb)
                act.op(lambda: nc.scalar.copy(out=o.va[:, :, 0:64], in_=ps_v[:].rearrange("p (h d) -> p h d", h=8)),
                       r=[t_v], w=[o.t_va])
                sp.dma(v_scr[:, t * 128:(t + 1) * 128, :].rearrange("h t c -> t h c"), o.va[:], r=[o.t_va])
                pe.op(lambda: nc.tensor.matmul(cb_ps[:, 0:16], lhsT=tri_f[:], rhs=o.nl[:], start=True, stop=True),
                      r=[o.t_nl, t_const], w=[t_cb])
                pe.op(lambda: nc.tensor.matmul(cb_ps[:, 16:32], lhsT=ones_f[:, 0:128], rhs=o.nl[:], start=True, stop=True),
                      r=[o.t_nl, t_const], w=[t_cb])
                dve.op(lambda: nc.vector.tensor_tensor(out=negF[:, t, :], in0=cb_ps[:, 0:8], in1=carF[:], op=ALU.add),
                       r=[t_cb, t_carF], w=[t_negF])
                dve.op(lambda: nc.vector.tensor_tensor(out=carF[:], in0=cb_ps[:, 16:24], in1=carF[:], op=ALU.add),
                       r=[t_cb, t_carF], w=[t_carF])
                dve.op(lambda: nc.vector.tensor_tensor(out=sm[:, 8:12], in0=cb_ps[:, 12:16], in1=o.gt[:, 8:12], op=ALU.add),
                       r=[t_cb, o.t_gt], w=[o.t_smA])
                dve.op(lambda: nc.vector.tensor_scalar(out=sm[:, 8:12], in0=sm[:, 8:12], scalar1=80.0, scalar2=None, op0=ALU.min),
                       r=[o.t_smA], w=[o.t_smA])
                dve.op(lambda: nc.vector.tensor_scalar(out=sm[:, 48:52], in0=cb_ps[:, 12:16], scalar1=80.0, scalar2=None, op0=ALU.min),
                       r=[t_cb, o.t_smA], w=[o.t_smA])
                act.op(lambda: nc.scalar.activation(out=sm[:, 0:4], in_=sm[:, 8:12], func=AF.Exp), r=[o.t_smA], w=[o.t_smA])
                act.op(lambda: nc.scalar.activation(out=sm[:, 4:8], in_=sm[:, 48:52], func=AF.Exp), r=[o.t_smA], w=[o.t_smA])
                act.op(lambda: nc.scalar.activation(out=o.a_all[:], in_=cb_ps[:, 28:32], func=AF.Exp, scale=-1.0),
                       r=[t_cb], w=[o.t_a])
                ps_k, t_k = proj_tm(1792, 512, hTg, t_h, tb)
                act.op(lambda: nc.scalar.mul(out=o.kmt[:], in_=ps_k[:, 0:256], mul=0.125), r=[t_k], w=[o.t_kmt])
                for h in range(4):
                    dve.op(lambda: nc.vector.tensor_scalar(out=o.vaug[:, h, 0:64], in0=ps_k[:, 256 + h * 64:320 + h * 64],
                                                           scalar1=sm[:, h:h + 1], scalar2=None, op0=ALU.mult),
                           r=[t_k, o.t_smA], w=[o.t_vaug])
                dve.op(lambda: nc.vector.tensor_copy(out=o.vaug[:, :, 64], in_=sm[:, 0:4]), r=[o.t_smA], w=[o.t_vaug])

            def stageB(t):
                g, bi = divmod(t, 4)
                tb = bi * 128
                gb2 = g % 2
                o = bbs[t % 2]
                sm = o.sm
                t_sm = o.t_smB
                for h in range(4):
                    pe.op(lambda: nc.tensor.matmul(sc_ps[:, h, :], lhsT=kmT[gb2][:, h, tb:tb + 128],
                                                   rhs=qmT[gb2][:, h, tb:tb + 128], start=True, stop=True),
                          r=[t_kmT[gb2][h], t_qmT[gb2][h]], w=[t_sc])
                dve.op(lambda: nc.vector.tensor_tensor(out=AT[:], in0=sc_ps[:], in1=tri4[:], op=ALU.mult),
                       r=[t_sc, t_const], w=[t_AT])
                dve.op(lambda: nc.vector.tensor_reduce(out=sm[:, 32:36], in_=o.vg[:].rearrange("p (g c) -> p g c", g=4), axis=AX.X, op=ALU.add),
                       r=[o.t_vg, t_sm], w=[t_sm])
                pool.op(lambda: nc.gpsimd.tensor_tensor(out=vsq[:], in0=o.vg[:], in1=o.vg[:], op=ALU.mult), r=[o.t_vg], w=[t_vsq])
                dve.op(lambda: nc.vector.tensor_reduce(out=sm[:, 36:40], in_=vsq[:].rearrange("p (g c) -> p g c", g=4), axis=AX.X, op=ALU.add),
                       r=[t_vsq, t_sm], w=[t_sm])
                dve.op(lambda: nc.vector.tensor_scalar(out=sm[:, 40:44], in0=sm[:, 32:36], scalar1=1.0 / 64, scalar2=None, op0=ALU.mult),
                       r=[t_sm], w=[t_sm])
                dve.op(lambda: nc.vector.tensor_tensor(out=sm[:, 44:48], in0=sm[:, 40:44], in1=sm[:, 40:44], op=ALU.mult),
                       r=[t_sm], w=[t_sm])
                dve.op(lambda: nc.vector.scalar_tensor_tensor(out=sm[:, 44:48], in0=sm[:, 36:40], scalar=1.0 / 64, in1=sm[:, 44:48],
                                                              op0=ALU.mult, op1=ALU.subtract),
                       r=[t_sm], w=[t_sm])
                act.op(lambda: nc.scalar.activation(out=sm[:, 44:48], in_=sm[:, 44:48], func=AF.Sqrt, bias=EPS, scale=1.0),
                       r=[t_sm], w=[t_sm])
                dve.op(lambda: nc.vector.reciprocal(out=sm[:, 44:48], in_=sm[:, 44:48]), r=[t_sm], w=[t_sm])
                for gi in range(4):
                    pool.op(lambda: nc.gpsimd.tensor_scalar(out=vtmp[:, gi * 64:(gi + 1) * 64], in0=o.vg[:, gi * 64:(gi + 1) * 64],
                                                            scalar1=sm[:, 40 + gi:41 + gi], scalar2=sm[:, 44 + gi:45 + gi],
                                                            op0=ALU.subtract, op1=ALU.mult),
                            r=[o.t_vg, t_sm, t_vtmp], w=[t_vtmp])
                pool.op(lambda: nc.gpsimd.tensor_tensor(out=vn[:], in0=vtmp[:], in1=gsg[:], op=ALU.mult), r=[t_vtmp, t_p], w=[t_vn])
                for h in range(4):
                    pe.op(lambda: nc.tensor.matmul(nm_ps[:, h, :], lhsT=AT[:, h, :], rhs=o.vaug[:, h, 0:65], start=True, stop=False),
                          r=[t_AT, o.t_vaug], w=[t_nm])
                    pe.op(lambda: nc.tensor.matmul(nm_ps[:, h, :], lhsT=qmT[gb2][:, h, tb:tb + 128], rhs=S_b[:, h, 0:65],
                                                   start=False, stop=True),
                          r=[t_qmT[gb2][h], t_Sb], w=[t_nm])
                for h in range(4):
                    pe.op(lambda: nc.tensor.matmul(dS_ps[:, h, :], lhsT=o.kmt[:, h * 64:(h + 1) * 64], rhs=o.vaug[:, h, 0:65],
                                                   start=True, stop=True),
                          r=[o.t_kmt, o.t_vaug], w=[t_dS])
                mx_ps, t_mx = big.next()
                for gi in range(4):
                    pe.op(lambda: nc.tensor.matmul(mx_ps[:, gi * 64:(gi + 1) * 64], lhsT=wsb[:, gi, :], rhs=vn[:, gi * 64:(gi + 1) * 64],
                                                   start=True, stop=True),
                          r=[t_vn, t_p], w=[t_mx])
                dve.op(lambda: nc.vector.tensor_tensor(out=S_f[:, :, 0:65], in0=dS_ps[:], in1=S_f[:, :, 0:65], op=ALU.add),
                       r=[t_dS, t_Sf], w=[t_Sf])
                for h in range(4):
                    dve.op(lambda: nc.vector.tensor_scalar(out=S_f[:, h, 0:65], in0=S_f[:, h, 0:65],
                                                           scalar1=o.a_all[0:64, h:h + 1], scalar2=None, op0=ALU.mult),
                           r=[t_Sf, o.t_a], w=[t_Sf])
                dve.op(lambda: nc.vector.tensor_copy(out=S_b[:], in_=S_f[:]), r=[t_Sf], w=[t_Sb])
                act.op(lambda: nc.scalar.copy(out=num_sb[:], in_=nm_ps[:]), r=[t_nm], w=[t_num])
                dve.op(lambda: nc.vector.tensor_scalar(out=sm[:, 12:16], in0=num_sb[:, :, 64], scalar1=-1.0, scalar2=None, op0=ALU.mult),
                       r=[t_num, t_sm], w=[t_sm])
                dve.op(lambda: nc.vector.tensor_tensor(out=sm[:, 16:20], in0=num_sb[:, :, 64], in1=sm[:, 12:16], op=ALU.max),
                       r=[t_num, t_sm], w=[t_sm])
                dve.op(lambda: nc.vector.tensor_tensor(out=sm[:, 16:20], in0=sm[:, 16:20], in1=sm[:, 4:8], op=ALU.max),
                       r=[t_sm, o.t_smA], w=[t_sm])
                dve.op(lambda: nc.vector.reciprocal(out=sm[:, 16:20], in_=sm[:, 16:20]), r=[t_sm], w=[t_sm])
                for h in range(4):
                    dve.op(lambda: nc.vector.tensor_scalar(out=num_sb[:, h, 0:64], in0=num_sb[:, h, 0:64],
                                                           scalar1=sm[:, 16 + h:17 + h], scalar2=None, op0=ALU.mult),
                           r=[t_num, t_sm], w=[t_num])
                pool.op(lambda: nc.gpsimd.tensor_tensor(out=sqn[:], in0=num_sb[:, :, 0:64], in1=num_sb[:, :, 0:64], op=ALU.mult),
                        r=[t_num], w=[t_sqn])
                dve.op(lambda: nc.vector.tensor_reduce(out=sm[:, 20:24], in_=sqn[:], axis=AX.X, op=ALU.add),
                       r=[t_sqn, t_sm], w=[t_sm])
                act.op(lambda: nc.scalar.activation(out=sm[:, 24:28], in_=sm[:, 20:24], func=AF.Sqrt, bias=EPS, scale=1.0 / 64),
                       r=[t_sm], w=[t_sm])
                dve.op(lambda: nc.vector.reciprocal(out=sm[:, 28:32], in_=sm[:, 24:28]), r=[t_sm], w=[t_sm])
                for h in range(4):
                    dve.op(lambda: nc.vector.scalar_tensor_tensor(out=ymix[:, h * 64:(h + 1) * 64], in0=num_sb[:, h, 0:64],
                                                                  scalar=sm[:, 28 + h:29 + h], in1=o.go[:, h * 64:(h + 1) * 64],
                                                                  op0=ALU.mult, op1=ALU.mult),
                           r=[t_num, t_sm, o.t_go], w=[t_ymix])
                for gi in range(4):
                    dve.op(lambda: nc.vector.scalar_tensor_tensor(out=ymix[:, 256 + gi * 64:320 + gi * 64], in0=mx_ps[:, gi * 64:(gi + 1) * 64],
                                                                  scalar=bs[:, gi:gi + 1], in1=o.ug[:, gi * 64:(gi + 1) * 64],
                                                                  op0=ALU.add, op1=ALU.mult),
                           r=[t_mx, o.t_ug, t_p], w=[t_ymix])
                for c in range(4):
                    pe.op(lambda: nc.tensor.transpose(tr_ps[:, c, :], ymix[:, c * 128:(c + 1) * 128], ident_b[:]),
                          r=[t_ymix, t_const], w=[t_tr])
                act.op(lambda: nc.scalar.copy(out=ymT[gb2][:, :, tb:tb + 128], in_=tr_ps[:]), r=[t_tr], w=[t_ymT[gb2]])
                if bi == 3:
                    sp.dma(yT_scr[512:1024, g * 512:(g + 1) * 512].rearrange("(c p) t -> p c t", p=128), ymT[gb2][:], r=[t_ymT[gb2]])

            load_x(0)
            norm_pre(0)
            norm_post(0)
            for t in range(NB + 1):
                if t < NB:
                    g, bi = divmod(t, 4)
                    if bi == 0:
                        if g + 1 < G:
                            load_x(g + 1)
                            norm_pre(g + 1)
                        FM(g)
                    if bi == 1 and g + 1 < G:
                        norm_post(g + 1)
                    stageA(t)
                if t >= 1:
                    stageB(t - 1)
        fw.barrier()

    def phase2(l):
        with ExitStack() as es:
            Qa = [sbt(es, f"Qa{i}", [65, S], BF16) for i in range(2)]
            Ka = [sbt(es, f"Ka{i}", [65, S], BF16) for i in range(2)]
            Va = [sbt(es, f"Va{i}", [128, NB, 128], BF16) for i in range(2)]
            t_Q = [Tok(), Tok()]
            t_K = [Tok(), Tok()]
            t_V = [Tok(), Tok()]
            for i in range(2):
                pool.op(lambda: nc.gpsimd.memset(Ka[i][64:65, :], 1.0), w=[t_K[i]])
            P = RR([(sbt(es, f"P{i}", [128, 512], BF16), Tok()) for i in range(6)])
            rden = [sbt(es, f"rden{i}", [128, 512], F32) for i in range(2)]
            t_rden = [Tok(), Tok()]
            yo = [sbt(es, f"yo{i}", [64, 512], BF16) for i in range(2)]
            t_yo = [Tok(), Tok()]
            scp = RR([(pst(es, f"scp{i}", [128, 512]), PTok()) for i in range(6)])
            op_ = RR([(pst(es, f"op{i}", [128, 512]), PTok()) for i in range(2)])
            def load_head(hh):
                bb = hh % 2
                sp.dma(Ka[bb][0:64, :], kT_scr[hh * 64:(hh + 1) * 64, :], w=[t_K[bb]])
                sp.dma(Qa[bb][0:64, :], qT_scr[hh * 64:(hh + 1) * 64, :], w=[t_Q[bb]])
                sp.dma(Qa[bb][64:65, :], FT_scr[hh:hh + 1, :], w=[t_Q[bb]])
                sp.dma(Va[bb][:], v_scr[hh].rearrange("(j p) c -> p j c", p=128), w=[t_V[bb]])

            units = [(h, g, j) for h in range(8) for g in range(G) for j in range(4 * g + 4)]
            LA = 3
            inflight = {}
            cur_o = {}
            io = 0
            load_head(0)
            for i in range(len(units) + LA):
                if i < len(units):
                    h, g, j = units[i]
                    hb = h % 2
                    c0 = max(0, j - 4 * g) * 128
                    s_ps, t_s = scp.next()
                    diag = j >= 4 * g
                    pe.op(lambda: nc.tensor.matmul(s_ps[:, c0:512], lhsT=Ka[hb][:, j * 128:(j + 1) * 128],
                                                   rhs=Qa[hb][:, g * 512 + c0:(g + 1) * 512], start=True, stop=not diag),
                          r=[t_K[hb], t_Q[hb]], w=[t_s])
                    if diag:
                        pe.op(lambda: nc.tensor.matmul(s_ps[:, c0:c0 + 128], lhsT=ident_b[:], rhs=mneg_b[:], start=False, stop=True),
                              r=[t_const], w=[t_s])
                    inflight[i] = (s_ps, t_s, c0)
                k = i - LA
                if k < 0:
                    continue
                h, g, j = units[k]
                hb = h % 2
                nj = 4 * g + 4
                if g == 0 and j == 0 and h + 1 < 8:
                    load_head(h + 1)
                s_ps, t_s, c0 = inflight.pop(k)
                if j == 0:
                    cur_o[(h, g)] = op_.next()
                o_ps, t_o = cur_o[(h, g)]
                p_sb, t_p_ = P.next()
                act.op(lambda: nc.scalar.activation(out=p_sb[:, c0:512], in_=s_ps[:, c0:512], func=AF.Exp,
                                                    bias=negF[:, j, h:h + 1], scale=1.0),
                       r=[t_s, t_negF], w=[t_p_])
                pe.op(lambda: nc.tensor.matmul(o_ps[:, c0:512], lhsT=Va[hb][:, j, :], rhs=p_sb[:, c0:512],
                                               start=(j == 0), stop=(j == nj - 1)),
                      r=[t_V[hb], t_p_], w=[t_o])
                if j == nj - 1:
                    del cur_o[(h, g)]
                    ib = io % 2
                    io += 1
                    dve.op(lambda: nc.vector.reciprocal(out=rden[ib][64:128, :], in_=o_ps[64:128, :]), r=[t_o], w=[t_rden[ib]])
                    dve.op(lambda: nc.vector.tensor_tensor(out=yo[ib][:], in0=o_ps[0:64, :], in1=rden[ib][64:128, :], op=ALU.mult),
                           r=[t_o, t_rden[ib]], w=[t_yo[ib]])
                    sp.dma(yT_scr[h * 64:(h + 1) * 64, g * 512:(g + 1) * 512], yo[ib][:], r=[t_yo[ib]])
        fw.barrier()

    def phase3a(l, x_src, prefetch=None):
        with ExitStack() as es:
            wo = sbt(es, "wo", [128, 8, D], BF16)
            t_wo = [Tok() for _ in range(8)]
            for k in range(8):
                pool.dma(wo[:, k, :], w_out[l, k * 128:(k + 1) * 128, :], w=[t_wo[k]])
            if prefetch is not None:
                prefetch()
            xg = [sbt(es, f"xg{i}", [128, 8, 512], F32) for i in range(2)]
            t_xg = [Tok(), Tok()]
            x1, t_x1 = xg, t_xg
            yg = [sbt(es, f"yg{i}", [128, 8, 512], BF16) for i in range(2)]
            t_yg = [Tok(), Tok()]
            sq = sbt(es, "sq", [128, 8, 512], BF16)
            t_sq = Tok()
            rstd = sbt(es, "rstd", [128, 512], F32)
            t_rstd = Tok()
            hT = [sbt(es, f"hT{i}", [128, 8, 512], BF16) for i in range(2)]
            t_hT = [Tok(), Tok()]
            st_ps = pst(es, "st_ps", [128, 512])
            t_st = PTok()
            big = RR([(pst(es, f"big{i}", [128, 512]), PTok()) for i in range(4)])
            xsrc = km(x_src)
            for g in range(G):
                b = g % 2
                sl = slice(g * 512, (g + 1) * 512)
                if g == 0:
                    sp.dma(xg[0][:], xsrc[:, :, 0:512], w=[t_xg[0]])
                    sp.dma(yg[0][:], km(yT_scr)[:, :, 0:512], w=[t_yg[0]])
                if g + 1 < G:
                    sl2 = slice((g + 1) * 512, (g + 2) * 512)
                    sp.dma(xg[1 - b][:], xsrc[:, :, sl2], w=[t_xg[1 - b]])
                    sp.dma(yg[1 - b][:], km(yT_scr)[:, :, sl2], w=[t_yg[1 - b]])
                for c in range(8):
                    ps_, t_ = big.next()
                    for k in range(8):
                        pe.op(lambda: nc.tensor.matmul(ps_[:], lhsT=wo[:, k, c * 128:(c + 1) * 128], rhs=yg[b][:, k, :],
                                                       start=(k == 0), stop=(k == 7)),
                              r=[t_wo[k], t_yg[b]], w=[t_])
                    dve.op(lambda: nc.vector.scalar_tensor_tensor(out=x1[b][:, c, :], in0=ps_[:], scalar=modT[l][:, 16 + c:17 + c],
                                                                  in1=xg[b][:, c, :], op0=ALU.mult, op1=ALU.add),
                           r=[t_, t_xg[b], t_mod[l]], w=[t_x1[b]])
                sp.dma(km(x1T_scr)[:, :, sl], x1[b][:], r=[t_x1[b]])
                norm_group(x1[b], t_x1[b], sq, t_sq, st_ps, t_st, rstd, t_rstd, hT[b], t_hT[b],
                           A2[l], modT[l][:, 24:32], t_mod[l])
                sp.dma(km(h2T_scr)[:, :, sl], hT[b][:], r=[t_hT[b]])
        fw.barrier()

    def phase3b(l, x_dst, wu, t_wu):
        with ExitStack() as es:
            wd = sbt(es, "wd", [128, NF, D], BF16)
            t_wd = [Tok() for _ in range(NF)]
            for f in range(NF):
                pool.dma(wd[:, f, :], w_down[l, f * 128:(f + 1) * 128, :], w=[t_wd[f]])
            wc = sbt(es, "wc", [128, 2 * NF, 3], F32)
            bc = sbt(es, "bc", [128, 2 * NF], F32)
            t_c = Tok()
            sp.dma(wc[:], w_convT[l], w=[t_c])
            sp.dma(bc[:], b_convT[l], w=[t_c])
            tail = sbt(es, "tail", [128, 2 * NF, 2], F32)
            t_tail = [Tok() for _ in range(2 * NF)]
            pool.op(lambda: nc.gpsimd.memset(tail[:], 0.0), w=t_tail)
            hT = [sbt(es, f"hT{i}", [128, 8, 512], BF16) for i in range(2)]
            t_hT = [Tok(), Tok()]
            zs = RR([(sbt(es, f"zs{i}", [128, 514], F32), Tok()) for i in range(4)])
            cv = RR([(sbt(es, f"cv{i}", [128, 512], F32), Tok()) for i in range(4)])
            sa = RR([(sbt(es, f"sa{i}", [128, 512], F32), Tok()) for i in range(2)])
            actT1 = sbt(es, "actT", [128, NF, 512], BF16)
            actT = [actT1, actT1]
            t_actT1 = [Tok() for _ in range(NF)]
            t_actT = [t_actT1, t_actT1]
            x1t = RR([(sbt(es, f"x1t{i}", [128, 512], F32), Tok()) for i in range(3)])
            zp = RR([(pst(es, f"zp{i}", [128, 512]), PTok()) for i in range(5)])
            dp = RR([(pst(es, f"dp{i}", [128, 512]), PTok()) for i in range(3)])
            dst = km(x_dst)
            for g in range(G):
                b = g % 2
                sl = slice(g * 512, (g + 1) * 512)
                if g == 0:
                    sp.dma(hT[0][:], km(h2T_scr)[:, :, 0:512], w=[t_hT[0]])
                if g + 1 < G:
                    sp.dma(hT[1 - b][:], km(h2T_scr)[:, :, (g + 1) * 512:(g + 2) * 512], w=[t_hT[1 - b]])
                for f in range(NF):
                    cvs = []
                    for br in range(2):
                        ti = br * NF + f
                        z_ps, t_z = zp.next()
                        for k in range(8):
                            pe.op(lambda: nc.tensor.matmul(z_ps[:], lhsT=wu[:, k, ti * 128:(ti + 1) * 128], rhs=hT[b][:, k, :],
                                                           start=(k == 0), stop=(k == 7)),
                                  r=[t_wu[k], t_hT[b]], w=[t_z])
                        z_sb, t_zs = zs.next()
                        act.op(lambda: nc.scalar.copy(out=z_sb[:, 2:514], in_=z_ps[:]), r=[t_z], w=[t_zs])
                        pool.op(lambda: nc.gpsimd.tensor_copy(out=z_sb[:, 0:2], in_=tail[:, ti, :]), r=[t_tail[ti]], w=[t_zs])
                        pool.op(lambda: nc.gpsimd.tensor_copy(out=tail[:, ti, :], in_=z_sb[:, 512:514]), r=[t_zs], w=[t_tail[ti]])
                        c_sb, t_cv = cv.next()
                        act.op(lambda: nc.scalar.activation(out=c_sb[:], in_=z_ps[:], func=AF.Identity,
                                                            bias=bc[:, ti:ti + 1], scale=wc[:, ti, 2:3]),
                               r=[t_z, t_c], w=[t_cv])
                        dve.op(lambda: nc.vector.scalar_tensor_tensor(out=c_sb[:], in0=z_sb[:, 1:513], scalar=wc[:, ti, 1:2],
                                                                      in1=c_sb[:], op0=ALU.mult, op1=ALU.add),
                               r=[t_zs, t_c, t_cv], w=[t_cv])
                        dve.op(lambda: nc.vector.scalar_tensor_tensor(out=c_sb[:], in0=z_sb[:, 0:512], scalar=wc[:, ti, 0:1],
                                                                      in1=c_sb[:], op0=ALU.mult, op1=ALU.add),
                               r=[t_zs, t_c, t_cv], w=[t_cv])
                        cvs.append((c_sb, t_cv))
                    s_sb, t_sa = sa.next()
                    act.op(lambda: nc.scalar.activation(out=s_sb[:], in_=cvs[0][0][:], func=AF.Silu), r=[cvs[0][1]], w=[t_sa])
                    pool.op(lambda: nc.gpsimd.tensor_tensor(out=actT[b][:, f, :], in0=s_sb[:], in1=cvs[1][0][:], op=ALU.mult),
                            r=[t_sa, cvs[1][1]], w=[t_actT[b][f]])
                for c in range(8):
                    xt_, t_xt = x1t.next()
                    sp.dma(xt_[:], x1T_scr[c * 128:(c + 1) * 128, sl], w=[t_xt])
                    d_ps, t_d = dp.next()
                    for f in range(NF):
                        pe.op(lambda: nc.tensor.matmul(d_ps[:], lhsT=wd[:, f, c * 128:(c + 1) * 128], rhs=actT[b][:, f, :],
                                                       start=(f == 0), stop=(f == NF - 1)),
                              r=[t_wd[f], t_actT[b][f]], w=[t_d])
                    dve.op(lambda: nc.vector.scalar_tensor_tensor(out=xt_[:], in0=d_ps[:], scalar=modT[l][:, 40 + c:41 + c],
                                                                  in1=xt_[:], op0=ALU.mult, op1=ALU.add),
                           r=[t_d, t_xt, t_mod[l]], w=[t_xt])
                    sp.dma(x_dst[c * 128:(c + 1) * 128, sl], xt_[:], r=[t_xt])
        fw.barrier()

    phase0()
    cur = xT
    for l in range(depth):
        nxt = outT if l == depth - 1 else xmid_scr
        if upto >= 1:
            phase1(l, cur)
        if upto >= 2:
            phase2(l)
        if upto >= 3:
            with ExitStack() as es3:
                wu = sbt(es3, "wu", [128, 8, 2 * DFF], BF16)
                t_wu = [Tok() for _ in range(8)]

                def pf(l=l, wu=wu, t_wu=t_wu):
                    for k in range(8):
                        pool.dma(wu[:, k, :], w_up[l, k * 128:(k + 1) * 128, :], w=[t_wu[k]])
                phase3a(l, cur, prefetch=pf)
                if upto >= 4:
                    phase3b(l, nxt, wu, t_wu)
        cur = nxt
    if upto < 4:
        with ExitStack() as es:
            dbg = sbt(es, "dbg", [128, 48], F32)
            t_dbg = Tok()
            dve.op(lambda: nc.vector.tensor_copy(out=dbg[:], in_=modT[0][:]), r=[t_mod[0]], w=[t_dbg])
            sp.dma(outT[0:128, 0:48], dbg[:], r=[t_dbg])
            dve.op(lambda: nc.vector.tensor_copy(out=dbg[:, 0:8], in_=A1[0][:]), r=[t_mod[0], t_dbg], w=[t_dbg])
            dve.op(lambda: nc.vector.tensor_copy(out=dbg[:, 8:16], in_=A2[0][:]), r=[t_mod[0], t_dbg], w=[t_dbg])
            sp.dma(outT[128:256, 0:48], dbg[:], r=[t_dbg])
            fw.finish()
    fw.finish()
    top.close()
    build.stats = {s.name: (s.n_ins, s.n_wait) for s in fw.streams}
    return nc


def prep_shared(inputs, depth=DEPTH):
    f = lambda a: np.ascontiguousarray(np.asarray(a, dtype=np.float32))
    sh = {}
    sh["w_ada"] = f(inputs["w_ada"])
    sh["b_adaT"] = f(np.asarray(inputs["b_ada"]).reshape(depth, 48, 128).transpose(0, 2, 1))
    sh["g_mixT"] = f(np.asarray(inputs["g_mix"]).reshape(depth, 8, 128).transpose(0, 2, 1))
    sh["g_ffnT"] = f(np.asarray(inputs["g_ffn"]).reshape(depth, 8, 128).transpose(0, 2, 1))
    sh["w_in"] = f(inputs["w_in"])
    sh["gbias"] = f(np.concatenate([np.asarray(inputs["b_fox_f"]), np.asarray(inputs["b_m_i"]), np.asarray(inputs["b_m_f"])], axis=1))
    sh["bfoxT"] = f(np.asarray(inputs["b_fox_f"])[:, :, None])
    gq = np.tile(np.asarray(inputs["g_q"]), (1, 2))
    gk = np.tile(np.asarray(inputs["g_k"]), (1, 2))
    sh["gqk"] = f(np.stack([gq, gk], axis=-1))
    sh["g_m_out"] = f(inputs["g_m_out"])
    sh["g_sgu"] = f(inputs["g_sgu"])
    sh["w_sT"] = f(np.asarray(inputs["w_s"]).transpose(0, 1, 3, 2))
    sh["b_sT"] = f(np.asarray(inputs["b_s"]).transpose(0, 2, 1))
    sh["w_out"] = f(inputs["w_out"])
    sh["w_up"] = f(inputs["w_up"])
    sh["w_down"] = f(inputs["w_down"])
    sh["w_convT"] = f(np.asarray(inputs["w_conv"]).reshape(depth, 3, 2 * NF, 128).transpose(0, 3, 2, 1))
    sh["b_convT"] = f(np.asarray(inputs["b_conv"]).reshape(depth, 2 * NF, 128).transpose(0, 2, 1))
    sh["tri"] = np.triu(np.ones((128, 128), np.float32))
    sh["ident"] = np.eye(128, dtype=np.float32)
    return sh


def prep_core(x_b, c_b):
    return {"xT": np.ascontiguousarray(np.asarray(x_b, np.float32).T),
            "ccol": np.ascontiguousarray(np.asarray(c_b, np.float32).reshape(8, 128).T)}


def kernel(**inputs):
    x = np.asarray(inputs["x"])
    c = np.asarray(inputs["c"])
    B, S, _ = x.shape
    nc = build(S=S, depth=DEPTH)
    sh = prep_shared(inputs)
    in_maps = []
    for b in range(B):
        m = dict(sh)
        m.update(prep_core(x[b], c[b]))
        in_maps.append(m)
    res = run_bass_kernel_spmd(nc, in_maps, core_ids=list(range(B)))
    out = np.stack([np.asarray(r["outT"]).T for r in res.results], axis=0)
    return np.ascontiguousarray(out.astype(np.float32))
```

```python
import numpy as np
from contextlib import ExitStack
import concourse.bass as bass
import concourse.mybir as mybir
from concourse.bass_utils import run_bass_kernel_spmd

F32 = mybir.dt.float32
BF16 = mybir.dt.bfloat16
AF = mybir.ActivationFunctionType
ALU = mybir.AluOpType
AX = mybir.AxisListType

D = 1024
NIN = 3088
DFF = 2816
NF = DFF // 128
EPS = 1e-6
SEQ = 8192
DEPTH = 2
NCORES = 8


class Tok:
    __slots__ = ("w", "rs", "excl")

    def __init__(self, excl=False):
        self.w = None
        self.rs = []
        self.excl = excl


def PTok():
    return Tok(excl=True)


class Stream:
    def __init__(self, fw, eng, name, n_dma_sems=0, self_sync=True):
        self.fw = fw
        self.eng = eng
        self.name = name
        self.sem = fw.nc.alloc_semaphore("s_" + name)
        self.cnt = 0
        self.known = {}
        self.self_sync = self_sync
        self.dma_sems = [[fw.nc.alloc_semaphore(f"d_{name}{i}"), 0] for i in range(n_dma_sems)]
        self.dma_rr = 0
        self.n_wait = 0
        self.n_ins = 0

    def wait(self, ev):
        if ev is None:
            return
        sem, val = ev
        if (not self.self_sync) and sem is self.sem:
            return
        if self.known.get(sem, 0) >= val:
            return
        self.eng.wait_ge(sem, val)
        self.known[sem] = val
        self.n_wait += 1

    def _deps(self, r, w):
        for t in r:
            self.wait(t.w)
        for t in w:
            self.wait(t.w)
            for ev in t.rs:
                self.wait(ev)

    def _commit(self, ev, r, w):
        for t in r:
            t.rs.append(ev)
            if len(t.rs) > 32:
                d = {}
                for s, v in t.rs:
                    if d.get(s, 0) < v:
                        d[s] = v
                t.rs = list(d.items())
        for t in w:
            t.w = ev
            t.rs = []

    def op(self, fn, r=(), w=()):
        if any(t.excl for t in r):
            w = list(w) + [t for t in r if t.excl]
            r = [t for t in r if not t.excl]
        self._deps(r, w)
        ins = fn()
        self.cnt += 1
        ins.then_inc(self.sem, 1)
        ev = (self.sem, self.cnt)
        self._commit(ev, r, w)
        self.n_ins += 1
        return ev

    def dma(self, out, in_, r=(), w=(), **kw):
        self._deps(r, w)
        slot = self.dma_sems[self.dma_rr]
        self.dma_rr = (self.dma_rr + 1) % len(self.dma_sems)
        sem, c = slot
        if c > 0:
            self.wait((sem, c))
        ins = self.eng.dma_start(out=out, in_=in_, **kw)
        slot[1] = c + 16
        ins.then_inc(sem, 16)
        ev = (sem, c + 16)
        self._commit(ev, r, w)
        self.n_ins += 1
        return ev


class FW:
    def __init__(self, nc):
        self.nc = nc
        self.pe = Stream(self, nc.tensor, "pe", self_sync=False)
        self.act = Stream(self, nc.scalar, "act", n_dma_sems=4)
        self.dve = Stream(self, nc.vector, "dve")
        self.pool = Stream(self, nc.gpsimd, "pool", n_dma_sems=12)
        self.sp = Stream(self, nc.sync, "sp", n_dma_sems=24)
        self.streams = [self.pe, self.act, self.dve, self.pool, self.sp]

    def all_events(self):
        evs = []
        for s in self.streams:
            if s.cnt > 0:
                evs.append((s.sem, s.cnt))
            for sem, c in s.dma_sems:
                if c > 0:
                    evs.append((sem, c))
        return evs

    def barrier(self):
        evs = self.all_events()
        for s in self.streams:
            for ev in evs:
                if ev[0] is s.sem:
                    continue
                s.wait(ev)

    def finish(self):
        for ev in self.all_events():
            self.sp.wait(ev)


class RR:
    def __init__(self, items):
        self.items = items
        self.i = 0

    def next(self):
        it = self.items[self.i]
        self.i = (self.i + 1) % len(self.items)
        return it


def build(S=SEQ, depth=DEPTH, debug=False, upto=4):
    assert S % 512 == 0
    NB = S // 128
    G = S // 512
    nc = bass.Bass("TRN2", target_bir_lowering=False)
    fw = FW(nc)
    pe, act, dve, pool, sp = fw.pe, fw.act, fw.dve, fw.pool, fw.sp

    def din(name, shape, dt=F32):
        return nc.dram_tensor(name, list(shape), dt, kind="ExternalInput").ap()

    skind = "ExternalOutput" if debug else "Internal"

    def dscr(name, shape, dt):
        return nc.dram_tensor(name, list(shape), dt, kind=skind).ap()

    xT = din("xT", [D, S])
    ccol = din("ccol", [128, 8])
    w_ada = din("w_ada", [depth, D, 6 * D])
    b_adaT = din("b_adaT", [depth, 128, 48])
    g_mixT = din("g_mixT", [depth, 128, 8])
    g_ffnT = din("g_ffnT", [depth, 128, 8])
    w_in = din("w_in", [depth, D, NIN])
    gbias = din("gbias", [depth, 16])
    bfoxT = din("bfoxT", [depth, 8, 1])
    gqk = din("gqk", [depth, 128, 2])
    g_m_out = din("g_m_out", [depth, 256])
    g_sgu = din("g_sgu", [depth, 256])
    w_sT = din("w_sT", [depth, 4, 128, 128])
    b_sT = din("b_sT", [depth, 128, 4])
    w_out = din("w_out", [depth, D, D])
    w_up = din("w_up", [depth, D, 2 * DFF])
    w_down = din("w_down", [depth, DFF, D])
    w_convT = din("w_convT", [depth, 128, 2 * NF, 3])
    b_convT = din("b_convT", [depth, 128, 2 * NF])
    tri_d = din("tri", [128, 128])
    ident_d = din("ident", [128, 128])
    outT = nc.dram_tensor("outT", [D, S], F32, kind="ExternalOutput").ap()

    qT_scr = dscr("qT_scr", [512, S], BF16)
    kT_scr = dscr("kT_scr", [512, S], BF16)
    FT_scr = dscr("FT_scr", [8, S], BF16)
    v_scr = dscr("v_scr", [8, S, 128], BF16)
    yT_scr = dscr("yT_scr", [D, S], BF16)
    x1T_scr = dscr("x1T_scr", [D, S], F32)
    h2T_scr = dscr("h2T_scr", [D, S], BF16)
    xmid_scr = dscr("xmid_scr", [D, S], F32)

    def km(ap):
        return ap.rearrange("(k p) t -> p k t", p=128)

    top = ExitStack()

    uid = [0]

    def sbt(es, name, shape, dt):
        uid[0] += 1
        return es.enter_context(nc.sbuf_tensor(f"{name}_{uid[0]}", list(shape), dt))

    def pst(es, name, shape, dt=F32):
        uid[0] += 1
        return es.enter_context(nc.psum_tensor(f"{name}_{uid[0]}", list(shape), dt))

    tri_f = sbt(top, "tri_f", [128, 128], F32)
    tri4 = sbt(top, "tri4", [128, 4, 128], F32)
    ident_f = sbt(top, "ident_f", [128, 128], F32)
    ident_b = sbt(top, "ident_b", [128, 128], BF16)
    ones_f = sbt(top, "ones_f", [128, 512], F32)
    ones_b = sbt(top, "ones_b", [128, 128], BF16)
    blk_b = sbt(top, "blk_b", [128, 128], BF16)
    mneg_b = sbt(top, "mneg_b", [128, 128], BF16)
    negF = sbt(top, "negF", [128, NB, 8], F32)
    modT = [sbt(top, f"modT{l}", [128, 48], F32) for l in range(depth)]
    A1 = [sbt(top, f"A1_{l}", [128, 8], F32) for l in range(depth)]
    A2 = [sbt(top, f"A2_{l}", [128, 8], F32) for l in range(depth)]
    t_const = Tok()
    t_negF = Tok()
    t_mod = [Tok() for _ in range(depth)]

    sp.dma(tri_f[:], tri_d[:, :], w=[t_const])
    sp.dma(ident_f[:], ident_d[:, :], w=[t_const])
    for h in range(4):
        dve.op(lambda: nc.vector.tensor_copy(out=tri4[:, h, :], in_=tri_f[:]), r=[t_const], w=[t_const])
    dve.op(lambda: nc.vector.tensor_copy(out=ident_b[:], in_=ident_f[:]), r=[t_const], w=[t_const])
    dve.op(lambda: nc.vector.memset(ones_f[:], 1.0), w=[t_const])
    dve.op(lambda: nc.vector.memset(ones_b[:], 1.0), w=[t_const])
    dve.op(lambda: nc.vector.memset(blk_b[:], 0.0), w=[t_const])
    dve.op(lambda: nc.vector.memset(blk_b[0:64, 0:64], 1.0), w=[t_const])
    dve.op(lambda: nc.vector.memset(blk_b[64:128, 64:128], 1.0), w=[t_const])
    dve.op(lambda: nc.vector.tensor_scalar(out=mneg_b[:], in0=tri_f[:], scalar1=-1.0, scalar2=30000.0, op0=ALU.add, op1=ALU.mult),
           r=[t_const], w=[t_const])

    def phase0():
        with ExitStack() as es:
            cc = sbt(es, "cc", [128, 8], F32)
            cact = sbt(es, "cact", [128, 8], F32)
            wa = [sbt(es, f"wa{i}", [128, 8, 512], F32) for i in range(2)]
            t_wa = [Tok(), Tok()]
            mrow = sbt(es, "mrow", [1, 6 * D], F32)
            t_mrow = Tok()
            badd = sbt(es, "badd", [128, 48], F32)
            gm = sbt(es, "gm", [128, 8], F32)
            gf = sbt(es, "gf", [128, 8], F32)
            t_small = Tok()
            prow = [pst(es, f"prow{i}", [1, 512]) for i in range(2)]
            t_prow = [PTok(), PTok()]
            pcol = pst(es, "pcol", [128, 48])
            t_pcol = PTok()
            t_c = Tok()
            sp.dma(cc[:], ccol[:, :], w=[t_c])
            act.op(lambda: nc.scalar.activation(out=cact[:], in_=cc[:], func=AF.Silu), r=[t_c], w=[t_c])
            it = 0
            for l in range(depth):
                sp.dma(badd[:], b_adaT[l], w=[t_small])
                sp.dma(gm[:], g_mixT[l], w=[t_small])
                sp.dma(gf[:], g_ffnT[l], w=[t_small])
                for jg in range(12):
                    b = it % 2
                    it += 1
                    src = w_ada[l].rearrange("(k p) c -> p k c", p=128)[:, :, jg * 512:(jg + 1) * 512]
                    (sp if jg % 2 == 0 else act).dma(wa[b][:], src, w=[t_wa[b]])
                    for k in range(8):
                        pe.op(lambda: nc.tensor.matmul(prow[b][:], lhsT=cact[:, k:k + 1], rhs=wa[b][:, k, :],
                                                       start=(k == 0), stop=(k == 7)),
                              r=[t_c, t_wa[b]], w=[t_prow[b]])
                    dve.op(lambda: nc.vector.tensor_copy(out=mrow[0:1, jg * 512:(jg + 1) * 512], in_=prow[b][:]),
                           r=[t_prow[b]], w=[t_mrow])
                for j in range(48):
                    pe.op(lambda: nc.tensor.matmul(pcol[:, j:j + 1], lhsT=mrow[0:1, j * 128:(j + 1) * 128],
                                                   rhs=ones_f[0:1, 0:1], start=True, stop=True),
                          r=[t_mrow, t_const], w=[t_pcol])
                dve.op(lambda: nc.vector.tensor_tensor(out=modT[l][:], in0=pcol[:], in1=badd[:], op=ALU.add),
                       r=[t_pcol, t_small], w=[t_mod[l]])
                dve.op(lambda: nc.vector.scalar_tensor_tensor(out=A1[l][:], in0=modT[l][:, 8:16], scalar=1.0, in1=gm[:],
                                                              op0=ALU.add, op1=ALU.mult),
                       r=[t_mod[l], t_small], w=[t_mod[l]])
                dve.op(lambda: nc.vector.scalar_tensor_tensor(out=A2[l][:], in0=modT[l][:, 32:40], scalar=1.0, in1=gf[:],
                                                              op0=ALU.add, op1=ALU.mult),
                       r=[t_mod[l], t_small], w=[t_mod[l]])
        fw.barrier()

    def norm_group(xg, t_xg, sq, t_sq, st_ps, t_st, rstd, t_rstd, hT, t_hT, Acol, Bcol, t_ab):
        act.op(lambda: nc.scalar.activation(out=sq[:], in_=xg[:], func=AF.Square), r=[t_xg], w=[t_sq])
        for k in range(8):
            pe.op(lambda: nc.tensor.matmul(st_ps[:], lhsT=ones_b[:], rhs=sq[:, k, :], start=(k == 0), stop=(k == 7)),
                  r=[t_sq, t_const], w=[t_st])
        act.op(lambda: nc.scalar.activation(out=rstd[:], in_=st_ps[:], func=AF.Sqrt, bias=EPS, scale=1.0 / D),
               r=[t_st], w=[t_rstd])
        dve.op(lambda: nc.vector.reciprocal(out=rstd[:], in_=rstd[:]), r=[t_rstd], w=[t_rstd])
        for k in range(8):
            dve.op(lambda: nc.vector.scalar_tensor_tensor(out=xg[:, k, :], in0=xg[:, k, :], scalar=Acol[:, k:k + 1],
                                                          in1=rstd[:], op0=ALU.mult, op1=ALU.mult),
                   r=[t_xg, t_rstd, t_ab], w=[t_xg])
            act.op(lambda: nc.scalar.activation(out=hT[:, k, :], in_=xg[:, k, :], func=AF.Identity,
                                                bias=Bcol[:, k:k + 1], scale=1.0),
                   r=[t_xg, t_ab], w=[t_hT])

    def phase1(l, x_src):
        with ExitStack() as es:
            w_sb = sbt(es, "w_in_sb", [128, 8, NIN], BF16)
            t_w = [Tok() for _ in range(8)]
            for k in range(8):
                pool.dma(w_sb[:, k, :], w_in[l, k * 128:(k + 1) * 128, :], w=[t_w[k]])
            gb = sbt(es, "gb", [128, 16], F32)
            nbfox = sbt(es, "nbfox", [8, 1], F32)
            gq = sbt(es, "gq", [128, 2], F32)
            gmo = sbt(es, "gmo", [128, 256], F32)
            gsg = sbt(es, "gsg", [128, 256], F32)
            wsf = sbt(es, "wsf", [128, 4, 128], F32)
            wsb = sbt(es, "wsb", [128, 4, 128], BF16)
            bs = sbt(es, "bs", [128, 4], F32)
            t_p = Tok()
            sp.dma(gb[:], gbias[l:l + 1, :].partition_broadcast(128), w=[t_p])
            sp.dma(nbfox[:], bfoxT[l], w=[t_p])
            sp.dma(gq[:], gqk[l], w=[t_p])
            sp.dma(gmo[:], g_m_out[l:l + 1, :].partition_broadcast(128), w=[t_p])
            sp.dma(gsg[:], g_sgu[l:l + 1, :].partition_broadcast(128), w=[t_p])
            sp.dma(wsf[:], w_sT[l].rearrange("g s t -> s g t"), w=[t_p])
            sp.dma(bs[:], b_sT[l], w=[t_p])
            dve.op(lambda: nc.vector.tensor_scalar(out=nbfox[:], in0=nbfox[:], scalar1=-1.0, scalar2=None, op0=ALU.mult),
                   r=[t_p], w=[t_p])
            dve.op(lambda: nc.vector.tensor_scalar(out=gq[:, 0:1], in0=gq[:, 0:1], scalar1=0.125, scalar2=None, op0=ALU.mult),
                   r=[t_p], w=[t_p])
            dve.op(lambda: nc.vector.tensor_tensor(out=wsb[:], in0=wsf[:], in1=tri4[:], op=ALU.mult),
                   r=[t_p, t_const], w=[t_p])

            xg = [sbt(es, f"xg{i}", [128, 8, 512], F32) for i in range(2)]
            t_xg = [Tok(), Tok()]
            sq = sbt(es, "sq", [128, 8, 512], BF16)
            t_sq = Tok()
            rstd = sbt(es, "rstd", [128, 512], F32)
            t_rstd = Tok()
            hT = [sbt(es, f"hT{i}", [128, 8, 512], BF16) for i in range(2)]
            t_hT = [Tok(), Tok()]
            sqq = [sbt(es, f"sqq{i}", [128, 512], BF16) for i in range(3)]
            rq = [sbt(es, f"rq{i}", [128, 512], F32) for i in range(3)]
            qn = [sbt(es, f"qn{i}", [128, 512], BF16) for i in range(3)]
            t_sqq = [Tok() for _ in range(3)]
            t_rq = [Tok() for _ in range(3)]
            t_qn = [Tok() for _ in range(3)]
            qmT = [sbt(es, f"qmT{i}", [64, 4, 512], BF16) for i in range(2)]
            kmT = [sbt(es, f"kmT{i}", [64, 4, 512], BF16) for i in range(2)]
            t_qmT = [[Tok() for _ in range(4)] for _ in range(2)]
            t_kmT = [[Tok() for _ in range(4)] for _ in range(2)]
            fe = sbt(es, "fe", [8, 512], F32)
            fneg = sbt(es, "fneg", [8, 512], F32)
            fcar = sbt(es, "fcar", [8, 1], F32)
            fbf = sbt(es, "fbf", [8, 512], BF16)
            t_fe, t_fneg, t_fcar, t_fbf = Tok(), Tok(), Tok(), Tok()
            dve.op(lambda: nc.vector.memset(fcar[:], 0.0), w=[t_fcar])
            ymT = [sbt(es, f"ymT{i}", [128, 4, 512], BF16) for i in range(2)]
            t_ymT = [Tok(), Tok()]

            class BB:
                pass
            bbs = []
            for i in range(2):
                o = BB()
                o.va = sbt(es, f"va{i}", [128, 8, 128], BF16)
                o.t_va = Tok()
                pool.op(lambda: nc.gpsimd.memset(o.va[:], 1.0), w=[o.t_va])
                o.kmt = sbt(es, f"kmt{i}", [128, 256], BF16)
                o.t_kmt = Tok()
                o.vaug = sbt(es, f"vaug{i}", [128, 4, 68], BF16)
                o.t_vaug = Tok()
                o.go = sbt(es, f"go{i}", [128, 256], F32)
                o.t_go = Tok()
                o.ug = sbt(es, f"ug{i}", [128, 256], F32)
                o.t_ug = Tok()
                o.vg = sbt(es, f"vg{i}", [128, 256], F32)
                o.t_vg = Tok()
                o.gt = sbt(es, f"gt{i}", [128, 16], F32)
                o.nl = sbt(es, f"nl{i}", [128, 16], F32)
                o.t_gt, o.t_nl = Tok(), Tok()
                o.sm = sbt(es, f"sm{i}", [128, 64], F32)
                o.t_smA = Tok()
                o.t_smB = Tok()
                o.a_all = sbt(es, f"a_all{i}", [128, 4], F32)
                o.t_a = Tok()
                bbs.append(o)
            carF = sbt(es, "carF", [128, 8], F32)
            t_carF = Tok()
            dve.op(lambda: nc.vector.memset(carF[:], 0.0), w=[t_carF])
            AT = sbt(es, "AT", [128, 4, 128], BF16)
            t_AT = Tok()
            S_f = sbt(es, "S_f", [64, 4, 68], F32)
            S_b = sbt(es, "S_b", [64, 4, 68], BF16)
            t_Sf, t_Sb = Tok(), Tok()
            dve.op(lambda: nc.vector.memset(S_f[:], 0.0), w=[t_Sf])
            dve.op(lambda: nc.vector.memset(S_b[:], 0.0), w=[t_Sb])
            num_sb = sbt(es, "num_sb", [128, 4, 65], F32)
            t_num = Tok()
            sqn = sbt(es, "sqn", [128, 4, 64], F32)
            t_sqn = Tok()
            ymix = sbt(es, "ymix", [128, 512], BF16)
            t_ymix = Tok()
            vtmp = sbt(es, "vtmp", [128, 256], F32)
            t_vtmp = Tok()
            vsq = sbt(es, "vsq", [128, 256], F32)
            t_vsq = Tok()
            vn = sbt(es, "vn", [128, 256], BF16)
            t_vn = Tok()

            st_ps = pst(es, "st_ps", [128, 512])
            t_st = PTok()
            big = RR([(pst(es, f"big{i}", [128, 512]), PTok()) for i in range(3)])
            sc_ps = pst(es, "sc_ps", [128, 4, 128])
            t_sc = PTok()
            misc = pst(es, "misc", [128, 512])
            nm_ps = misc[:, 0:260].rearrange("p (h c) -> p h c", h=4)
            t_nm = PTok()
            misc2 = pst(es, "misc2", [128, 512])
            cb_ps = misc2[:, 260:292]
            t_cb = PTok()
            dS_ps = misc2[0:64, 0:260].rearrange("p (h c) -> p h c", h=4)
            t_dS = t_cb
            tr_ps = pst(es, "tr_ps", [128, 4, 128], BF16)
            t_tr = PTok()

            def proj_fm(col0, ncols, rhs_hT, t_rhs):
                ps_, t_ = big.next()
                for k in range(8):
                    pe.op(lambda: nc.tensor.matmul(ps_[0:ncols, :], lhsT=w_sb[:, k, col0:col0 + ncols], rhs=rhs_hT[:, k, :],
                                                   start=(k == 0), stop=(k == 7)),
                          r=[t_w[k], t_rhs], w=[t_])
                return ps_, t_

            def proj_tm(col0, ncols, hTg, t_rhs, tb):
                ps_, t_ = big.next()
                for k in range(8):
                    pe.op(lambda: nc.tensor.matmul(ps_[:, 0:ncols], lhsT=hTg[:, k, tb:tb + 128], rhs=w_sb[:, k, col0:col0 + ncols],
                                                   start=(k == 0), stop=(k == 7)),
                          r=[t_w[k], t_rhs], w=[t_])
                return ps_, t_

            xsrc = km(x_src)

            def load_x(g):
                sp.dma(xg[g % 2][:], xsrc[:, :, g * 512:(g + 1) * 512], w=[t_xg[g % 2]])

            def norm_pre(g):
                b = g % 2
                act.op(lambda: nc.scalar.activation(out=sq[:], in_=xg[b][:], func=AF.Square), r=[t_xg[b]], w=[t_sq])

            def norm_post(g):
                b = g % 2
                Acol, Bcol, t_ab = A1[l], modT[l][:, 0:8], t_mod[l]
                for k in range(8):
                    pe.op(lambda: nc.tensor.matmul(st_ps[:], lhsT=ones_b[:], rhs=sq[:, k, :], start=(k == 0), stop=(k == 7)),
                          r=[t_sq, t_const], w=[t_st])
                act.op(lambda: nc.scalar.activation(out=rstd[:], in_=st_ps[:], func=AF.Sqrt, bias=EPS, scale=1.0 / D),
                       r=[t_st], w=[t_rstd])
                dve.op(lambda: nc.vector.reciprocal(out=rstd[:], in_=rstd[:]), r=[t_rstd], w=[t_rstd])
                for k in range(8):
                    dve.op(lambda: nc.vector.scalar_tensor_tensor(out=xg[b][:, k, :], in0=xg[b][:, k, :], scalar=Acol[:, k:k + 1],
                                                                  in1=rstd[:], op0=ALU.mult, op1=ALU.mult),
                           r=[t_xg[b], t_rstd, t_ab], w=[t_xg[b]])
                    act.op(lambda: nc.scalar.activation(out=hT[b][:, k, :], in_=xg[b][:, k, :], func=AF.Identity,
                                                        bias=Bcol[:, k:k + 1], scale=1.0),
                           r=[t_xg[b], t_ab], w=[t_hT[b]])

            iq = [0]

            def FM(g):
                b = g % 2
                hTg, t_h = hT[b], t_hT[b]
                pend = None

                def finish_qk(ps_, t_, which, j, i3):
                    ps2, t2 = st_ps, t_st
                    pe.op(lambda: nc.tensor.matmul(ps2[:], lhsT=blk_b[:], rhs=sqq[i3][:], start=True, stop=True),
                          r=[t_sqq[i3], t_const], w=[t2])
                    act.op(lambda: nc.scalar.activation(out=rq[i3][:], in_=ps2[:], func=AF.Sqrt, bias=EPS, scale=1.0 / 64),
                           r=[t2], w=[t_rq[i3]])
                    dve.op(lambda: nc.vector.reciprocal(out=rq[i3][:], in_=rq[i3][:]), r=[t_rq[i3]], w=[t_rq[i3]])
                    dve.op(lambda: nc.vector.scalar_tensor_tensor(out=qn[i3][:], in0=ps_[:], scalar=gq[:, which:which + 1],
                                                                  in1=rq[i3][:], op0=ALU.mult, op1=ALU.mult),
                           r=[t_, t_rq[i3], t_p], w=[t_qn[i3]])
                    dst = (qT_scr if which == 0 else kT_scr)[j * 128:(j + 1) * 128, g * 512:(g + 1) * 512]
                    sp.dma(dst, qn[i3][:], r=[t_qn[i3]])

                for which in range(2):
                    for j in range(4):
                        ps_, t_ = proj_fm(which * 512 + j * 128, 128, hTg, t_h)
                        i3 = iq[0] % 3
                        iq[0] += 1
                        act.op(lambda: nc.scalar.activation(out=sqq[i3][:], in_=ps_[:], func=AF.Square), r=[t_], w=[t_sqq[i3]])
                        if pend is not None:
                            finish_qk(*pend)
                        pend = (ps_, t_, which, j, i3)
                for j in range(4):
                    ps_, t_ = proj_fm(1536 + j * 64, 64, hTg, t_h)
                    act.op(lambda: nc.scalar.copy(out=qmT[b][:, j, :], in_=ps_[0:64, :]), r=[t_], w=[t_qmT[b][j]])
                    if pend is not None:
                        finish_qk(*pend)
                        pend = None
                    ps_, t_ = proj_fm(1792 + j * 64, 64, hTg, t_h)
                    act.op(lambda: nc.scalar.mul(out=kmT[b][:, j, :], in_=ps_[0:64, :], mul=0.125), r=[t_], w=[t_kmT[b][j]])
                ps_, t_ = proj_fm(3072, 8, hTg, t_h)
                act.op(lambda: nc.scalar.activation(out=fe[:], in_=ps_[0:8, :], func=AF.Exp, bias=nbfox[:, 0:1], scale=-1.0),
                       r=[t_, t_p], w=[t_fe])
                act.op(lambda: nc.scalar.activation(out=fe[:], in_=fe[:], func=AF.Ln, bias=1.0, scale=1.0), r=[t_fe], w=[t_fe])
                dve.op(lambda: nc.vector.tensor_tensor_scan(out=fneg[:], data0=ones_f[0:8, :], data1=fe[:], initial=fcar[:, 0:1],
                                                            op0=ALU.mult, op1=ALU.add),
                       r=[t_fe, t_fcar, t_const], w=[t_fneg])
                dve.op(lambda: nc.vector.tensor_copy(out=fcar[:], in_=fneg[:, 511:512]), r=[t_fneg], w=[t_fcar])
                dve.op(lambda: nc.vector.tensor_scalar(out=fbf[:], in0=fneg[:], scalar1=-1.0, scalar2=None, op0=ALU.mult),
                       r=[t_fneg], w=[t_fbf])
                sp.dma(FT_scr[:, g * 512:(g + 1) * 512], fbf[:], r=[t_fbf])

            def stageA(t):
                g, bi = divmod(t, 4)
                tb = bi * 128
                hTg, t_h = hT[g % 2], t_hT[g % 2]
                o = bbs[t % 2]
                sm = o.sm
                ps_g, t_g = proj_tm(2816, 272, hTg, t_h, tb)
                dve.op(lambda: nc.vector.tensor_tensor(out=o.gt[:], in0=ps_g[:, 256:272], in1=gb[:], op=ALU.add),
                       r=[t_g, t_p], w=[o.t_gt])
                act.op(lambda: nc.scalar.activation(out=o.vg[:], in_=ps_g[:, 0:256], func=AF.Gelu_apprx_tanh), r=[t_g], w=[o.t_vg])
                act.op(lambda: nc.scalar.activation(out=o.nl[:], in_=o.gt[:], func=AF.Exp, scale=-1.0), r=[o.t_gt], w=[o.t_nl])
                act.op(lambda: nc.scalar.activation(out=o.nl[:], in_=o.nl[:], func=AF.Ln, bias=1.0, scale=1.0), r=[o.t_nl], w=[o.t_nl])
                ps_o, t_o = proj_tm(2304, 512, hTg, t_h, tb)
                act.op(lambda: nc.scalar.activation(out=o.go[:], in_=ps_o[:, 0:256], func=AF.Sigmoid), r=[t_o], w=[o.t_go])
                act.op(lambda: nc.scalar.activation(out=o.ug[:], in_=ps_o[:, 256:512], func=AF.Gelu_apprx_tanh), r=[t_o], w=[o.t_ug])
                pool.op(lambda: nc.gpsimd.tensor_tensor(out=o.go[:], in0=o.go[:], in1=gmo[:], op=ALU.mult), r=[o.t_go, t_p], w=[o.t_go])
                ps_v, t_v = proj_tm(1024, 512, hTg, t_h, tb)
                act.op(lambda: nc.scalar.copy(out=o.va[:, :, 0:64], in_=ps_v[:].rearrange("p (h d) -> p h d", h=8)),
                       r=[t_v], w=[o.t_va])
                sp.dma(v_scr[:, t * 128:(t + 1) * 128, :].rearrange("h t c -> t h c"), o.va[:], r=[o.t_va])
                pe.op(lambda: nc.tensor.matmul(cb_ps[:, 0:16], lhsT=tri_f[:], rhs=o.nl[:], start=True, stop=True),
                      r=[o.t_nl, t_const], w=[t_cb])
                pe.op(lambda: nc.tensor.matmul(cb_ps[:, 16:32], lhsT=ones_f[:, 0:128], rhs=o.nl[:], start=True, stop=True),
                      r=[o.t_nl, t_const], w=[t_cb])
                dve.op(lambda: nc.vector.tensor_tensor(out=negF[:, t, :], in0=cb_ps[:, 0:8], in1=carF[:], op=ALU.add),
                       r=[t_cb, t_carF], w=[t_negF])
                dve.op(lambda: nc.vector.tensor_tensor(out=carF[:], in0=cb_ps[:, 16:24], in1=carF[:], op=ALU.add),
                       r=[t_cb, t_carF], w=[t_carF])
                dve.op(lambda: nc.vector.tensor_tensor(out=sm[:, 8:12], in0=cb_ps[:, 12:16], in1=o.gt[:, 8:12], op=ALU.add),
                       r=[t_cb, o.t_gt], w=[o.t_smA])
                dve.op(lambda: nc.vector.tensor_scalar(out=sm[:, 8:12], in0=sm[:, 8:12], scalar1=80.0, scalar2=None, op0=ALU.min),
                       r=[o.t_smA], w=[o.t_smA])
                dve.op(lambda: nc.vector.tensor_scalar(out=sm[:, 48:52], in0=cb_ps[:, 12:16], scalar1=80.0, scalar2=None, op0=ALU.min),
                       r=[t_cb, o.t_smA], w=[o.t_smA])
                act.op(lambda: nc.scalar.activation(out=sm[:, 0:4], in_=sm[:, 8:12], func=AF.Exp), r=[o.t_smA], w=[o.t_smA])
                act.op(lambda: nc.scalar.activation(out=sm[:, 4:8], in_=sm[:, 48:52], func=AF.Exp), r=[o.t_smA], w=[o.t_smA])
                act.op(lambda: nc.scalar.activation(out=o.a_all[:], in_=cb_ps[:, 28:32], func=AF.Exp, scale=-1.0),
                       r=[t_cb], w=[o.t_a])
                ps_k, t_k = proj_tm(1792, 512, hTg, t_h, tb)
                act.op(lambda: nc.scalar.mul(out=o.kmt[:], in_=ps_k[:, 0:256], mul=0.125), r=[t_k], w=[o.t_kmt])
                for h in range(4):
                    dve.op(lambda: nc.vector.tensor_scalar(out=o.vaug[:, h, 0:64], in0=ps_k[:, 256 + h * 64:320 + h * 64],
                                                           scalar1=sm[:, h:h + 1], scalar2=None, op0=ALU.mult),
                           r=[t_k, o.t_smA], w=[o.t_vaug])
                dve.op(lambda: nc.vector.tensor_copy(out=o.vaug[:, :, 64], in_=sm[:, 0:4]), r=[o.t_smA], w=[o.t_vaug])

            def stageB1(t):
                g, bi = divmod(t, 4)
                tb = bi * 128
                gb2 = g % 2
                o = bbs[t % 2]
                sm = o.sm
                t_sm = o.t_smB
                for h in range(4):
                    pe.op(lambda: nc.tensor.matmul(sc_ps[:, h, :], lhsT=kmT[gb2][:, h, tb:tb + 128],
                                                   rhs=qmT[gb2][:, h, tb:tb + 128], start=True, stop=True),
                          r=[t_kmT[gb2][h], t_qmT[gb2][h]], w=[t_sc])
                dve.op(lambda: nc.vector.tensor_tensor(out=AT[:], in0=sc_ps[:], in1=tri4[:], op=ALU.mult),
                       r=[t_sc, t_const], w=[t_AT])
                dve.op(lambda: nc.vector.tensor_reduce(out=sm[:, 32:36], in_=o.vg[:].rearrange("p (g c) -> p g c", g=4), axis=AX.X, op=ALU.add),
                       r=[o.t_vg, t_sm], w=[t_sm])
                pool.op(lambda: nc.gpsimd.tensor_tensor(out=vsq[:], in0=o.vg[:], in1=o.vg[:], op=ALU.mult), r=[o.t_vg], w=[t_vsq])
                dve.op(lambda: nc.vector.tensor_reduce(out=sm[:, 36:40], in_=vsq[:].rearrange("p (g c) -> p g c", g=4), axis=AX.X, op=ALU.add),
                       r=[t_vsq, t_sm], w=[t_sm])
                dve.op(lambda: nc.vector.tensor_scalar(out=sm[:, 40:44], in0=sm[:, 32:36], scalar1=1.0 / 64, scalar2=None, op0=ALU.mult),
                       r=[t_sm], w=[t_sm])
                dve.op(lambda: nc.vector.tensor_tensor(out=sm[:, 44:48], in0=sm[:, 40:44], in1=sm[:, 40:44], op=ALU.mult),
                       r=[t_sm], w=[t_sm])
                dve.op(lambda: nc.vector.scalar_tensor_tensor(out=sm[:, 44:48], in0=sm[:, 36:40], scalar=1.0 / 64, in1=sm[:, 44:48],
                                                              op0=ALU.mult, op1=ALU.subtract),
                       r=[t_sm], w=[t_sm])
                act.op(lambda: nc.scalar.activation(out=sm[:, 44:48], in_=sm[:, 44:48], func=AF.Sqrt, bias=EPS, scale=1.0),
                       r=[t_sm], w=[t_sm])
                dve.op(lambda: nc.vector.reciprocal(out=sm[:, 44:48], in_=sm[:, 44:48]), r=[t_sm], w=[t_sm])
                for gi in range(4):
                    pool.op(lambda: nc.gpsimd.tensor_scalar(out=vtmp[:, gi * 64:(gi + 1) * 64], in0=o.vg[:, gi * 64:(gi + 1) * 64],
                                                            scalar1=sm[:, 40 + gi:41 + gi], scalar2=sm[:, 44 + gi:45 + gi],
                                                            op0=ALU.subtract, op1=ALU.mult),
                            r=[o.t_vg, t_sm, t_vtmp], w=[t_vtmp])
                pool.op(lambda: nc.gpsimd.tensor_tensor(out=vn[:], in0=vtmp[:], in1=gsg[:], op=ALU.mult), r=[t_vtmp, t_p], w=[t_vn])

            def stageB2(t):
                g, bi = divmod(t, 4)
                tb = bi * 128
                gb2 = g % 2
                o = bbs[t % 2]
                sm = o.sm
                t_sm = o.t_smB
                for h in range(4):
                    pe.op(lambda: nc.tensor.matmul(nm_ps[:, h, :], lhsT=AT[:, h, :], rhs=o.vaug[:, h, 0:65], start=True, stop=False),
                          r=[t_AT, o.t_vaug], w=[t_nm])
                    pe.op(lambda: nc.tensor.matmul(nm_ps[:, h, :], lhsT=qmT[gb2][:, h, tb:tb + 128], rhs=S_b[:, h, 0:65],
                                                   start=False, stop=True),
                          r=[t_qmT[gb2][h], t_Sb], w=[t_nm])
                for h in range(4):
                    pe.op(lambda: nc.tensor.matmul(dS_ps[:, h, :], lhsT=o.kmt[:, h * 64:(h + 1) * 64], rhs=o.vaug[:, h, 0:65],
                                                   start=True, stop=True),
                          r=[o.t_kmt, o.t_vaug], w=[t_dS])
                mx_ps, t_mx = big.next()
                for gi in range(4):
                    pe.op(lambda: nc.tensor.matmul(mx_ps[:, gi * 64:(gi + 1) * 64], lhsT=wsb[:, gi, :], rhs=vn[:, gi * 64:(gi + 1) * 64],
                                                   start=True, stop=True),
                          r=[t_vn, t_p], w=[t_mx])
                dve.op(lambda: nc.vector.tensor_tensor(out=S_f[:, :, 0:65], in0=dS_ps[:], in1=S_f[:, :, 0:65], op=ALU.add),
                       r=[t_dS, t_Sf], w=[t_Sf])
                for h in range(4):
                    dve.op(lambda: nc.vector.tensor_scalar(out=S_f[:, h, 0:65], in0=S_f[:, h, 0:65],
                                                           scalar1=o.a_all[0:64, h:h + 1], scalar2=None, op0=ALU.mult),
                           r=[t_Sf, o.t_a], w=[t_Sf])
                dve.op(lambda: nc.vector.tensor_copy(out=S_b[:], in_=S_f[:]), r=[t_Sf], w=[t_Sb])
                act.op(lambda: nc.scalar.copy(out=num_sb[:], in_=nm_ps[:]), r=[t_nm], w=[t_num])
                dve.op(lambda: nc.vector.tensor_scalar(out=sm[:, 12:16], in0=num_sb[:, :, 64], scalar1=-1.0, scalar2=None, op0=ALU.mult),
                       r=[t_num, t_sm], w=[t_sm])
                dve.op(lambda: nc.vector.tensor_tensor(out=sm[:, 16:20], in0=num_sb[:, :, 64], in1=sm[:, 12:16], op=ALU.max),
                       r=[t_num, t_sm], w=[t_sm])
                dve.op(lambda: nc.vector.tensor_tensor(out=sm[:, 16:20], in0=sm[:, 16:20], in1=sm[:, 4:8], op=ALU.max),
                       r=[t_sm, o.t_smA], w=[t_sm])
                dve.op(lambda: nc.vector.reciprocal(out=sm[:, 16:20], in_=sm[:, 16:20]), r=[t_sm], w=[t_sm])
                for h in range(4):
                    dve.op(lambda: nc.vector.tensor_scalar(out=num_sb[:, h, 0:64], in0=num_sb[:, h, 0:64],
                                                           scalar1=sm[:, 16 + h:17 + h], scalar2=None, op0=ALU.mult),
                           r=[t_num, t_sm], w=[t_num])
                pool.op(lambda: nc.gpsimd.tensor_tensor(out=sqn[:], in0=num_sb[:, :, 0:64], in1=num_sb[:, :, 0:64], op=ALU.mult),
                        r=[t_num], w=[t_sqn])
                dve.op(lambda: nc.vector.tensor_reduce(out=sm[:, 20:24], in_=sqn[:], axis=AX.X, op=ALU.add),
                       r=[t_sqn, t_sm], w=[t_sm])
                act.op(lambda: nc.scalar.activation(out=sm[:, 24:28], in_=sm[:, 20:24], func=AF.Sqrt, bias=EPS, scale=1.0 / 64),
                       r=[t_sm], w=[t_sm])
                dve.op(lambda: nc.vector.reciprocal(out=sm[:, 28:32], in_=sm[:, 24:28]), r=[t_sm], w=[t_sm])
                for h in range(4):
                    dve.op(lambda: nc.vector.scalar_tensor_tensor(out=ymix[:, h * 64:(h + 1) * 64], in0=num_sb[:, h, 0:64],
                                                                  scalar=sm[:, 28 + h:29 + h], in1=o.go[:, h * 64:(h + 1) * 64],
                                                                  op0=ALU.mult, op1=ALU.mult),
                           r=[t_num, t_sm, o.t_go], w=[t_ymix])
                for gi in range(4):
                    dve.op(lambda: nc.vector.scalar_tensor_tensor(out=ymix[:, 256 + gi * 64:320 + gi * 64], in0=mx_ps[:, gi * 64:(gi + 1) * 64],
                                                                  scalar=bs[:, gi:gi + 1], in1=o.ug[:, gi * 64:(gi + 1) * 64],
                                                                  op0=ALU.add, op1=ALU.mult),
                           r=[t_mx, o.t_ug, t_p], w=[t_ymix])

            def stageB3(t):
                g, bi = divmod(t, 4)
                tb = bi * 128
                gb2 = g % 2
                for c in range(4):
                    pe.op(lambda: nc.tensor.transpose(tr_ps[:, c, :], ymix[:, c * 128:(c + 1) * 128], ident_b[:]),
                          r=[t_ymix, t_const], w=[t_tr])
                act.op(lambda: nc.scalar.copy(out=ymT[gb2][:, :, tb:tb + 128], in_=tr_ps[:]), r=[t_tr], w=[t_ymT[gb2]])
                if bi == 3:
                    sp.dma(yT_scr[512:1024, g * 512:(g + 1) * 512].rearrange("(c p) t -> p c t", p=128), ymT[gb2][:], r=[t_ymT[gb2]])

            load_x(0)
            norm_pre(0)
            norm_post(0)
            for t in range(NB + 2):
                if t < NB:
                    g, bi = divmod(t, 4)
                    if bi == 0:
                        if g + 1 < G:
                            load_x(g + 1)
                            norm_pre(g + 1)
                        FM(g)
                    if bi == 1 and g + 1 < G:
                        norm_post(g + 1)
                if 1 <= t <= NB:
                    stageB1(t - 1)
                if t < NB:
                    stageA(t)
                if 2 <= t <= NB + 1:
                    stageB3(t - 2)
                if 1 <= t <= NB:
                    stageB2(t - 1)
        fw.barrier()

    def phase2(l):
        with ExitStack() as es:
            Qa = [sbt(es, f"Qa{i}", [65, S], BF16) for i in range(2)]
            Ka = [sbt(es, f"Ka{i}", [65, S], BF16) for i in range(2)]
            Va = [sbt(es, f"Va{i}", [128, NB, 128], BF16) for i in range(2)]
            t_Q = [Tok(), Tok()]
            t_K = [Tok(), Tok()]
            t_V = [Tok(), Tok()]
            for i in range(2):
                pool.op(lambda: nc.gpsimd.memset(Ka[i][64:65, :], 1.0), w=[t_K[i]])
            P = RR([(sbt(es, f"P{i}", [128, 512], BF16), Tok()) for i in range(6)])
            rden = [sbt(es, f"rden{i}", [128, 512], F32) for i in range(2)]
            t_rden = [Tok(), Tok()]
            yo = [sbt(es, f"yo{i}", [64, 512], BF16) for i in range(2)]
            t_yo = [Tok(), Tok()]
            scp = RR([(pst(es, f"scp{i}", [128, 512]), PTok()) for i in range(6)])
            op_ = RR([(pst(es, f"op{i}", [128, 512]), PTok()) for i in range(2)])
            def load_head(hh):
                bb = hh % 2
                sp.dma(Ka[bb][0:64, :], kT_scr[hh * 64:(hh + 1) * 64, :], w=[t_K[bb]])
                sp.dma(Qa[bb][0:64, :], qT_scr[hh * 64:(hh + 1) * 64, :], w=[t_Q[bb]])
                sp.dma(Qa[bb][64:65, :], FT_scr[hh:hh + 1, :], w=[t_Q[bb]])
                sp.dma(Va[bb][:], v_scr[hh].rearrange("(j p) c -> p j c", p=128), w=[t_V[bb]])

            units = [(h, g, j) for h in range(8) for g in range(G) for j in range(4 * g + 4)]
            LA = 3
            inflight = {}
            cur_o = {}
            io = 0
            load_head(0)
            for i in range(len(units) + LA):
                if i < len(units):
                    h, g, j = units[i]
                    hb = h % 2
                    c0 = max(0, j - 4 * g) * 128
                    s_ps, t_s = scp.next()
                    diag = j >= 4 * g
                    pe.op(lambda: nc.tensor.matmul(s_ps[:, c0:512], lhsT=Ka[hb][:, j * 128:(j + 1) * 128],
                                                   rhs=Qa[hb][:, g * 512 + c0:(g + 1) * 512], start=True, stop=not diag),
                          r=[t_K[hb], t_Q[hb]], w=[t_s])
                    if diag:
                        pe.op(lambda: nc.tensor.matmul(s_ps[:, c0:c0 + 128], lhsT=ident_b[:], rhs=mneg_b[:], start=False, stop=True),
                              r=[t_const], w=[t_s])
                    inflight[i] = (s_ps, t_s, c0)
                k = i - LA
                if k < 0:
                    continue
                h, g, j = units[k]
                hb = h % 2
                nj = 4 * g + 4
                if g == 0 and j == 0 and h + 1 < 8:
                    load_head(h + 1)
                s_ps, t_s, c0 = inflight.pop(k)
                if j == 0:
                    cur_o[(h, g)] = op_.next()
                o_ps, t_o = cur_o[(h, g)]
                p_sb, t_p_ = P.next()
                act.op(lambda: nc.scalar.activation(out=p_sb[:, c0:512], in_=s_ps[:, c0:512], func=AF.Exp,
                                                    bias=negF[:, j, h:h + 1], scale=1.0),
                       r=[t_s, t_negF], w=[t_p_])
                pe.op(lambda: nc.tensor.matmul(o_ps[:, c0:512], lhsT=Va[hb][:, j, :], rhs=p_sb[:, c0:512],
                                               start=(j == 0), stop=(j == nj - 1)),
                      r=[t_V[hb], t_p_], w=[t_o])
                if j == nj - 1:
                    del cur_o[(h, g)]
                    ib = io % 2
                    io += 1
                    dve.op(lambda: nc.vector.reciprocal(out=rden[ib][64:128, :], in_=o_ps[64:128, :]), r=[t_o], w=[t_rden[ib]])
                    dve.op(lambda: nc.vector.tensor_tensor(out=yo[ib][:], in0=o_ps[0:64, :], in1=rden[ib][64:128, :], op=ALU.mult),
                           r=[t_o, t_rden[ib]], w=[t_yo[ib]])
                    sp.dma(yT_scr[h * 64:(h + 1) * 64, g * 512:(g + 1) * 512], yo[ib][:], r=[t_yo[ib]])
        fw.barrier()

    def phase3a(l, x_src, prefetch=None):
        with ExitStack() as es:
            wo = sbt(es, "wo", [128, 8, D], BF16)
            t_wo = [Tok() for _ in range(8)]
            for k in range(8):
                pool.dma(wo[:, k, :], w_out[l, k * 128:(k + 1) * 128, :], w=[t_wo[k]])
            if prefetch is not None:
                prefetch()
            xg = [sbt(es, f"xg{i}", [128, 8, 512], F32) for i in range(2)]
            t_xg = [Tok(), Tok()]
            x1, t_x1 = xg, t_xg
            yg = [sbt(es, f"yg{i}", [128, 8, 512], BF16) for i in range(2)]
            t_yg = [Tok(), Tok()]
            sq = sbt(es, "sq", [128, 8, 512], BF16)
            t_sq = Tok()
            rstd = sbt(es, "rstd", [128, 512], F32)
            t_rstd = Tok()
            hT = [sbt(es, f"hT{i}", [128, 8, 512], BF16) for i in range(2)]
            t_hT = [Tok(), Tok()]
            st_ps = pst(es, "st_ps", [128, 512])
            t_st = PTok()
            big = RR([(pst(es, f"big{i}", [128, 512]), PTok()) for i in range(4)])
            xsrc = km(x_src)
            for g in range(G):
                b = g % 2
                sl = slice(g * 512, (g + 1) * 512)
                if g == 0:
                    sp.dma(xg[0][:], xsrc[:, :, 0:512], w=[t_xg[0]])
                    sp.dma(yg[0][:], km(yT_scr)[:, :, 0:512], w=[t_yg[0]])
                if g + 1 < G:
                    sl2 = slice((g + 1) * 512, (g + 2) * 512)
                    sp.dma(xg[1 - b][:], xsrc[:, :, sl2], w=[t_xg[1 - b]])
                    sp.dma(yg[1 - b][:], km(yT_scr)[:, :, sl2], w=[t_yg[1 - b]])
                for c in range(8):
                    ps_, t_ = big.next()
                    for k in range(8):
                        pe.op(lambda: nc.tensor.matmul(ps_[:], lhsT=wo[:, k, c * 128:(c + 1) * 128], rhs=yg[b][:, k, :],
                                                       start=(k == 0), stop=(k == 7)),
                              r=[t_wo[k], t_yg[b]], w=[t_])
                    dve.op(lambda: nc.vector.scalar_tensor_tensor(out=x1[b][:, c, :], in0=ps_[:], scalar=modT[l][:, 16 + c:17 + c],
                                                                  in1=xg[b][:, c, :], op0=ALU.mult, op1=ALU.add),
                           r=[t_, t_xg[b], t_mod[l]], w=[t_x1[b]])
                sp.dma(km(x1T_scr)[:, :, sl], x1[b][:], r=[t_x1[b]])
                norm_group(x1[b], t_x1[b], sq, t_sq, st_ps, t_st, rstd, t_rstd, hT[b], t_hT[b],
                           A2[l], modT[l][:, 24:32], t_mod[l])
                sp.dma(km(h2T_scr)[:, :, sl], hT[b][:], r=[t_hT[b]])
        fw.barrier()

    def phase3b(l, x_dst, wu, t_wu):
        with ExitStack() as es:
            wd = sbt(es, "wd", [128, NF, D], BF16)
            t_wd = [Tok() for _ in range(NF)]
            for f in range(NF):
                pool.dma(wd[:, f, :], w_down[l, f * 128:(f + 1) * 128, :], w=[t_wd[f]])
            wc = sbt(es, "wc", [128, 2 * NF, 3], F32)
            bc = sbt(es, "bc", [128, 2 * NF], F32)
            t_c = Tok()
            sp.dma(wc[:], w_convT[l], w=[t_c])
            sp.dma(bc[:], b_convT[l], w=[t_c])
            tail = sbt(es, "tail", [128, 2 * NF, 2], F32)
            t_tail = [Tok() for _ in range(2 * NF)]
            pool.op(lambda: nc.gpsimd.memset(tail[:], 0.0), w=t_tail)
            hT = [sbt(es, f"hT{i}", [128, 8, 512], BF16) for i in range(2)]
            t_hT = [Tok(), Tok()]
            zs = RR([(sbt(es, f"zs{i}", [128, 514], F32), Tok()) for i in range(4)])
            cv = RR([(sbt(es, f"cv{i}", [128, 512], F32), Tok()) for i in range(4)])
            sa = RR([(sbt(es, f"sa{i}", [128, 512], F32), Tok()) for i in range(2)])
            actT1 = sbt(es, "actT", [128, NF, 512], BF16)
            actT = [actT1, actT1]
            t_actT1 = [Tok() for _ in range(NF)]
            t_actT = [t_actT1, t_actT1]
            x1t = RR([(sbt(es, f"x1t{i}", [128, 512], F32), Tok()) for i in range(3)])
            zp = RR([(pst(es, f"zp{i}", [128, 512]), PTok()) for i in range(5)])
            dp = RR([(pst(es, f"dp{i}", [128, 512]), PTok()) for i in range(3)])
            dst = km(x_dst)
            for g in range(G):
                b = g % 2
                sl = slice(g * 512, (g + 1) * 512)
                if g == 0:
                    sp.dma(hT[0][:], km(h2T_scr)[:, :, 0:512], w=[t_hT[0]])
                if g + 1 < G:
                    sp.dma(hT[1 - b][:], km(h2T_scr)[:, :, (g + 1) * 512:(g + 2) * 512], w=[t_hT[1 - b]])
                for f in range(NF):
                    cvs = []
                    for br in range(2):
                        ti = br * NF + f
                        z_ps, t_z = zp.next()
                        for k in range(8):
                            pe.op(lambda: nc.tensor.matmul(z_ps[:], lhsT=wu[:, k, ti * 128:(ti + 1) * 128], rhs=hT[b][:, k, :],
                                                           start=(k == 0), stop=(k == 7)),
                                  r=[t_wu[k], t_hT[b]], w=[t_z])
                        z_sb, t_zs = zs.next()
                        act.op(lambda: nc.scalar.copy(out=z_sb[:, 2:514], in_=z_ps[:]), r=[t_z], w=[t_zs])
                        pool.op(lambda: nc.gpsimd.tensor_copy(out=z_sb[:, 0:2], in_=tail[:, ti, :]), r=[t_tail[ti]], w=[t_zs])
                        pool.op(lambda: nc.gpsimd.tensor_copy(out=tail[:, ti, :], in_=z_sb[:, 512:514]), r=[t_zs], w=[t_tail[ti]])
                        c_sb, t_cv = cv.next()
                        act.op(lambda: nc.scalar.activation(out=c_sb[:], in_=z_ps[:], func=AF.Identity,
                                                            bias=bc[:, ti:ti + 1], scale=wc[:, ti, 2:3]),
                               r=[t_z, t_c], w=[t_cv])
                        dve.op(lambda: nc.vector.scalar_tensor_tensor(out=c_sb[:], in0=z_sb[:, 1:513], scalar=wc[:, ti, 1:2],
                                                                      in1=c_sb[:], op0=ALU.mult, op1=ALU.add),
                               r=[t_zs, t_c, t_cv], w=[t_cv])
                        dve.op(lambda: nc.vector.scalar_tensor_tensor(out=c_sb[:], in0=z_sb[:, 0:512], scalar=wc[:, ti, 0:1],
                                                                      in1=c_sb[:], op0=ALU.mult, op1=ALU.add),
                               r=[t_zs, t_c, t_cv], w=[t_cv])
                        cvs.append((c_sb, t_cv))
                    s_sb, t_sa = sa.next()
                    act.op(lambda: nc.scalar.activation(out=s_sb[:], in_=cvs[0][0][:], func=AF.Silu), r=[cvs[0][1]], w=[t_sa])
                    pool.op(lambda: nc.gpsimd.tensor_tensor(out=actT[b][:, f, :], in0=s_sb[:], in1=cvs[1][0][:], op=ALU.mult),
                            r=[t_sa, cvs[1][1]], w=[t_actT[b][f]])
                for c in range(8):
                    xt_, t_xt = x1t.next()
                    sp.dma(xt_[:], x1T_scr[c * 128:(c + 1) * 128, sl], w=[t_xt])
                    d_ps, t_d = dp.next()
                    for f in range(NF):
                        pe.op(lambda: nc.tensor.matmul(d_ps[:], lhsT=wd[:, f, c * 128:(c + 1) * 128], rhs=actT[b][:, f, :],
                                                       start=(f == 0), stop=(f == NF - 1)),
                              r=[t_wd[f], t_actT[b][f]], w=[t_d])
                    dve.op(lambda: nc.vector.scalar_tensor_tensor(out=xt_[:], in0=d_ps[:], scalar=modT[l][:, 40 + c:41 + c],
                                                                  in1=xt_[:], op0=ALU.mult, op1=ALU.add),
                           r=[t_d, t_xt, t_mod[l]], w=[t_xt])
                    sp.dma(x_dst[c * 128:(c + 1) * 128, sl], xt_[:], r=[t_xt])
        fw.barrier()

    phase0()
    cur = xT
    for l in range(depth):
        nxt = outT if l == depth - 1 else xmid_scr
        if upto >= 1:
            phase1(l, cur)
        if upto >= 2:
            phase2(l)
        if upto >= 3:
            with ExitStack() as es3:
                wu = sbt(es3, "wu", [128, 8, 2 * DFF], BF16)
                t_wu = [Tok() for _ in range(8)]

                def pf(l=l, wu=wu, t_wu=t_wu):
                    for k in range(8):
                        pool.dma(wu[:, k, :], w_up[l, k * 128:(k + 1) * 128, :], w=[t_wu[k]])
                phase3a(l, cur, prefetch=pf)
                if upto >= 4:
                    phase3b(l, nxt, wu, t_wu)
        cur = nxt
    if upto < 4:
        with ExitStack() as es:
            dbg = sbt(es, "dbg", [128, 48], F32)
            t_dbg = Tok()
            dve.op(lambda: nc.vector.tensor_copy(out=dbg[:], in_=modT[0][:]), r=[t_mod[0]], w=[t_dbg])
            sp.dma(outT[0:128, 0:48], dbg[:], r=[t_dbg])
            dve.op(lambda: nc.vector.tensor_copy(out=dbg[:, 0:8], in_=A1[0][:]), r=[t_mod[0], t_dbg], w=[t_dbg])
            dve.op(lambda: nc.vector.tensor_copy(out=dbg[:, 8:16], in_=A2[0][:]), r=[t_mod[0], t_dbg], w=[t_dbg])
            sp.dma(outT[128:256, 0:48], dbg[:], r=[t_dbg])
            fw.finish()
    fw.finish()
    top.close()
    build.stats = {s.name: (s.n_ins, s.n_wait) for s in fw.streams}
    return nc


def prep_shared(inputs, depth=DEPTH):
    f = lambda a: np.ascontiguousarray(np.asarray(a, dtype=np.float32))
    sh = {}
    sh["w_ada"] = f(inputs["w_ada"])
    sh["b_adaT"] = f(np.asarray(inputs["b_ada"]).reshape(depth, 48, 128).transpose(0, 2, 1))
    sh["g_mixT"] = f(np.asarray(inputs["g_mix"]).reshape(depth, 8, 128).transpose(0, 2, 1))
    sh["g_ffnT"] = f(np.asarray(inputs["g_ffn"]).reshape(depth, 8, 128).transpose(0, 2, 1))
    sh["w_in"] = f(inputs["w_in"])
    sh["gbias"] = f(np.concatenate([np.asarray(inputs["b_fox_f"]), np.asarray(inputs["b_m_i"]), np.asarray(inputs["b_m_f"])], axis=1))
    sh["bfoxT"] = f(np.asarray(inputs["b_fox_f"])[:, :, None])
    gq = np.tile(np.asarray(inputs["g_q"]), (1, 2))
    gk = np.tile(np.asarray(inputs["g_k"]), (1, 2))
    sh["gqk"] = f(np.stack([gq, gk], axis=-1))
    sh["g_m_out"] = f(inputs["g_m_out"])
    sh["g_sgu"] = f(inputs["g_sgu"])
    sh["w_sT"] = f(np.asarray(inputs["w_s"]).transpose(0, 1, 3, 2))
    sh["b_sT"] = f(np.asarray(inputs["b_s"]).transpose(0, 2, 1))
    sh["w_out"] = f(inputs["w_out"])
    sh["w_up"] = f(inputs["w_up"])
    sh["w_down"] = f(inputs["w_down"])
    sh["w_convT"] = f(np.asarray(inputs["w_conv"]).reshape(depth, 3, 2 * NF, 128).transpose(0, 3, 2, 1))
    sh["b_convT"] = f(np.asarray(inputs["b_conv"]).reshape(depth, 2 * NF, 128).transpose(0, 2, 1))
    sh["tri"] = np.triu(np.ones((128, 128), np.float32))
    sh["ident"] = np.eye(128, dtype=np.float32)
    return sh


def prep_core(x_b, c_b):
    return {"xT": np.ascontiguousarray(np.asarray(x_b, np.float32).T),
            "ccol": np.ascontiguousarray(np.asarray(c_b, np.float32).reshape(8, 128).T)}


def kernel(**inputs):
    x = np.asarray(inputs["x"])
    c = np.asarray(inputs["c"])
    B, S, _ = x.shape
    nc = build(S=S, depth=DEPTH)
    sh = prep_shared(inputs)
    in_maps = []
    for b in range(B):
        m = dict(sh)
        m.update(prep_core(x[b], c[b]))
        in_maps.append(m)
    res = run_bass_kernel_spmd(nc, in_maps, core_ids=list(range(B)))
    out = np.stack([np.asarray(r["outT"]).T for r in res.results], axis=0)
    return np.ascontiguousarray(out.astype(np.float32))
```
